# Optimizing a Trainium2 kernel written in Bass

```python
import math
import jax, jax.numpy as jnp
from jax import lax
import numpy as np

D_MODEL = 2048
BATCH = 4
SEQ = 2048
DEPTH = 2
DEC_BATCH = 32
DEC_SEQ = 8
PAST_LEN = 16384
PAGE_SIZE = 128

D_INNER = D_MODEL
SSD_HEAD_DIM = 64
SSD_HEADS = D_INNER // SSD_HEAD_DIM
SSD_GROUPS = 4
D_STATE = 128
CONV_W = 4
CONV_DIM = D_INNER + 2 * SSD_GROUPS * D_STATE
SSD_CHUNK = 128
SWA_HEAD_DIM = 64
SWA_HEADS = D_MODEL // SWA_HEAD_DIM
SWA_KV_HEADS = SWA_HEADS // 8
WINDOW = 128
N_MEM = 256
MEM_HEADS = 4
MEM_HEAD_DIM = D_MODEL // MEM_HEADS
PEER_HEADS = 8
N_KEYS = 128
N_EXPERTS = N_KEYS * N_KEYS
PEER_QUERY_DIM = 256
PEER_HALF = PEER_QUERY_DIM // 2
PEER_TOPK = 16
PEER_TOKEN_BLOCK = 128
N_BRANCH = 3
EPS = 1e-6

_SPLITS = (D_INNER, CONV_DIM, SSD_HEADS, SWA_HEADS * SWA_HEAD_DIM, SWA_KV_HEADS * SWA_HEAD_DIM,
           SWA_KV_HEADS * SWA_HEAD_DIM, MEM_HEADS * MEM_HEAD_DIM, N_BRANCH * D_MODEL)
IN_DIM = sum(_SPLITS)

kernel_name = 'hybrid_ssd_swa_mem_peer_step'

F32 = jnp.float32


def _split_columns(y):
    parts, start = [], 0
    for w in _SPLITS:
        parts.append(y[..., start:start + w])
        start += w
    return parts


def _rmsnorm(x, g):
    xf = x.astype(F32)
    xf = xf * lax.rsqrt(jnp.mean(xf * xf, axis=-1, keepdims=True) + EPS)
    return (xf * g.astype(F32)).astype(x.dtype)


def _alibi_slopes(n):
    return jnp.exp2(-8.0 * jnp.arange(1, n + 1, dtype=F32) / n)


def _causal_dwconv(xbc, buf, w, b):
    xpad = jnp.concatenate([buf.astype(xbc.dtype), xbc], axis=1)
    y = lax.conv_general_dilated(xpad, w[:, None, :].astype(xbc.dtype), (1,), 'VALID',
                                 dimension_numbers=('NWC', 'WIO', 'NWC'),
                                 feature_group_count=xbc.shape[-1])
    return y + b.astype(xbc.dtype), xpad[:, -(CONV_W - 1):]


def _ssd_scan(xs, dt, A, bm, cm, state0):
    bsz, L = xs.shape[0], xs.shape[1]
    hg = SSD_HEADS // SSD_GROUPS
    q = min(SSD_CHUNK, L)
    pad = (-L) % q
    nc = (L + pad) // q

    def chunks(t):
        t = t.astype(F32)
        t = jnp.pad(t, [(0, 0), (0, pad)] + [(0, 0)] * (t.ndim - 2))
        return jnp.moveaxis(t.reshape((bsz, nc, q) + t.shape[2:]), 1, 0)

    xc = chunks(xs.reshape(bsz, L, SSD_GROUPS, hg, SSD_HEAD_DIM))
    dc = chunks(dt.reshape(bsz, L, SSD_GROUPS, hg))
    bc = chunks(bm)
    cc = chunks(cm)
    a_g = A.astype(F32).reshape(SSD_GROUPS, hg)
    causal = jnp.tril(jnp.ones((q, q), bool))[None, :, :, None, None]

    def step(S, inp):
        x, d, b, c = inp
        cum = jnp.cumsum(d * a_g, axis=1)
        seg = cum[:, :, None] - cum[:, None, :]
        decay = jnp.exp(jnp.where(causal, seg, -jnp.inf))
        cb = jnp.einsum('bign,bjgn->bijg', c, b)
        y = jnp.einsum('bijg,bijgh,bjgh,bjghp->bighp', cb, decay, d, x)
        y = y + jnp.einsum('bign,bghpn->bighp', c, S) * jnp.exp(cum)[..., None]
        w_end = jnp.exp(cum[:, -1:] - cum) * d
        S = S * jnp.exp(cum[:, -1])[..., None, None] + jnp.einsum('bjgh,bjghp,bjgn->bghpn', w_end, x, b)
        return S, y

    S0 = state0.astype(F32).reshape(bsz, SSD_GROUPS, hg, SSD_HEAD_DIM, D_STATE)
    S, ys = lax.scan(step, S0, (xc, dc, bc, cc))
    ys = jnp.moveaxis(ys, 0, 1).reshape(bsz, nc * q, SSD_HEADS, SSD_HEAD_DIM)[:, :L]
    return ys, S.reshape(bsz, SSD_HEADS, SSD_HEAD_DIM, D_STATE)


def _window_attention(qb, kb, vb, key_valid, sinks, slopes):
    lq, lk = qb.shape[2], kb.shape[2]
    dist = jnp.arange(lq)[:, None] + WINDOW - jnp.arange(lk)[None, :]
    mask = ((dist >= 0) & (dist <= WINDOW))[None] & key_valid[:, None, :]
    s = jnp.einsum('bnqhgd,bnkhd->bnhgqk', qb, kb).astype(F32) * (SWA_HEAD_DIM ** -0.5)
    s = s - slopes[:, :, None, None] * dist.astype(F32)
    s = jnp.where(mask[None, :, None, None], s, -jnp.inf)
    sink = sinks[:, :, None, None]
    m = jnp.maximum(jnp.max(s, axis=-1, keepdims=True), sink)
    p = jnp.exp(s - m)
    attn = p / (jnp.sum(p, axis=-1, keepdims=True) + jnp.exp(sink - m))
    return jnp.einsum('bnhgqk,bnkhd->bnqhgd', attn.astype(vb.dtype), vb)


def _mem_attention(q, mk, mv):
    s = jnp.einsum('bthd,bmhd->bhtm', q, mk).astype(F32) * (MEM_HEAD_DIM ** -0.5)
    attn = jax.nn.softmax(s, axis=-1).astype(mv.dtype)
    return jnp.einsum('bhtm,bmhd->bthd', attn, mv)


def _mem_kv(mem, g_mem, w_mem_kv, g_km):
    b, m, _ = mem.shape
    k, v = jnp.split(_rmsnorm(mem, g_mem) @ w_mem_kv, 2, axis=-1)
    k = _rmsnorm(k.reshape(b, m, MEM_HEADS, MEM_HEAD_DIM), g_km)
    return k, v.reshape(b, m, MEM_HEADS, MEM_HEAD_DIM)


def _peer(h, w_q, sub_keys, u_tab, v_tab):
    T = h.shape[0]
    q = (h @ w_q).reshape(T, PEER_HEADS, 2, PEER_HALF)
    s = jnp.einsum('thcd,hcnd->thcn', q, sub_keys).astype(F32)
    v_top, i_top = lax.top_k(s, PEER_TOPK)
    cand = (v_top[:, :, 0, :, None] + v_top[:, :, 1, None, :]).reshape(T, PEER_HEADS, PEER_TOPK * PEER_TOPK)
    cand_idx = (i_top[:, :, 0, :, None] * N_KEYS + i_top[:, :, 1, None, :]).reshape(T, PEER_HEADS, -1)
    sc, pos = lax.top_k(cand, PEER_TOPK)
    idx = jnp.take_along_axis(cand_idx, pos, axis=-1)
    gates = jax.nn.softmax(sc, axis=-1)
    pad = (-T) % PEER_TOKEN_BLOCK
    nb = (T + pad) // PEER_TOKEN_BLOCK
    hb = jnp.pad(h, ((0, pad), (0, 0))).reshape(nb, PEER_TOKEN_BLOCK, D_MODEL)
    ib = jnp.pad(idx, ((0, pad), (0, 0), (0, 0))).reshape(nb, PEER_TOKEN_BLOCK, PEER_HEADS, PEER_TOPK)
    gb = jnp.pad(gates, ((0, pad), (0, 0), (0, 0))).reshape(nb, PEER_TOKEN_BLOCK, PEER_HEADS, PEER_TOPK)

    def block(args):
        hx, ix, gx = args
        a = jnp.einsum('thkd,td->thk', u_tab[ix], hx).astype(F32)
        w = (gx * jax.nn.gelu(a, approximate=False)).astype(hx.dtype)
        return jnp.einsum('thk,thkd->td', w, v_tab[ix])

    out = lax.map(block, (hb, ib, gb))
    return out.reshape(nb * PEER_TOKEN_BLOCK, D_MODEL)[:T]


def _layer(x, mem_k, mem_v, conv_buf, ssm_state, win_k, win_v, slopes, p):
    B, L, _ = x.shape
    kvh, gq, hd = SWA_KV_HEADS, SWA_HEADS // SWA_KV_HEADS, SWA_HEAD_DIM
    h = _rmsnorm(x, p['g_mix'])
    z, xbc, dt_raw, q, k, v, qm, gate_pre = _split_columns(h @ p['w_in'])

    xbc, new_conv = _causal_dwconv(xbc, conv_buf, p['conv_w'], p['conv_b'])
    xbc = jax.nn.silu(xbc)
    gn = SSD_GROUPS * D_STATE
    xs = xbc[..., :D_INNER].reshape(B, L, SSD_HEADS, SSD_HEAD_DIM)
    bm = xbc[..., D_INNER:D_INNER + gn].reshape(B, L, SSD_GROUPS, D_STATE)
    cm = xbc[..., D_INNER + gn:].reshape(B, L, SSD_GROUPS, D_STATE)
    dt = jax.nn.softplus((dt_raw + p['dt_bias']).astype(F32))
    A = -jnp.exp(p['a_log'].astype(F32))
    y, new_ssm = _ssd_scan(xs, dt, A, bm, cm, ssm_state)
    y = y.astype(x.dtype) + p['d_skip'][:, None] * xs
    y = y.reshape(B, L, D_INNER) * jax.nn.silu(z)
    y = _rmsnorm(y.reshape(B, L, SSD_GROUPS, D_INNER // SSD_GROUPS),
                 p['g_ssd_norm'].reshape(SSD_GROUPS, D_INNER // SSD_GROUPS)).reshape(B, L, D_INNER)
    br_ssm = y @ p['w_o_ssm']

    q = _rmsnorm(q.reshape(B, L, SWA_HEADS, hd), p['g_q'])
    k = _rmsnorm(k.reshape(B, L, kvh, hd), p['g_k'])
    v = v.reshape(B, L, kvh, hd)
    if win_k is None:
        nb = L // WINDOW
        qb = q.reshape(B, nb, WINDOW, kvh, gq, hd)
        kc = k.reshape(B, nb, WINDOW, kvh, hd)
        vc = v.reshape(B, nb, WINDOW, kvh, hd)
        kb = jnp.concatenate([jnp.concatenate([jnp.zeros_like(kc[:, :1]), kc[:, :-1]], axis=1), kc], axis=2)
        vb = jnp.concatenate([jnp.concatenate([jnp.zeros_like(vc[:, :1]), vc[:, :-1]], axis=1), vc], axis=2)
        valid = (jnp.arange(nb)[:, None] > 0) | (jnp.arange(2 * WINDOW)[None, :] >= WINDOW)
        new_wk, new_wv = k[:, -WINDOW:], v[:, -WINDOW:]
    else:
        qb = q.reshape(B, 1, L, kvh, gq, hd)
        k_all = jnp.concatenate([win_k.astype(k.dtype), k], axis=1)
        v_all = jnp.concatenate([win_v.astype(v.dtype), v], axis=1)
        kb, vb = k_all[:, None], v_all[:, None]
        valid = jnp.ones((1, WINDOW + L), bool)
        new_wk, new_wv = k_all[:, -WINDOW:], v_all[:, -WINDOW:]
    o = _window_attention(qb, kb, vb, valid, p['sinks'].astype(F32).reshape(kvh, gq),
                          slopes.reshape(kvh, gq)).reshape(B, L, SWA_HEADS * hd)
    br_swa = o @ p['w_o_swa']

    qm = _rmsnorm(qm.reshape(B, L, MEM_HEADS, MEM_HEAD_DIM), p['g_qm'])
    br_mem = _mem_attention(qm, mem_k.astype(qm.dtype), mem_v.astype(qm.dtype)).reshape(B, L, -1) @ p['w_o_mem']

    g = jax.nn.sigmoid((gate_pre + p['b_gate']).astype(F32)).astype(x.dtype).reshape(B, L, N_BRANCH, D_MODEL)
    merged = g[:, :, 0] * br_ssm + g[:, :, 1] * br_swa + g[:, :, 2] * br_mem
    x = x + merged @ p['w_out']

    h2 = _rmsnorm(x, p['g_ffn']).reshape(B * L, D_MODEL)
    x = x + _peer(h2, p['w_peer_q'], p['peer_sub_keys'], p['peer_u'], p['peer_v']).reshape(B, L, D_MODEL)
    return x, new_conv, new_ssm.astype(x.dtype), new_wk, new_wv


def setup_inputs(seed: int = 0) -> dict:
    key = jax.random.key(seed)
    ks = jax.random.split(key, 40)
    nrm = lambda k, shape, scale: jax.random.normal(k, shape, F32) * scale
    gain = lambda k, shape: 1.0 + 0.02 * jax.random.normal(k, shape, F32)
    dt0 = jnp.exp(jax.random.uniform(ks[13], (DEPTH, SSD_HEADS), F32, math.log(1e-3), math.log(1e-1)))
    return {
        'x_prompt': nrm(ks[0], (BATCH, SEQ, D_MODEL), 1.0),
        'x_sample': nrm(ks[1], (DEC_BATCH, DEC_SEQ, D_MODEL), 1.0),
        'cache_win_k': nrm(ks[2], (DEPTH, DEC_BATCH, WINDOW, SWA_KV_HEADS, SWA_HEAD_DIM), 1.0),
        'cache_win_v': nrm(ks[3], (DEPTH, DEC_BATCH, WINDOW, SWA_KV_HEADS, SWA_HEAD_DIM), 1.0),
        'state_ssm': nrm(ks[4], (DEPTH, DEC_BATCH, SSD_HEADS, SSD_HEAD_DIM, D_STATE), 1.0),
        'state_conv': nrm(ks[5], (DEPTH, DEC_BATCH, CONV_W - 1, CONV_DIM), 1.0),
        'cache_mem_k': nrm(ks[6], (DEPTH, DEC_BATCH, N_MEM, MEM_HEADS, MEM_HEAD_DIM), 1.0),
        'cache_mem_v': nrm(ks[7], (DEPTH, DEC_BATCH, N_MEM, MEM_HEADS, MEM_HEAD_DIM), 1.0),
        'mem_prompt': nrm(ks[8], (BATCH, N_MEM, D_MODEL), 1.0),
        'w_in': nrm(ks[9], (DEPTH, D_MODEL, IN_DIM), D_MODEL ** -0.5),
        'b_gate': nrm(ks[10], (DEPTH, N_BRANCH * D_MODEL), 0.02),
        'conv_w': nrm(ks[11], (DEPTH, CONV_W, CONV_DIM), CONV_W ** -0.5),
        'conv_b': nrm(ks[12], (DEPTH, CONV_DIM), 0.02),
        'dt_bias': dt0 + jnp.log(-jnp.expm1(-dt0)),
        'a_log': jnp.log(jax.random.uniform(ks[14], (DEPTH, SSD_HEADS), F32, 1.0, 16.0)),
        'd_skip': gain(ks[15], (DEPTH, SSD_HEADS)),
        'g_ssd_norm': gain(ks[16], (DEPTH, D_INNER)),
        'g_q': gain(ks[17], (DEPTH, SWA_HEAD_DIM)),
        'g_k': gain(ks[18], (DEPTH, SWA_HEAD_DIM)),
        'attn_sinks': nrm(ks[19], (DEPTH, SWA_HEADS), 0.5),
        'g_mem': gain(ks[20], (DEPTH, D_MODEL)),
        'w_mem_kv': nrm(ks[21], (DEPTH, D_MODEL, 2 * MEM_HEADS * MEM_HEAD_DIM), D_MODEL ** -0.5),
        'g_qm': gain(ks[22], (DEPTH, MEM_HEAD_DIM)),
        'g_km': gain(ks[23], (DEPTH, MEM_HEAD_DIM)),
        'w_o_ssm': nrm(ks[24], (DEPTH, D_INNER, D_MODEL), D_INNER ** -0.5),
        'w_o_swa': nrm(ks[25], (DEPTH, SWA_HEADS * SWA_HEAD_DIM, D_MODEL), (SWA_HEADS * SWA_HEAD_DIM) ** -0.5),
        'w_o_mem': nrm(ks[26], (DEPTH, MEM_HEADS * MEM_HEAD_DIM, D_MODEL), (MEM_HEADS * MEM_HEAD_DIM) ** -0.5),
        'w_out': nrm(ks[27], (DEPTH, D_MODEL, D_MODEL), D_MODEL ** -0.5),
        'g_mix': gain(ks[28], (DEPTH, D_MODEL)),
        'g_ffn': gain(ks[29], (DEPTH, D_MODEL)),
        'w_peer_q': nrm(ks[30], (DEPTH, D_MODEL, PEER_HEADS * PEER_QUERY_DIM), D_MODEL ** -0.5),
        'peer_sub_keys': nrm(ks[31], (DEPTH, PEER_HEADS, 2, N_KEYS, PEER_HALF), PEER_HALF ** -0.5),
        'peer_u': nrm(ks[32], (DEPTH, N_EXPERTS, D_MODEL), D_MODEL ** -0.5),
        'peer_v': nrm(ks[33], (DEPTH, N_EXPERTS, D_MODEL), (PEER_HEADS * PEER_TOPK) ** -0.5),
    }


def reference(x_prompt, x_sample, cache_win_k, cache_win_v, state_ssm, state_conv, cache_mem_k, cache_mem_v,
              mem_prompt, w_in, b_gate, conv_w, conv_b, dt_bias, a_log, d_skip, g_ssd_norm, g_q, g_k,
              attn_sinks, g_mem, w_mem_kv, g_qm, g_km, w_o_ssm, w_o_swa, w_o_mem, w_out, g_mix, g_ffn,
              w_peer_q, peer_sub_keys, peer_u, peer_v):
    slopes = _alibi_slopes(SWA_HEADS)
    bp = x_prompt.shape[0]
    yp, ys = x_prompt, x_sample
    p_wk, p_wv, p_ssm, p_conv, p_mk, p_mv = [], [], [], [], [], []
    s_wk, s_wv, s_ssm, s_conv = [], [], [], []
    for l in range(DEPTH):
        prm = {'w_in': w_in[l], 'b_gate': b_gate[l], 'conv_w': conv_w[l], 'conv_b': conv_b[l],
               'dt_bias': dt_bias[l], 'a_log': a_log[l], 'd_skip': d_skip[l], 'g_ssd_norm': g_ssd_norm[l],
               'g_q': g_q[l], 'g_k': g_k[l], 'sinks': attn_sinks[l], 'g_qm': g_qm[l],
               'w_o_ssm': w_o_ssm[l], 'w_o_swa': w_o_swa[l], 'w_o_mem': w_o_mem[l], 'w_out': w_out[l],
               'g_mix': g_mix[l], 'g_ffn': g_ffn[l], 'w_peer_q': w_peer_q[l],
               'peer_sub_keys': peer_sub_keys[l], 'peer_u': peer_u[l], 'peer_v': peer_v[l]}
        mk, mv = _mem_kv(mem_prompt, g_mem[l], w_mem_kv[l], g_km[l])
        conv0 = jnp.zeros((bp, CONV_W - 1, CONV_DIM), x_prompt.dtype)
        ssm0 = jnp.zeros((bp, SSD_HEADS, SSD_HEAD_DIM, D_STATE), x_prompt.dtype)
        yp, pc, ps, pk, pv = _layer(yp, mk, mv, conv0, ssm0, None, None, slopes, prm)
        p_wk.append(pk); p_wv.append(pv); p_ssm.append(ps); p_conv.append(pc); p_mk.append(mk); p_mv.append(mv)
        ys, sc, ss, sk, sv = _layer(ys, cache_mem_k[l], cache_mem_v[l], state_conv[l], state_ssm[l],
                                    cache_win_k[l], cache_win_v[l], slopes, prm)
        s_wk.append(sk); s_wv.append(sv); s_ssm.append(ss); s_conv.append(sc)
    return (yp, ys, jnp.stack(p_wk), jnp.stack(p_wv), jnp.stack(p_ssm), jnp.stack(p_conv),
            jnp.stack(p_mk), jnp.stack(p_mv), jnp.stack(s_wk), jnp.stack(s_wv), jnp.stack(s_ssm),
            jnp.stack(s_conv))
```

```python
import numpy as np
from contextlib import ExitStack
import concourse.bass as bass
import concourse.mybir as mybir
from concourse.bass_utils import run_bass_kernel_spmd

F32 = mybir.dt.float32
BF16 = mybir.dt.bfloat16
I32 = mybir.dt.int32
U32 = mybir.dt.uint32
ALU = mybir.AluOpType
AF = mybir.ActivationFunctionType
AX = mybir.AxisListType

SELF_SYNC = True
SEM_EPOCH = 30000
D = 2048
IN_DIM = 15904
OFF_Z, OFF_XBC, OFF_DT, OFF_Q, OFF_K, OFF_V, OFF_QM, OFF_G = 0, 2048, 5120, 5152, 7200, 7456, 7712, 9760
EPS = 1e-6
GSZ = 3200


class Res:
    __slots__ = ("name", "w", "r")

    def __init__(self, name):
        self.name = name
        self.w = None
        self.r = {}


class KB:
    def __init__(self, nc, stack, n_dma_sems=20):
        self.nc = nc
        self.stack = stack
        self.engs = {"pe": nc.tensor, "dve": nc.vector, "act": nc.scalar,
                     "pool": nc.gpsimd, "sp": nc.sync}
        self.sem = {}
        self.semidx = {}
        self.cnt = {}
        for k in self.engs:
            self._new_sem(k)
        self.waited = {k: {} for k in self.engs}
        self.dma_sems = [stack.enter_context(nc.semaphore("dq%d" % i)) for i in range(n_dma_sems)]
        self.dma_n = 0
        self.dma_tgt = [0] * n_dma_sems
        self.ninst = 0

    def _new_sem(self, k):
        i = self.semidx.get(k, -1) + 1
        self.semidx[k] = i
        self.sem[(k, i)] = self.stack.enter_context(self.nc.semaphore("s_%s_%d" % (k, i)))
        self.cnt[k] = 0

    def _semof(self, src):
        if src[0] == "e":
            return self.sem[(src[1], src[2])]
        return self.dma_sems[src[1]]

    def _collect(self, reads, writes):
        deps = {}

        def add(s, c):
            if deps.get(s, 0) < c:
                deps[s] = c
        for r in reads:
            if r.w:
                add(*r.w)
        for w in writes:
            if w.w:
                add(*w.w)
            for s, c in w.r.items():
                add(s, c)
        return deps

    def _emit_waits(self, eng, deps):
        e = self.engs[eng]
        for s, c in deps.items():
            if s[0] == "e" and s[1] == eng:
                if eng in ("pe", "sp") or not SELF_SYNC:
                    continue
            if self.waited[eng].get(s, 0) >= c:
                continue
            e.wait_ge(self._semof(s), c)
            self.waited[eng][s] = c

    def op(self, eng, emit, reads=(), writes=()):
        deps = self._collect(reads, writes)
        self._emit_waits(eng, deps)
        ins = emit(self.engs[eng])
        if self.cnt[eng] >= SEM_EPOCH:
            self._new_sem(eng)
        self.cnt[eng] += 1
        key = ("e", eng, self.semidx[eng])
        ins.then_inc(self.sem[(eng, self.semidx[eng])], 1)
        c = self.cnt[eng]
        for r in reads:
            r.r[key] = c
        for w in writes:
            w.w = (key, c)
            w.r = {}
        self.ninst += 1
        return ins

    def dma(self, q, out, in_, reads=(), writes=(), indirect=None, **kw):
        deps = self._collect(reads, writes)
        slot = self.dma_n % len(self.dma_sems)
        self.dma_n += 1
        if self.dma_tgt[slot] > 0:
            deps[("d", slot)] = max(deps.get(("d", slot), 0), self.dma_tgt[slot])
        self._emit_waits(q, deps)
        e = self.engs[q]
        if indirect is not None:
            ins = e.indirect_dma_start(out, None, in_, indirect, **kw)
        else:
            ins = e.dma_start(out, in_, **kw)
        self.dma_tgt[slot] += 16
        ins.then_inc(self.dma_sems[slot], 16)
        key = ("d", slot)
        c = self.dma_tgt[slot]
        for r in reads:
            r.r[key] = c
        for w in writes:
            w.w = (key, c)
            w.r = {}
        self.ninst += 1
        return ins

    def dma_barrier(self, eng):
        e = self.engs[eng]
        for i, t in enumerate(self.dma_tgt):
            if t > 0 and self.waited[eng].get(("d", i), 0) < t:
                e.wait_ge(self.dma_sems[i], t)
                self.waited[eng][("d", i)] = t

    def wait_all(self, eng, ress):
        deps = {}
        for r in ress:
            if r.w and deps.get(r.w[0], 0) < r.w[1]:
                deps[r.w[0]] = r.w[1]
            for s, c in r.r.items():
                if deps.get(s, 0) < c:
                    deps[s] = c
        e = self.engs[eng]
        for s, c in deps.items():
            e.wait_ge(self._semof(s), c)


def build(NPT, NSQ, n_layers=2, do_peer=True):
    nc = bass.Bass("TRN2", target_bir_lowering=False)

    def din(name, shape, dt=F32):
        return nc.dram_tensor(name, list(shape), dt, kind="ExternalInput").ap()

    def dout(name, shape):
        return nc.dram_tensor(name, list(shape), F32, kind="ExternalOutput").ap()

    xp = din("xp", [max(NPT, 1) * 128, D])
    xsm = din("xsm", [max(NSQ, 1) * 8, D])
    cwk = din("cwk", [2, max(NSQ, 1), 128, 256])
    cwv = din("cwv", [2, max(NSQ, 1), 128, 256])
    cwkT = din("cwkT", [2, max(NSQ, 1), 64, 4, 128])
    sst = din("sst", [2, max(NSQ, 1), 128, 2048])
    scv = din("scv", [2, max(NSQ, 1), 128, 24, 3])
    cmkT = din("cmkT", [2, max(NSQ, 1), 128, 16, 256])
    cmv = din("cmv", [2, max(NSQ, 1), 256, 2048])
    memp = din("memp", [256, D])
    w_in = din("w_in", [2, D, IN_DIM])
    w_mem_kv = din("w_mem_kv", [2, D, 4096])
    w_o_ssm = din("w_o_ssm", [2, D, D])
    w_o_swa = din("w_o_swa", [2, D, D])
    w_o_mem = din("w_o_mem", [2, D, D])
    w_out = din("w_out", [2, D, D])
    w_peer_q = din("w_peer_q", [2, D, D])
    peer_u = din("peer_u", [2 * 16384, D])
    peer_v = din("peer_v", [2 * 16384, D])
    skT_d = din("skT", [2, 128, 16, 128])
    NPC = 2 * 200
    pcols_d = din("pcols", [128, NPC])
    NPR = 2 * 384
    prow_d = din("prow", [128, NPR])
    gkm_bc = din("gkm_bc", [2, 128, D])
    gffn_bc = din("gffn_bc", [2, 128, D])
    bgate = din("bgate", [2, 1, 6144])
    ident_d = din("ident", [128, 128])
    triu_d = din("triu", [128, 128])
    emask_d = din("emask", [4, 128, 2048])

    yp = dout("yp", [max(NPT, 1) * 128, D])
    ys = dout("ys", [max(NSQ, 1) * 8, D])
    pwk = dout("pwk", [2, 128, 256])
    pwv = dout("pwv", [2, 128, 256])
    pssm = dout("pssm", [2, 2048, 128])
    pconv = dout("pconv", [2, 3, 3072])
    pmk = dout("pmk", [2, 256, D])
    pmv = dout("pmv", [2, 256, D])
    swk = dout("swk", [2, max(NSQ, 1), 128, 256])
    swv = dout("swv", [2, max(NSQ, 1), 128, 256])
    sssm = dout("sssm", [2, max(NSQ, 1), 2048, 128])
    sconv = dout("sconv", [2, max(NSQ, 1), 3, 3072])
    pmkT = nc.dram_tensor("pmkT", [2, 128, 16, 256], F32, kind="Internal").ap()
    NBLK = 52
    wsc = nc.dram_tensor("wsc", [2 * NBLK, 128, 8192], BF16, kind="Internal").ap()
    usc = nc.dram_tensor("usc", [2 * 16384, D], BF16, kind="Internal").ap()
    vsc = nc.dram_tensor("vsc", [2 * 16384, D], BF16, kind="Internal").ap()
    BID = {"in": 0, "dt": 31, "ossm": 32, "oswa": 36, "omem": 44, "out": 48}

    out_res = []

    def ores(name):
        r = Res(name)
        out_res.append(r)
        return r

    with ExitStack() as st:
        kb = KB(nc, st)

        def sb(name, shape, dt=F32):
            t = st.enter_context(nc.sbuf_tensor(name, list(shape), dt))
            return t, Res(name)

        def V(fn, r=(), w=()):
            return kb.op("dve", fn, reads=r, writes=w)

        def A(fn, r=(), w=()):
            return kb.op("act", fn, reads=r, writes=w)

        def P(fn, r=(), w=()):
            return kb.op("pe", fn, reads=r, writes=w)

        def G(fn, r=(), w=()):
            return kb.op("pool", fn, reads=r, writes=w)

        X, Xr = sb("X", [128, D])
        XN, XNr = sb("XN", [128, D])
        HT, HTr = sb("HT", [128, 16, 128])
        HTb = HT[:, 0:8, :].rearrange("p c t -> p (c t)").bitcast(BF16).rearrange("p (c t) -> p c t", t=128)
        MG, MGr = sb("MG", [128, D])
        WB = [sb("WB%d" % i, [128, 4096]) for i in range(4)]
        Gb = [sb("G%d" % i, [128, GSZ if i == 1 else 3072]) for i in range(6)]
        S = [sb("S%d" % l, [128, D]) for l in range(2)]
        CT = [sb("CT%d" % l, [128, 24, 3]) for l in range(2)]
        KTP = [sb("KTP%d" % l, [64, 4, 128]) for l in range(2)]
        VP = [sb("VP%d" % l, [128, 256]) for l in range(2)]
        KTC, KTCr = sb("KTC", [64, 4, 128])
        ident, identr = sb("ident_s", [128, 128])
        triu, triur = sb("triu_s", [128, 128])
        ones, onesr = sb("ones_s", [128, 128])
        pcols, pcolsr = sb("pcols_s", [128, NPC])
        prow, prowr = sb("prow_s", [128, NPR])
        epst, epstr = sb("epst", [128, 1])
        onec, onecr = sb("onec", [128, 1])
        iot16, iot16r = sb("iot16", [128, 16])
        SM = {}
        for nm, w_ in [("ss", 40), ("rs", 40), ("dt", 32), ("dA", 32), ("cum", 32), ("ncum", 32), ("ecum", 32),
                       ("cl", 32), ("wend", 32), ("A", 32), ("esink", 32), ("bg", 512), ("mx", 8), ("sm", 8),
                       ("vtop", 256), ("itopf", 256), ("t8", 8)]:
            SM[nm] = sb("sm_" + nm, [128, w_])
        itop_u, itop_ur = sb("itop_u", [128, 256], U32)
        pos_u, pos_ur = sb("pos_u", [128, 128], U32)
        idx_i, idx_ir = sb("idx_i", [128, 128], I32)
        PS = [(st.enter_context(nc.psum_tensor("ps%d" % i, [128, 512], F32)), Res("ps%d" % i)) for i in range(8)]

        pmv_r = [Res("pmv%d" % l) for l in range(2)]
        pmkT_r = [Res("pmkT%d" % l) for l in range(2)]
        ores_pmk = ores("pmk")

        kb.dma("sp", ident[:], ident_d, writes=[identr])
        kb.dma("sp", triu[:], triu_d, writes=[triur])
        kb.dma("sp", pcols[:], pcols_d, writes=[pcolsr])
        kb.dma("sp", prow[:], prow_d, writes=[prowr])
        G(lambda e: e.memset(ones[:], 1.0), w=[onesr])
        G(lambda e: e.memset(epst[:], EPS), w=[epstr])
        G(lambda e: e.memset(onec[:], 1.0), w=[onecr])
        G(lambda e: e.iota(iot16[:], [[1, 16]], base=0, channel_multiplier=0, allow_small_or_imprecise_dtypes=True),
          w=[iot16r])

        def pc(l, off, n=1):
            return pcols[:, l * 200 + off: l * 200 + off + n]

        def pr(l, off, n):
            return prow[:, l * 384 + off: l * 384 + off + n]

        wctr = [0]

        def proj(lhs_fn, nk, kp, wsrc, ncols, T, ps_ap, ps_res, lhs_res, extra=None):
            for c0 in range(0, ncols, 256):
                n_ = min(256, ncols - c0)
                i = wctr[0] % 4
                wctr[0] += 1
                wt, wr = WB[i]
                wv = wt[0:kp, 0:nk * n_].rearrange("p (c n) -> p c n", n=n_)
                kb.dma("sp", wv, wsrc[:, c0:c0 + n_].rearrange("(c p) n -> p c n", p=kp), writes=[wr])
                for c in range(nk):
                    P(lambda e: e.matmul(ps_ap[:, c0:c0 + n_], lhs_fn(c), wv[:, c, :], start=(c == 0), stop=(c == nk - 1)),
                      r=[lhs_res, wr], w=[ps_res])

        def projb(lhs_fn, nk, kp, bid, ncols, T, ps_ap, ps_res, lhs_res, extra=None):
            i = wctr[0] % 4
            wctr[0] += 1
            wt, wr = WB[i]
            wv = wt[0:kp, :].bitcast(BF16)[:, 0:nk * ncols].rearrange("p (c n) -> p c n", n=ncols)
            kb.dma("sp", wt[0:kp, :].bitcast(BF16)[:, 0:nk * ncols], wsc[bid][0:kp, 0:nk * ncols], writes=[wr])
            for c in range(nk):
                last = (c == nk - 1) and extra is None
                P(lambda e: e.matmul(ps_ap, lhs_fn(c), wv[:, c, :], start=(c == 0), stop=last),
                  r=[lhs_res, wr], w=[ps_res])
            if extra is not None:
                P(lambda e: e.matmul(ps_ap, extra[0], extra[1], start=False, stop=True),
                  r=extra[2], w=[ps_res])

        def precast_all():
            jobs = []
            for l in range(n_layers):
                base = l * NBLK

                def std(src, c0, bid):
                    us = []
                    for hf in range(2):
                        v = src[hf * 1024:(hf + 1) * 1024, c0:c0 + 512].rearrange("(c p) n -> p c n", p=128)
                        us.append((v, 8, 512, hf * 4096))
                    jobs.append((128, us, ("w", bid, 8192)))
                cols = [OFF_Z + i * 512 for i in range(4)] + [OFF_XBC + i * 512 for i in range(6)] + \
                       [OFF_Q + i * 512 for i in range(4)] + [OFF_K] + [OFF_QM + i * 512 for i in range(4)] + \
                       [OFF_G + i * 512 for i in range(12)]
                for i, c0 in enumerate(cols):
                    std(w_in[l], c0, base + BID["in"] + i)
                jobs.append((128, [(w_in[l][:, OFF_DT:OFF_DT + 32].rearrange("(c p) n -> p c n", p=128), 16, 32, 0)],
                             ("w", base + BID["dt"], 512)))
                for nm, wm in (("ossm", w_o_ssm), ("omem", w_o_mem), ("out", w_out)):
                    for i in range(4):
                        std(wm[l], i * 512, base + BID[nm] + i)
                for i in range(8):
                    us = []
                    for hf in range(2):
                        v = w_o_swa[l][hf * 1024:(hf + 1) * 1024, i * 256:(i + 1) * 256].rearrange("(c p) n -> p c n", p=64)
                        us.append((v, 16, 256, hf * 4096))
                    jobs.append((64, us, ("w", base + BID["oswa"] + i, 8192)))
            if do_peer:
                for (src, dst) in ((peer_u, usc), (peer_v, vsc)):
                    for blk in range(64 * n_layers):
                        r0 = blk * 256
                        jobs.append((128, [(src[r0:r0 + 256, :].rearrange("(a p) d -> p a d", p=128), 2, D, 0)],
                                     ("t", dst[r0:r0 + 256, :].rearrange("(a p) d -> p a d", p=128), 4096)))
            units = []
            for ji, (kp, us, dst) in enumerate(jobs):
                for ui, u in enumerate(us):
                    units.append((ji, kp, u, ui == len(us) - 1, dst))

            def issue_in(k):
                ji, kp, (src, a_, n_, off), last, dst = units[k]
                wt, wr = WB[k % 2]
                kb.dma("sp", wt[0:kp, 0:a_ * n_].rearrange("p (c n) -> p c n", n=n_), src, writes=[wr])
            if units:
                issue_in(0)
            for k in range(len(units)):
                if k + 1 < len(units):
                    issue_in(k + 1)
                ji, kp, (src, a_, n_, off), last, dst = units[k]
                wt, wr = WB[k % 2]
                stg, stgr = WB[2 + ji % 2]
                stb = stg[0:kp, :].bitcast(BF16)
                dstv = stb[:, off:off + a_ * n_]
                srcv = wt[0:kp, 0:a_ * n_]
                e_ = k % 3
                if e_ == 0:
                    V(lambda e: e.tensor_copy(dstv, srcv), r=[wr], w=[stgr])
                elif e_ == 1:
                    A(lambda e: e.copy(dstv, srcv), r=[wr], w=[stgr])
                else:
                    G(lambda e: e.tensor_copy(dstv, srcv), r=[wr], w=[stgr])
                if last:
                    if dst[0] == "w":
                        kb.dma("sp", wsc[dst[1]][0:kp, 0:dst[2]], stb[:, 0:dst[2]], reads=[stgr], writes=[Res("wsc_tmp")])
                    else:
                        kb.dma("sp", dst[1], stb[:, 0:dst[2]].rearrange("p (a d) -> p a d", d=D), reads=[stgr],
                               writes=[Res("tsc_tmp")])
            kb.dma_barrier("sp")

        INB = {"z": 0, "xbc": 4, "q": 10, "kv": 14, "qm": 15, "g": 19}

        def rms_rstd(ss_ap, rs_ap, n, T, ssr, rsr):
            A(lambda e: e.activation(rs_ap, ss_ap, AF.Sqrt, bias=epst[:T, :], scale=1.0 / n), r=[ssr, epstr], w=[rsr])
            V(lambda e: e.reciprocal(rs_ap, rs_ap), r=[rsr], w=[rsr])

        def norm_T(src, srcr, T, gcol_fn, dst, dstr, scale=None):
            ss, ssr = SM["ss"]
            rs, rsr = SM["rs"]
            V(lambda e: e.memset(ss[:T, 0:1], 0.0), w=[ssr])
            A(lambda e: e.activation(XN[:T, :], src[:T, :], AF.Square, accum_out=ss[:T, 0:1]), r=[srcr], w=[XNr, ssr])
            rms_rstd(ss[:T, 0:1], rs[:T, 0:1], D, T, ssr, rsr)
            A(lambda e: e.activation(XN[:T, :], src[:T, :], AF.Copy, scale=rs[:T, 0:1]), r=[srcr, rsr], w=[XNr])
            transp(XN, XNr, T, 16, gcol_fn, dst, dstr)

        tctr = [0]

        def transp(src, srcr, T, nch, gcol_fn, dst, dstr, width=128, src_off=0):
            for c in range(nch):
                b = tctr[0] % 2
                tctr[0] += 1
                pt, ptr = PS[b]
                P(lambda e: e.transpose(pt[:width, 0:T], src[:T, src_off + c * width: src_off + (c + 1) * width],
                                        ident[:T, :T]), r=[srcr, identr], w=[ptr])
                if gcol_fn is None:
                    if c % 2 == 0:
                        V(lambda e: e.tensor_copy(dst[:width, c, 0:T], pt[:width, 0:T]), r=[ptr], w=[dstr])
                    else:
                        A(lambda e: e.copy(dst[:width, c, 0:T], pt[:width, 0:T]), r=[ptr], w=[dstr])
                else:
                    V(lambda e: e.tensor_scalar(dst[:width, c, 0:T], pt[:width, 0:T], gcol_fn(c), None, ALU.mult),
                      r=[ptr, pcolsr], w=[dstr])

        def mem_precompute(l):
            g0, g0r = Gb[0]
            g1, g1r = Gb[1]
            g2, g2r = Gb[2]
            kT = g2[:, 0:2048].rearrange("p (c t) -> p c t", t=128)
            kb.dma("sp", g1[:, 0:D], gkm_bc[l], writes=[g1r])
            for mt in range(2):
                kb.dma("sp", X[:, :], memp[mt * 128:(mt + 1) * 128, :], writes=[Xr])
                norm_T(X, Xr, 128, lambda c: pc(l, 32 + c), HT, HTr)
                for blk in range(8):
                    pt, ptr = PS[2 + blk % 2]
                    proj(lambda c: HT[:, c, :], 16, 128, w_mem_kv[l][:, blk * 512:(blk + 1) * 512], 512, 128,
                         pt[:, :], ptr, HTr)
                    if blk < 4:
                        ss, ssr = SM["ss"]
                        rs, rsr = SM["rs"]
                        V(lambda e: e.memset(ss[:, 1:2], 0.0), w=[ssr])
                        A(lambda e: e.activation(g0[:, blk * 512:(blk + 1) * 512], pt[:, :], AF.Square,
                                                 accum_out=ss[:, 1:2]), r=[ptr], w=[g0r, ssr])
                        rms_rstd(ss[:, 1:2], rs[:, 1:2], 512, 128, ssr, rsr)
                        V(lambda e: e.scalar_tensor_tensor(g0[:, blk * 512:(blk + 1) * 512], pt[:, :], rs[:, 1:2],
                                                           g1[:, blk * 512:(blk + 1) * 512], ALU.mult, ALU.mult),
                          r=[ptr, rsr, g1r], w=[g0r])
                    else:
                        A(lambda e: e.copy(MG[:, (blk - 4) * 512:(blk - 3) * 512], pt[:, :]), r=[ptr], w=[MGr])
                kb.dma("sp", pmk[l][mt * 128:(mt + 1) * 128, :], g0[:, 0:D], reads=[g0r], writes=[ores_pmk])
                kb.dma("sp", pmv[l][mt * 128:(mt + 1) * 128, :], MG[:, :], reads=[MGr], writes=[pmv_r[l]])
                transp(g0, g0r, 128, 16, None, kT, g2r)
                kb.dma("sp", pmkT[l][:, :, mt * 128:(mt + 1) * 128], kT, reads=[g2r], writes=[pmkT_r[l]])

        def tile_layer(T, l, has_prev, memK_ap, memK_r, memV_ap, memV_r, conv_out_ap, conv_out_r, win_out=None, run_peer=True):
            Sl, Slr = S[l]
            CTl, CTlr = CT[l]
            KTPl, KTPlr = KTP[l]
            VPl, VPlr = VP[l]
            ss, ssr = SM["ss"]
            rs, rsr = SM["rs"]
            g0, g0r = Gb[0]
            g1, g1r = Gb[1]
            g2, g2r = Gb[2]
            g3, g3r = Gb[3]
            g4, g4r = Gb[4]
            g5, g5r = Gb[5]
            bg, bgr = SM["bg"]

            wb0 = l * NBLK
            norm_T(X, Xr, T, lambda c: pc(l, c), HTb, HTr)

            def gate_merge(br, blk, br_ps, br_psr, first):
                pg, pgr = PS[4 + blk % 2]
                col = OFF_G + br * 2048 + blk * 512
                kb.dma("sp", bg[0:1, :], bgate[l][:, br * 2048 + blk * 512: br * 2048 + (blk + 1) * 512], writes=[bgr])
                projb(lambda c: HTb[:, c, :T], 16, 128, wb0 + BID["in"] + INB["g"] + br * 4 + blk, 512, T, pg[:T, :], pgr, HTr,
                      extra=(ones[0:1, 0:T], bg[0:1, 0:512], [onesr, bgr]))
                gs = g1[:T, 2600:3112]
                A(lambda e: e.activation(gs, pg[:T, :], AF.Sigmoid), r=[pgr], w=[g1r])
                mgs = MG[:T, blk * 512:(blk + 1) * 512]
                if first:
                    V(lambda e: e.tensor_tensor(mgs, gs, br_ps, ALU.mult), r=[g1r, br_psr], w=[MGr])
                else:
                    V(lambda e: e.tensor_tensor(gs, gs, br_ps, ALU.mult), r=[g1r, br_psr], w=[g1r])
                    V(lambda e: e.tensor_tensor(mgs, mgs, gs, ALU.add), r=[g1r, MGr], w=[MGr])

            def out_branch(br, lhs_fn, nk, kp, bid0, lhs_res, ncols):
                for blk in range(4):
                    pb, pbr = PS[6 + blk % 2]
                    projb(lhs_fn, nk, kp, bid0 + blk, 512, T, pb[:T, :], pbr, lhs_res)
                    gate_merge(br, blk, pb[:T, :], pbr, br == 0)

            xraw = g0
            for blk in range(6):
                pt, ptr = PS[2 + blk % 2]
                projb(lambda c: HTb[:, c, :T], 16, 128, wb0 + BID["in"] + INB["xbc"] + blk, 512, T,
                      pt[:T, :], ptr, HTr)
                if blk % 2 == 0:
                    A(lambda e: e.copy(xraw[:T, blk * 512:(blk + 1) * 512], pt[:T, :]), r=[ptr], w=[g0r])
                else:
                    V(lambda e: e.tensor_copy(xraw[:T, blk * 512:(blk + 1) * 512], pt[:T, :]), r=[ptr], w=[g0r])
            if conv_out_ap is not None:
                kb.dma("sp", conv_out_ap, xraw[T - 3:T, 0:3072], reads=[g0r], writes=[conv_out_r])
            xbcT = g1[:, 0:24 * 131].rearrange("p (c t) -> p c t", t=131)
            V(lambda e: e.tensor_copy(xbcT[:, :, 0:3], CTl[:, :, :]), r=[CTlr], w=[g1r])
            for c in range(24):
                b = tctr[0] % 2
                tctr[0] += 1
                pt, ptr = PS[b]
                P(lambda e: e.transpose(pt[:, 0:T], xraw[:T, c * 128:(c + 1) * 128], ident[:T, :T]),
                  r=[g0r, identr], w=[ptr])
                if c % 2 == 0:
                    V(lambda e: e.tensor_copy(xbcT[:, c, 3:3 + T], pt[:, 0:T]), r=[ptr], w=[g1r])
                else:
                    A(lambda e: e.copy(xbcT[:, c, 3:3 + T], pt[:, 0:T]), r=[ptr], w=[g1r])
            V(lambda e: e.tensor_copy(CTl[:, :, :], xbcT[:, :, T:T + 3]), r=[g1r], w=[CTlr])
            xact = g2[:, 0:24 * 128].rearrange("p (c t) -> p c t", t=128)
            for c in range(24):
                V(lambda e: e.tensor_scalar(xact[:, c, 0:T], xbcT[:, c, 0:T], pc(l, 68 + c), pc(l, 164 + c),
                                            ALU.mult, ALU.add), r=[g1r, pcolsr], w=[g2r])
                for k in range(1, 4):
                    V(lambda e: e.scalar_tensor_tensor(xact[:, c, 0:T], xbcT[:, c, k:k + T], pc(l, 68 + k * 24 + c),
                                                       xact[:, c, 0:T], ALU.mult, ALU.add),
                      r=[g1r, g2r, pcolsr], w=[g2r])
            A(lambda e: e.activation(xact[:, :, 0:T], xact[:, :, 0:T], AF.Silu), r=[g2r], w=[g2r])
            xs = g3
            xs3 = g3[:, 0:2048].rearrange("p (c t) -> p c t", t=128)
            transp_fm(xact, g2r, T, 0, 16, xs3, g3r)
            bt3 = g4[:, 0:512].rearrange("p (c t) -> p c t", t=128)
            transp_fm(xact, g2r, T, 16, 4, bt3, g4r)
            dt, dtr = SM["dt"]
            dA, dAr = SM["dA"]
            Aneg, Anegr = SM["A"]
            pt, ptr = PS[2]
            projb(lambda c: HTb[:, c, :T], 16, 128, wb0 + BID["dt"], 32, T, pt[:T, 0:32], ptr, HTr)
            V(lambda e: e.tensor_tensor(dt[:T, :], pt[:T, 0:32], pr(l, 0, 32)[:T, :], ALU.add), r=[ptr, prowr], w=[dtr])
            A(lambda e: e.activation(dt[:T, :], dt[:T, :], AF.Exp), r=[dtr], w=[dtr])
            A(lambda e: e.activation(dt[:T, :], dt[:T, :], AF.Ln, bias=onec[:T, :], scale=1.0), r=[dtr, onecr], w=[dtr])
            A(lambda e: e.activation(Aneg[:, :], pr(l, 32, 32), AF.Exp), r=[prowr], w=[Anegr])
            V(lambda e: e.scalar_tensor_tensor(dA[:T, :], dt[:T, :], -1.0, Aneg[:T, :], ALU.mult, ALU.mult),
              r=[dtr, Anegr], w=[dAr])
            cum, cumr = SM["cum"]
            cl, clr = SM["cl"]
            ecum, ecumr = SM["ecum"]
            wend, wendr = SM["wend"]
            pt, ptr = PS[3]
            P(lambda e: e.matmul(pt[:T, 0:32], triu[:T, :T], dA[:T, :], start=True, stop=True), r=[triur, dAr], w=[ptr])
            P(lambda e: e.matmul(pt[:, 32:64], ones[:T, :], dA[:T, :], start=True, stop=True), r=[onesr, dAr], w=[ptr])
            V(lambda e: e.tensor_copy(cum[:T, :], pt[:T, 0:32]), r=[ptr], w=[cumr])
            V(lambda e: e.tensor_copy(cl[:, :], pt[:, 32:64]), r=[ptr], w=[clr])
            A(lambda e: e.activation(ecum[:T, :], cum[:T, :], AF.Exp), r=[cumr], w=[ecumr])
            V(lambda e: e.tensor_tensor(wend[:T, :], cl[:T, :], cum[:T, :], ALU.subtract), r=[clr, cumr], w=[wendr])
            A(lambda e: e.activation(wend[:T, :], wend[:T, :], AF.Exp), r=[wendr], w=[wendr])
            V(lambda e: e.tensor_tensor(wend[:T, :], wend[:T, :], dt[:T, :], ALU.mult), r=[wendr, dtr], w=[wendr])
            cbm = g4[:, 512:1024].rearrange("p (c t) -> p c t", t=128)
            pt, ptr = PS[2]
            for gq in range(4):
                P(lambda e: e.matmul(pt[:T, gq * 128: gq * 128 + T], xact[:, 16 + gq, 0:T], xact[:, 20 + gq, 0:T],
                                     start=True, stop=True), r=[g2r], w=[ptr])
            for gq in range(4):
                V(lambda e: e.tensor_tensor(cbm[:T, gq, 0:T], pt[:T, gq * 128: gq * 128 + T], triu[:T, :T], ALU.mult),
                  r=[ptr, triur], w=[g4r])
            ysb = g5
            for hf in range(2):
                Rp = g0[:, 0:2048].rearrange("p (h t) -> p h t", t=128)
                for hh in range(16):
                    h = hf * 16 + hh
                    V(lambda e: e.tensor_scalar(Rp[:T, hh, 0:T], triu[:T, :T], dA[:T, h:h + 1], None, ALU.mult),
                      r=[triur, dAr], w=[g0r])
                MT = g1[:, 0:2048].rearrange("p (h t) -> p h t", t=128)
                for q4 in range(4):
                    pt, ptr = PS[4 + q4]
                    for hq in range(4):
                        hh = q4 * 4 + hq
                        P(lambda e: e.matmul(pt[:T, hq * 128: hq * 128 + T], ones[:T, :T], Rp[:T, hh, 0:T],
                                             start=True, stop=True), r=[onesr, g0r], w=[ptr])
                    for hq in range(4):
                        hh = q4 * 4 + hq
                        h = hf * 16 + hh
                        V(lambda e: e.tensor_scalar(MT[:T, hh, 0:T], pt[:T, hq * 128: hq * 128 + T], cum[:T, h:h + 1], 0.0,
                                                    ALU.subtract, ALU.min), r=[ptr, cumr], w=[g1r])
                A(lambda e: e.activation(MT[:T, :, 0:T], MT[:T, :, 0:T], AF.Exp), r=[g1r], w=[g1r])
                for hh in range(16):
                    h = hf * 16 + hh
                    V(lambda e: e.scalar_tensor_tensor(MT[:T, hh, 0:T], MT[:T, hh, 0:T], dt[:T, h:h + 1],
                                                       cbm[:T, h // 8, 0:T], ALU.mult, ALU.mult),
                      r=[g1r, dtr, g4r], w=[g1r])
                for hh in range(16):
                    h = hf * 16 + hh
                    pt, ptr = PS[hh // 8]
                    P(lambda e: e.matmul(pt[:T, (hh % 8) * 64:(hh % 8 + 1) * 64], MT[:T, hh, 0:T],
                                         xs[:T, h * 64:(h + 1) * 64], start=True, stop=True), r=[g1r, g3r], w=[ptr])
                for gq in range(2):
                    gg = hf * 2 + gq
                    pt, ptr = PS[2 + gq]
                    P(lambda e: e.matmul(pt[:T, :], xact[:, 20 + gg, 0:T], Sl[:, gg * 512:(gg + 1) * 512],
                                         start=True, stop=True), r=[g2r, Slr], w=[ptr])
                for gq in range(2):
                    gg = hf * 2 + gq
                    pi, pir = PS[gq]
                    pst, pstr = PS[2 + gq]
                    yv = ysb[:T, gg * 512:(gg + 1) * 512].rearrange("p (h d) -> p h d", d=64)
                    V(lambda e: e.tensor_tensor(yv, pst[:T, :].rearrange("p (h d) -> p h d", d=64),
                                                ecum[:T, gg * 8:(gg + 1) * 8].unsqueeze(2).to_broadcast([T, 8, 64]),
                                                ALU.mult), r=[pstr, ecumr], w=[g5r])
                    V(lambda e: e.tensor_tensor(ysb[:T, gg * 512:(gg + 1) * 512], ysb[:T, gg * 512:(gg + 1) * 512],
                                                pi[:T, :], ALU.add), r=[pir, g5r], w=[g5r])
            xw = g0
            V(lambda e: e.tensor_tensor(xw[:T, 0:2048].rearrange("p (h d) -> p h d", d=64),
                                        xs[:T, 0:2048].rearrange("p (h d) -> p h d", d=64),
                                        pr(l, 64, 32)[:T, :].unsqueeze(2).to_broadcast([T, 32, 64]), ALU.mult),
              r=[g3r, prowr], w=[g0r])
            V(lambda e: e.tensor_tensor(ysb[:T, 0:2048], ysb[:T, 0:2048], xw[:T, 0:2048], ALU.add), r=[g0r, g5r], w=[g5r])
            V(lambda e: e.tensor_tensor(xw[:T, 0:2048].rearrange("p (h d) -> p h d", d=64),
                                        xs[:T, 0:2048].rearrange("p (h d) -> p h d", d=64),
                                        wend[:T, :].unsqueeze(2).to_broadcast([T, 32, 64]), ALU.mult),
              r=[g3r, wendr], w=[g0r])
            A(lambda e: e.activation(cl[:, :], cl[:, :], AF.Exp), r=[clr], w=[clr])
            for gg in range(4):
                pt, ptr = PS[4 + gg]
                P(lambda e: e.matmul(pt[:, :], bt3[:T, gg, :], xw[:T, gg * 512:(gg + 1) * 512], start=True, stop=True),
                  r=[g4r, g0r], w=[ptr])
            V(lambda e: e.tensor_tensor(Sl[:, :].rearrange("p (h d) -> p h d", d=64),
                                        Sl[:, :].rearrange("p (h d) -> p h d", d=64),
                                        cl[:, :].unsqueeze(2).to_broadcast([128, 32, 64]), ALU.mult),
              r=[Slr, clr], w=[Slr])
            for gg in range(4):
                pt, ptr = PS[4 + gg]
                V(lambda e: e.tensor_tensor(Sl[:, gg * 512:(gg + 1) * 512], Sl[:, gg * 512:(gg + 1) * 512], pt[:, :],
                                            ALU.add), r=[Slr, ptr], w=[Slr])
            for blk in range(4):
                pt, ptr = PS[blk % 2]
                projb(lambda c: HTb[:, c, :T], 16, 128, wb0 + BID["in"] + INB["z"] + blk, 512, T,
                      pt[:T, :], ptr, HTr)
                zs = g1[:T, 0:512]
                A(lambda e: e.activation(zs, pt[:T, :], AF.Silu), r=[ptr], w=[g1r])
                V(lambda e: e.tensor_tensor(ysb[:T, blk * 512:(blk + 1) * 512], ysb[:T, blk * 512:(blk + 1) * 512], zs,
                                            ALU.mult), r=[g1r, g5r], w=[g5r])
                V(lambda e: e.memset(ss[:T, 4 + blk:5 + blk], 0.0), w=[ssr])
                A(lambda e: e.activation(zs, ysb[:T, blk * 512:(blk + 1) * 512], AF.Square,
                                         accum_out=ss[:T, 4 + blk:5 + blk]), r=[g5r], w=[g1r, ssr])
            rms_rstd(ss[:T, 4:8], rs[:T, 4:8], 512, T, ssr, rsr)
            V(lambda e: e.tensor_tensor(ysb[:T, 0:2048].rearrange("p (g d) -> p g d", d=512),
                                        ysb[:T, 0:2048].rearrange("p (g d) -> p g d", d=512),
                                        rs[:T, 4:8].unsqueeze(2).to_broadcast([T, 4, 512]), ALU.mult),
              r=[g5r, rsr], w=[g5r])
            yT = g2[:, 0:1024].bitcast(BF16).rearrange("p (c t) -> p c t", t=128)
            transp(ysb, g5r, T, 16, lambda c: pc(l, 48 + c), yT, g2r)
            out_branch(0, lambda c: yT[:, c, 0:T], 16, 128, wb0 + BID["ossm"], g2r, 512)

            qsb = g0
            for blk in range(4):
                pt, ptr = PS[2 + blk % 2]
                projb(lambda c: HTb[:, c, :T], 16, 128, wb0 + BID["in"] + INB["q"] + blk, 512, T,
                      pt[:T, :], ptr, HTr)
                A(lambda e: e.copy(qsb[:T, blk * 512:(blk + 1) * 512], pt[:T, :]), r=[ptr], w=[g0r])
            pt, ptr = PS[2]
            projb(lambda c: HTb[:, c, :T], 16, 128, wb0 + BID["in"] + INB["kv"], 512, T, pt[:T, :], ptr, HTr)
            kv = g1
            A(lambda e: e.copy(kv[:T, 0:512], pt[:T, :]), r=[ptr], w=[g1r])
            sq = g2
            V(lambda e: e.tensor_tensor(sq[:T, 0:2048], qsb[:T, 0:2048], qsb[:T, 0:2048], ALU.mult), r=[g0r], w=[g2r])
            V(lambda e: e.tensor_reduce(ss[:T, 8:40], sq[:T, 0:2048].rearrange("p (h d) -> p h d", d=64), AX.X, ALU.add),
              r=[g2r], w=[ssr])
            rms_rstd(ss[:T, 8:40], rs[:T, 8:40], 64, T, ssr, rsr)
            V(lambda e: e.tensor_tensor(qsb[:T, 0:2048].rearrange("p (h d) -> p h d", d=64),
                                        qsb[:T, 0:2048].rearrange("p (h d) -> p h d", d=64),
                                        rs[:T, 8:40].unsqueeze(2).to_broadcast([T, 32, 64]), ALU.mult),
              r=[g0r, rsr], w=[g0r])
            V(lambda e: e.tensor_tensor(sq[:T, 0:256], kv[:T, 0:256], kv[:T, 0:256], ALU.mult), r=[g1r], w=[g2r])
            V(lambda e: e.tensor_reduce(ss[:T, 0:4], sq[:T, 0:256].rearrange("p (h d) -> p h d", d=64), AX.X, ALU.add),
              r=[g2r], w=[ssr])
            rms_rstd(ss[:T, 0:4], rs[:T, 0:4], 64, T, ssr, rsr)
            ktok = kv[:T, 512:768]
            V(lambda e: e.tensor_tensor(ktok.rearrange("p (h d) -> p h d", d=64),
                                        kv[:T, 0:256].rearrange("p (h d) -> p h d", d=64),
                                        rs[:T, 0:4].unsqueeze(2).to_broadcast([T, 4, 64]), ALU.mult), r=[g1r, rsr], w=[g1r])
            V(lambda e: e.tensor_tensor(ktok, ktok, pr(l, 128, 256)[:T, :], ALU.mult), r=[g1r, prowr], w=[g1r])
            transp(kv, g1r, T, 4, None, KTC, KTCr, width=64, src_off=512)
            if win_out is not None:
                win_out(kv, g1r)
            esink, esinkr = SM["esink"]
            A(lambda e: e.activation(esink[:, :], pr(l, 96, 32), AF.Exp), r=[prowr], w=[esinkr])
            nT = 8 * T
            for gq in range(4):
                qT = g2[:64, 0:1024].rearrange("p (h t) -> p h t", t=128)
                transp(qsb, g0r, T, 8, lambda c: pc(l, 188)[:64, :], qT, g2r, width=64, src_off=gq * 512)
                Em = g3
                kb.dma("sp", Em[:, 0:2048], emask_d[gq], writes=[g3r])
                Em4 = g3[:, 0:2048].rearrange("p (b h q) -> p b h q", b=2, h=8)
                PT = g4[:, 0:2048].rearrange("p (b h q) -> p b h q", b=2, h=8)
                blocks = ([0] if has_prev else []) + [1]
                for bi, kbk in enumerate(blocks):
                    nk = 128 if kbk == 0 else T
                    for hb in range(2):
                        pt, ptr = PS[2 + hb]
                        for hq in range(4):
                            hh = hb * 4 + hq
                            if kbk == 0:
                                P(lambda e: e.matmul(pt[:nk, hq * 128: hq * 128 + T], KTPl[:64, gq, 0:nk], qT[:64, hh, 0:T],
                                                     start=True, stop=True), r=[KTPlr, g2r], w=[ptr])
                            else:
                                P(lambda e: e.matmul(pt[:nk, hq * 128: hq * 128 + T], KTC[:64, gq, 0:nk], qT[:64, hh, 0:T],
                                                     start=True, stop=True), r=[KTCr, g2r], w=[ptr])
                        pv = pt[:nk, :].rearrange("p (h q) -> p h q", q=128)[:, :, 0:T]
                        A(lambda e: e.activation(PT[:nk, kbk, hb * 4:(hb + 1) * 4, 0:T], pv, AF.Exp, scale=0.125),
                          r=[ptr], w=[g4r])
                        V(lambda e: e.tensor_tensor(PT[:nk, kbk, hb * 4:(hb + 1) * 4, 0:T],
                                                    PT[:nk, kbk, hb * 4:(hb + 1) * 4, 0:T],
                                                    Em4[:nk, kbk, hb * 4:(hb + 1) * 4, 0:T], ALU.mult),
                          r=[g4r, g3r], w=[g4r])
                for hb in range(2):
                    po, por = PS[4 + hb]
                    pd, pdr = PS[6 + hb]
                    for hq in range(4):
                        hh = hb * 4 + hq
                        for bi, kbk in enumerate(blocks):
                            nk = 128 if kbk == 0 else T
                            if kbk == 0:
                                vsrc, vr = VPl[:nk, gq * 64:(gq + 1) * 64], VPlr
                            else:
                                vsrc, vr = kv[:nk, 256 + gq * 64: 256 + (gq + 1) * 64], g1r
                            P(lambda e: e.matmul(po[:64, hq * 128: hq * 128 + T], vsrc, PT[:nk, kbk, hh, 0:T],
                                                 start=(bi == 0), stop=(bi == len(blocks) - 1)), r=[vr, g4r], w=[por])
                            P(lambda e: e.matmul(pd[:64, hq * 128: hq * 128 + T], ones[:nk, 0:64], PT[:nk, kbk, hh, 0:T],
                                                 start=(bi == 0), stop=(bi == len(blocks) - 1)), r=[onesr, g4r], w=[pdr])
                    oT = g5[:64, 0:2048].bitcast(BF16).rearrange("p (h t) -> p h t", t=128)
                    oTr = g5r
                    hbase = gq * 8 + hb * 4
                    dn = g1[:64, 1024:1536].rearrange("p (h t) -> p h t", t=128)
                    V(lambda e: e.tensor_tensor(dn[:, :, 0:T], pd[:64, :].rearrange("p (h q) -> p h q", q=128)[:, :, 0:T],
                                                esink[:64, gq * 8 + hb * 4: gq * 8 + hb * 4 + 4].unsqueeze(2).to_broadcast([64, 4, T]),
                                                ALU.add), r=[pdr, esinkr], w=[g1r])
                    V(lambda e: e.reciprocal(dn[:, :, 0:T], dn[:, :, 0:T]), r=[g1r], w=[g1r])
                    V(lambda e: e.tensor_tensor(oT[:, hbase:hbase + 4, 0:T],
                                                po[:64, :].rearrange("p (h q) -> p h q", q=128)[:, :, 0:T],
                                                dn[:, :, 0:T], ALU.mult), r=[por, g1r], w=[oTr])
            if T == 128:
                V(lambda e: e.tensor_copy(KTPl[:, :, :], KTC[:, :, :]), r=[KTCr], w=[KTPlr])
                V(lambda e: e.tensor_copy(VPl[:, :], kv[:, 256:512]), r=[g1r], w=[VPlr])
            win_src = (kv, g1r)

            oTa = g5[:64, 0:2048].bitcast(BF16).rearrange("p (h t) -> p h t", t=128)
            for blk in range(4):
                pb, pbr = PS[6 + blk % 2]
                for sub in range(2):
                    projb(lambda c: oTa[:, c, 0:T], 32, 64, wb0 + BID["oswa"] + blk * 2 + sub, 256, T,
                          pb[:T, sub * 256:(sub + 1) * 256], pbr, g5r)
                gate_merge(1, blk, pb[:T, :], pbr, False)

            qm = g0
            for blk in range(4):
                pt, ptr = PS[2 + blk % 2]
                projb(lambda c: HTb[:, c, :T], 16, 128, wb0 + BID["in"] + INB["qm"] + blk, 512, T,
                      pt[:T, :], ptr, HTr)
                V(lambda e: e.memset(ss[:T, blk:blk + 1], 0.0), w=[ssr])
                A(lambda e: e.activation(qm[:T, blk * 512:(blk + 1) * 512], pt[:T, :], AF.Square,
                                         accum_out=ss[:T, blk:blk + 1]), r=[ptr], w=[g0r, ssr])
                rms_rstd(ss[:T, blk:blk + 1], rs[:T, blk:blk + 1], 512, T, ssr, rsr)
                A(lambda e: e.activation(qm[:T, blk * 512:(blk + 1) * 512], pt[:T, :], AF.Copy, scale=rs[:T, blk:blk + 1]),
                  r=[ptr, rsr], w=[g0r])
            qmT = g1[:, 0:2048].rearrange("p (c t) -> p c t", t=128)
            transp(qm, g0r, T, 16, lambda c: pc(l, 64 + c % 4), qmT, g1r)
            omT = g5[:, 0:1024].bitcast(BF16).rearrange("p (c t) -> p c t", t=128)
            mx, mxr = SM["mx"]
            smm, smr = SM["sm"]
            for hm in range(4):
                KTh = g2[:, 0:1024].rearrange("p (c m) -> p c m", m=256)
                Vh = g3[:, 0:1024].rearrange("p (b d) -> p b d", d=512)
                kb.dma("sp", KTh, memK_ap[:, hm * 4:(hm + 1) * 4, :], reads=[memK_r], writes=[g2r])
                kb.dma("sp", Vh, memV_ap[:, hm * 512:(hm + 1) * 512].rearrange("(b p) d -> p b d", p=128),
                       reads=[memV_r], writes=[g3r])
                pt, ptr = PS[2]
                for c in range(4):
                    P(lambda e: e.matmul(pt[:T, 0:256], qmT[:, hm * 4 + c, 0:T], KTh[:, c, :], start=(c == 0), stop=(c == 3)),
                      r=[g1r, g2r], w=[ptr])
                V(lambda e: e.tensor_reduce(mx[:T, 0:1], pt[:T, 0:256], AX.X, ALU.max), r=[ptr], w=[mxr])
                V(lambda e: e.tensor_scalar(mx[:T, 0:1], mx[:T, 0:1], -(512 ** -0.5), None, ALU.mult), r=[mxr], w=[mxr])
                Pm = g4[:, 0:256]
                V(lambda e: e.memset(smm[:T, 0:1], 0.0), w=[smr])
                A(lambda e: e.activation(Pm[:T, :], pt[:T, 0:256], AF.Exp, bias=mx[:T, 0:1], scale=512 ** -0.5,
                                         accum_out=smm[:T, 0:1]), r=[ptr, mxr], w=[g4r, smr])
                V(lambda e: e.reciprocal(smm[:T, 0:1], smm[:T, 0:1]), r=[smr], w=[smr])
                V(lambda e: e.tensor_scalar(Pm[:T, :], Pm[:T, :], smm[:T, 0:1], None, ALU.mult), r=[g4r, smr], w=[g4r])
                PmT = g4[:, 512:768].rearrange("p (c t) -> p c t", t=128)
                transp(g4, g4r, T, 2, None, PmT, g4r)
                pt2, pt2r = PS[3]
                for dc in range(4):
                    for mc in range(2):
                        P(lambda e: e.matmul(pt2[:, dc * 128: dc * 128 + T], Vh[:, mc, dc * 128:(dc + 1) * 128],
                                             PmT[:, mc, 0:T], start=(mc == 0), stop=(mc == 1)), r=[g3r, g4r], w=[pt2r])
                A(lambda e: e.copy(omT[:, hm * 4:(hm + 1) * 4, 0:T],
                                   pt2[:, :].rearrange("p (c t) -> p c t", t=128)[:, :, 0:T]), r=[pt2r], w=[g5r])
            out_branch(2, lambda c: omT[:, c, 0:T], 16, 128, wb0 + BID["omem"], g5r, 512)

            mT = g2[:, 0:1024].bitcast(BF16).rearrange("p (c t) -> p c t", t=128)
            transp(MG, MGr, T, 16, None, mT, g2r)
            for blk in range(4):
                pt, ptr = PS[2 + blk % 2]
                projb(lambda c: mT[:, c, 0:T], 16, 128, wb0 + BID["out"] + blk, 512, T, pt[:T, :], ptr, g2r)
                V(lambda e: e.tensor_tensor(X[:T, blk * 512:(blk + 1) * 512], X[:T, blk * 512:(blk + 1) * 512], pt[:T, :],
                                            ALU.add), r=[ptr, Xr], w=[Xr])
            if do_peer and run_peer:
                peer(T, l, X, Xr)
            return win_src

        def transp_fm(src3, srcr, T, c0, nch, dst3, dstr):
            for c in range(nch):
                b = tctr[0] % 2
                tctr[0] += 1
                pt, ptr = PS[b]
                P(lambda e: e.transpose(pt[:T, 0:128], src3[:, c0 + c, 0:T], ident[:, :]), r=[srcr, identr], w=[ptr])
                if c % 2 == 0:
                    V(lambda e: e.tensor_copy(dst3[:T, c, :], pt[:T, 0:128]), r=[ptr], w=[dstr])
                else:
                    A(lambda e: e.copy(dst3[:T, c, :], pt[:T, 0:128]), r=[ptr], w=[dstr])

        def peer(T, l, Xt, Xtr):
            ss, ssr = SM["ss"]
            rs, rsr = SM["rs"]
            g0, g0r = Gb[0]
            g1, g1r = Gb[1]
            g2, g2r = Gb[2]
            g3, g3r = Gb[3]
            g4, g4r = Gb[4]
            g5, g5r = Gb[5]
            for i_, nm_ in enumerate(["k1", "k2", "posf", "idxf", "gates", "acol", "wcol", "sc16"]):
                SM[nm_] = (g0[:, 2048 + i_ * 128: 2048 + (i_ + 1) * 128], g0r)
            norm_T(Xt, Xtr, T, lambda c: pc(l, 16 + c), HT, HTr)
            h2b = g0[:, 0:1024].bitcast(BF16)
            kb.dma("sp", g5[:, 0:D], gffn_bc[l], writes=[g5r])
            V(lambda e: e.tensor_tensor(h2b[:T, :], g5[:T, 0:D], XN[:T, :], ALU.mult), r=[g5r, XNr], w=[g0r])
            for blk in range(4):
                pt, ptr = PS[2 + blk % 2]
                proj(lambda c: HT[:, c, :T], 16, 128, w_peer_q[l][:, blk * 512:(blk + 1) * 512], 512, T, pt[:T, :], ptr, HTr)
                A(lambda e: e.copy(g1[:T, blk * 512:(blk + 1) * 512], pt[:T, :]), r=[ptr], w=[g1r])
            qT = g2[:, 0:2048].rearrange("p (c t) -> p c t", t=128)
            transp(g1, g1r, T, 16, None, qT, g2r)
            skT = g3[:, 0:2048].rearrange("p (c n) -> p c n", n=128)
            kb.dma("sp", skT, skT_d[l], writes=[g3r])
            sc = g4[:, 0:2048].rearrange("p (c n) -> p c n", n=128)
            for q4 in range(4):
                pt, ptr = PS[4 + q4]
                for j in range(4):
                    hc = q4 * 4 + j
                    P(lambda e: e.matmul(pt[:T, j * 128:(j + 1) * 128], qT[:, hc, 0:T], skT[:, hc, :], start=True, stop=True),
                      r=[g2r, g3r], w=[ptr])
                A(lambda e: e.copy(g4[:T, q4 * 512:(q4 + 1) * 512], pt[:T, :]), r=[ptr], w=[g4r])
            sc2 = g5[:, 0:2048].rearrange("p (c n) -> p c n", n=128)
            vtop, vtopr = SM["vtop"]
            itopf, itopfr = SM["itopf"]
            for hc in range(16):
                V(lambda e: e.max(vtop[:T, hc * 16: hc * 16 + 8], sc[:T, hc, :]), r=[g4r], w=[vtopr])
                V(lambda e: e.max_index(itop_u[:T, hc * 16: hc * 16 + 8], vtop[:T, hc * 16: hc * 16 + 8], sc[:T, hc, :]),
                  r=[g4r, vtopr], w=[itop_ur])
                V(lambda e: e.match_replace(sc2[:T, hc, :], vtop[:T, hc * 16: hc * 16 + 8], sc[:T, hc, :], -1e30),
                  r=[g4r, vtopr], w=[g5r])
                V(lambda e: e.max(vtop[:T, hc * 16 + 8: hc * 16 + 16], sc2[:T, hc, :]), r=[g5r], w=[vtopr])
                V(lambda e: e.max_index(itop_u[:T, hc * 16 + 8: hc * 16 + 16], vtop[:T, hc * 16 + 8: hc * 16 + 16],
                                        sc2[:T, hc, :]), r=[g5r, vtopr], w=[itop_ur])
            V(lambda e: e.tensor_copy(itopf[:T, :], itop_u[:T, :]), r=[itop_ur], w=[itopfr])
            cand = g1[:, 0:2048].rearrange("p (h a b) -> p h a b", h=8, a=16)
            cand2 = g3[:, 0:2048].rearrange("p (h n) -> p h n", n=256)
            v4 = vtop[:T, :].rearrange("p (h c k) -> p h c k", h=8, c=2)
            V(lambda e: e.tensor_tensor(cand[:T], v4[:, :, 0, :].unsqueeze(3).to_broadcast([T, 8, 16, 16]),
                                        v4[:, :, 1, :].unsqueeze(2).to_broadcast([T, 8, 16, 16]), ALU.add),
              r=[vtopr], w=[g1r])
            candf = g1[:, 0:2048].rearrange("p (h n) -> p h n", n=256)
            sc16, sc16r = SM["sc16"]
            for h in range(8):
                V(lambda e: e.max(sc16[:T, h * 16: h * 16 + 8], candf[:T, h, :]), r=[g1r], w=[sc16r])
                V(lambda e: e.max_index(pos_u[:T, h * 16: h * 16 + 8], sc16[:T, h * 16: h * 16 + 8], candf[:T, h, :]),
                  r=[g1r, sc16r], w=[pos_ur])
                V(lambda e: e.match_replace(cand2[:T, h, :], sc16[:T, h * 16: h * 16 + 8], candf[:T, h, :], -1e30),
                  r=[g1r, sc16r], w=[g3r])
                V(lambda e: e.max(sc16[:T, h * 16 + 8: h * 16 + 16], cand2[:T, h, :]), r=[g3r], w=[sc16r])
                V(lambda e: e.max_index(pos_u[:T, h * 16 + 8: h * 16 + 16], sc16[:T, h * 16 + 8: h * 16 + 16],
                                        cand2[:T, h, :]), r=[g3r, sc16r], w=[pos_ur])
            k1, k1r = SM["k1"]
            k2, k2r = SM["k2"]
            V(lambda e: e.tensor_single_scalar(idx_i[:T, :], pos_u[:T, :].bitcast(I32), 4, ALU.logical_shift_right), r=[pos_ur], w=[idx_ir])
            V(lambda e: e.tensor_copy(k1[:T, :], idx_i[:T, :]), r=[idx_ir], w=[k1r])
            V(lambda e: e.tensor_single_scalar(idx_i[:T, :], pos_u[:T, :].bitcast(I32), 15, ALU.bitwise_and), r=[pos_ur], w=[idx_ir])
            V(lambda e: e.tensor_copy(k2[:T, :], idx_i[:T, :]), r=[idx_ir], w=[k2r])
            oh = g4[:, 0:2048].rearrange("p (h k j) -> p h k j", h=8, k=16)
            i4 = itopf[:T, :].rearrange("p (h c k) -> p h c k", h=8, c=2)
            idxf, idxfr = SM["idxf"]
            posf, posfr = SM["posf"]
            for (kk, kkr, ci, dst) in ((k1, k1r, 0, idxf), (k2, k2r, 1, posf)):
                V(lambda e: e.tensor_tensor(oh[:T], kk[:T, :].rearrange("p (h k) -> p h k", h=8).unsqueeze(3).to_broadcast([T, 8, 16, 16]),
                                            iot16[:T, :].unsqueeze(1).unsqueeze(1).to_broadcast([T, 8, 16, 16]), ALU.is_equal),
                  r=[kkr, iot16r], w=[g4r])
                V(lambda e: e.tensor_tensor(oh[:T], oh[:T], i4[:, :, ci, :].unsqueeze(2).to_broadcast([T, 8, 16, 16]), ALU.mult),
                  r=[g4r, itopfr], w=[g4r])
                V(lambda e: e.tensor_reduce(dst[:T, :], g4[:T, 0:2048].rearrange("p (a j) -> p a j", j=16), AX.X, ALU.add),
                  r=[g4r], w=[idxfr if ci == 0 else posfr])
            V(lambda e: e.scalar_tensor_tensor(idxf[:T, :], idxf[:T, :], 128.0, posf[:T, :], ALU.mult, ALU.add),
              r=[idxfr, posfr], w=[idxfr])
            if l > 0:
                V(lambda e: e.tensor_scalar(idxf[:T, :], idxf[:T, :], float(l * 16384), None, ALU.add), r=[idxfr], w=[idxfr])
            V(lambda e: e.tensor_copy(idx_i[:T, :], idxf[:T, :]), r=[idxfr], w=[idx_ir])
            gates, gatesr = SM["gates"]
            t8, t8r = SM["t8"]
            s3 = sc16[:T, :].rearrange("p (h k) -> p h k", k=16)
            g3v = gates[:T, :].rearrange("p (h k) -> p h k", k=16)
            V(lambda e: e.tensor_tensor(g3v, s3, s3[:, :, 0:1].to_broadcast([T, 8, 16]), ALU.subtract), r=[sc16r], w=[gatesr])
            A(lambda e: e.activation(gates[:T, :], gates[:T, :], AF.Exp), r=[gatesr], w=[gatesr])
            V(lambda e: e.tensor_reduce(t8[:T, :], g3v, AX.X, ALU.add), r=[gatesr], w=[t8r])
            V(lambda e: e.reciprocal(t8[:T, :], t8[:T, :]), r=[t8r], w=[t8r])
            V(lambda e: e.tensor_tensor(g3v, g3v, t8[:T, :].unsqueeze(2).to_broadcast([T, 8, 16]), ALU.mult),
              r=[gatesr, t8r], w=[gatesr])
            acol, acolr = SM["acol"]
            wcol, wcolr = SM["wcol"]
            V(lambda e: e.memset(acol[:T, :], 0.0), w=[acolr])
            acres = [Res("ac%d" % i) for i in range(128)]
            for r_ in acres:
                r_.w = acolr.w
            gbufs = [Gb[1], Gb[2], Gb[3], Gb[4]]
            for s in range(128):
                gb, gbr = gbufs[s % 4]
                gbv = gb[:, 0:1024].bitcast(BF16)
                kb.dma("pool", gbv[:T, :], usc, reads=[idx_ir], writes=[gbr],
                       indirect=bass.IndirectOffsetOnAxis(idx_i[:T, s:s + 1], 0))
                V(lambda e: e.scalar_tensor_tensor(gbv[:T, :], gbv[:T, :], 1.0, h2b[:T, :], ALU.mult, ALU.mult,
                                                   accum_out=acol[:T, s:s + 1]), r=[gbr, g0r], w=[gbr, acres[s]])
            A(lambda e: e.activation(wcol[:T, :], acol[:T, :], AF.Gelu), r=[acolr] + acres, w=[wcolr])
            V(lambda e: e.tensor_tensor(wcol[:T, :], wcol[:T, :], gates[:T, :], ALU.mult), r=[wcolr, gatesr], w=[wcolr])
            NDS = 8
            dres = [Res("diag%d" % i) for i in range(NDS)]
            for s in range(128):
                gb, gbr = gbufs[s % 4]
                gbv = gb[:, 0:1024].bitcast(BF16)
                kb.dma("pool", gbv[:T, :], vsc, reads=[idx_ir], writes=[gbr],
                       indirect=bass.IndirectOffsetOnAxis(idx_i[:T, s:s + 1], 0))
                if s % 2 == 1:
                    V(lambda e: e.scalar_tensor_tensor(Xt[:T, :], gbv[:T, :], wcol[:T, s:s + 1], Xt[:T, :], ALU.mult, ALU.add),
                      r=[gbr, wcolr, Xtr], w=[Xtr])
                    continue
                ds_ = (s // 2) % NDS
                dv = g5[:, 0:1024].bitcast(BF16)[:T, ds_ * 128: ds_ * 128 + T]
                wl = [dres[ds_]] + ([g5r] if (s < 2 * NDS or s >= 128 - 2 * NDS) else [])
                V(lambda e: e.tensor_scalar(dv, ident[:T, :T], wcol[:T, s:s + 1], None, ALU.mult),
                  r=[identr, wcolr], w=wl)
                for q in range(4):
                    pq, pqr = PS[4 + q]
                    rl = [dres[ds_], gbr] + ([g5r] if s >= 128 - 2 * NDS else [])
                    P(lambda e: e.matmul(pq[:T, :], dv, gbv[:T, q * 512:(q + 1) * 512], start=(s == 0), stop=(s == 126)),
                      r=rl, w=[pqr])
            for q in range(4):
                pq, pqr = PS[4 + q]
                V(lambda e: e.tensor_tensor(Xt[:T, q * 512:(q + 1) * 512], Xt[:T, q * 512:(q + 1) * 512], pq[:T, :], ALU.add),
                  r=[pqr, Xtr], w=[Xtr])

        def ssm_out(l, dst_ap, dst_r):
            Sl, Slr = S[l]
            g0, g0r = Gb[0]
            so = g0[:, 0:2048].rearrange("p (c n) -> p c n", n=128)
            transp(Sl, Slr, 128, 16, None, so, g0r)
            kb.dma("sp", dst_ap.rearrange("(c p) n -> p c n", p=128), so, reads=[g0r], writes=[dst_r])

        r_yp, r_ys = ores("yp"), ores("ys")
        r_pw, r_ps, r_pc = ores("pw"), ores("pssm"), ores("pconv")
        r_sw, r_ss, r_scv = ores("sw"), ores("sssm"), ores("sconv")

        precast_all()
        if NPT > 0:
            for l in range(n_layers):
                mem_precompute(l)
                V(lambda e: e.memset(S[l][0][:, :], 0.0), w=[S[l][1]])
                V(lambda e: e.memset(CT[l][0][:, :, :], 0.0), w=[CT[l][1]])
            for ti in range(NPT):
                kb.dma("sp", X[:, :], xp[ti * 128:(ti + 1) * 128, :], writes=[Xr])
                for l in range(n_layers):
                    last = ti == NPT - 1
                    def wo(kv, kvr, l=l):
                        kb.dma("sp", pwk[l], kv[:, 512:768], reads=[kvr], writes=[r_pw])
                        kb.dma("sp", pwv[l], kv[:, 256:512], reads=[kvr], writes=[r_pw])
                    tile_layer(128, l, ti > 0, pmkT[l], pmkT_r[l], pmv[l], pmv_r[l],
                               pconv[l] if last else None, r_pc, wo if last else None)
                    if last:
                        ssm_out(l, pssm[l], r_ps)
                kb.dma("sp", yp[ti * 128:(ti + 1) * 128, :], X[:, :], reads=[Xr], writes=[r_yp])
        cres = Res("cin")
        TS = 8 * NSQ
        XBt, XBr = None, None
        for l in range(n_layers if NSQ > 0 else 0):
            XBt, XBr = S[1] if l == 0 else S[0]
            for sq in range(NSQ):
                if l == 0:
                    kb.dma("sp", X[0:8, :], xsm[sq * 8:(sq + 1) * 8, :], writes=[Xr])
                else:
                    kb.dma("sp", X[0:8, :], XBt[sq * 8:(sq + 1) * 8, :], reads=[XBr], writes=[Xr])
                kb.dma("sp", S[l][0][:, :], sst[l, sq], writes=[S[l][1]])
                kb.dma("sp", CT[l][0][:, :, :], scv[l, sq], writes=[CT[l][1]])
                kb.dma("sp", KTP[l][0][:, :, :], cwkT[l, sq], writes=[KTP[l][1]])
                kb.dma("sp", VP[l][0][:, :], cwv[l, sq], writes=[VP[l][1]])

                def wo(kv, kvr, l=l, sq=sq):
                    kb.dma("sp", swk[l, sq, 0:120, :], cwk[l, sq, 8:128, :], writes=[r_sw])
                    kb.dma("sp", swv[l, sq, 0:120, :], cwv[l, sq, 8:128, :], writes=[r_sw])
                    kb.dma("sp", swk[l, sq, 120:128, :], kv[0:8, 512:768], reads=[kvr], writes=[r_sw])
                    kb.dma("sp", swv[l, sq, 120:128, :], kv[0:8, 256:512], reads=[kvr], writes=[r_sw])
                tile_layer(8, l, True, cmkT[l, sq], cres, cmv[l, sq], cres, sconv[l, sq], r_scv, wo, run_peer=False)
                kb.dma("sp", XBt[sq * 8:(sq + 1) * 8, :], X[0:8, :], reads=[Xr], writes=[XBr])
                ssm_out(l, sssm[l, sq], r_ss)
            if do_peer:
                peer(TS, l, XBt, XBr)
            if l == 0 and n_layers > 1:
                V(lambda e: e.tensor_copy(S[0][0][0:TS, :], S[1][0][0:TS, :]), r=[S[1][1]], w=[S[0][1]])
        if NSQ > 0:
            kb.dma("sp", ys[0:TS, :], XBt[0:TS, :], reads=[XBr], writes=[r_ys])
        kb.wait_all("sp", out_res)
        build.ninst = kb.ninst
    return nc


def _consts():
    ident = np.eye(128, dtype=np.float32)
    j = np.arange(128)
    triu = (j[:, None] <= j[None, :]).astype(np.float32)
    slopes = np.exp2(-8.0 * np.arange(1, 33, dtype=np.float32) / 32).astype(np.float32)
    k = np.arange(128)[:, None].astype(np.float32)
    q = np.arange(128)[None, :].astype(np.float32)
    em = np.zeros((4, 128, 2, 8, 128), np.float32)
    for h in range(32):
        d0 = q + 128 - k
        d1 = q - k
        em[h // 8, :, 0, h % 8, :] = np.where((d0 >= 0) & (d0 <= 128), np.exp(-slopes[h] * d0), 0.0)
        em[h // 8, :, 1, h % 8, :] = np.where((d1 >= 0) & (d1 <= 128), np.exp(-slopes[h] * d1), 0.0)
    return ident, triu, em.reshape(4, 128, 2048)


def _col(v, n):
    return np.ascontiguousarray(np.asarray(v, np.float32).reshape(n, 128).T)


def make_in_maps(inp, NPT, NSQ, cores):
    f = lambda a: np.ascontiguousarray(np.asarray(a, dtype=np.float32))
    ident, triu, em = _consts()
    pcols = np.zeros((128, 400), np.float32)
    prow = np.zeros((128, 768), np.float32)
    for l in range(2):
        o = l * 200
        pcols[:, o:o + 16] = _col(inp["g_mix"][l], 16)
        pcols[:, o + 16:o + 32] = _col(inp["g_ffn"][l], 16)
        pcols[:, o + 32:o + 48] = _col(inp["g_mem"][l], 16)
        pcols[:, o + 48:o + 64] = _col(inp["g_ssd_norm"][l], 16)
        pcols[:, o + 64:o + 68] = _col(inp["g_qm"][l], 4)
        for k in range(4):
            pcols[:, o + 68 + k * 24:o + 68 + (k + 1) * 24] = _col(inp["conv_w"][l][k], 24)
        pcols[:, o + 164:o + 188] = _col(inp["conv_b"][l], 24)
        pcols[:64, o + 188] = np.asarray(inp["g_q"][l], np.float32)
        r = l * 384
        prow[:, r:r + 32] = np.asarray(inp["dt_bias"][l], np.float32)[None, :]
        prow[:, r + 32:r + 64] = np.asarray(inp["a_log"][l], np.float32)[None, :]
        prow[:, r + 64:r + 96] = np.asarray(inp["d_skip"][l], np.float32)[None, :]
        prow[:, r + 96:r + 128] = np.asarray(inp["attn_sinks"][l], np.float32)[None, :]
        prow[:, r + 128:r + 384] = np.tile(np.asarray(inp["g_k"][l], np.float32), 4)[None, :]
    gkm_bc = f(np.broadcast_to(np.tile(np.asarray(inp["g_km"], np.float32), (1, 4))[:, None, :], (2, 128, D)))
    gffn_bc = f(np.broadcast_to(np.asarray(inp["g_ffn"], np.float32)[:, None, :], (2, 128, D)))
    skT = f(np.asarray(inp["peer_sub_keys"], np.float32).reshape(2, 16, 128, 128).transpose(0, 3, 1, 2))
    shared = dict(
        w_in=f(inp["w_in"]), w_mem_kv=f(inp["w_mem_kv"]), w_o_ssm=f(inp["w_o_ssm"]), w_o_swa=f(inp["w_o_swa"]),
        w_o_mem=f(inp["w_o_mem"]), w_out=f(inp["w_out"]), w_peer_q=f(inp["w_peer_q"]), peer_u=f(inp["peer_u"]).reshape(2 * 16384, D),
        peer_v=f(inp["peer_v"]).reshape(2 * 16384, D), skT=skT, pcols=pcols, prow=prow, gkm_bc=gkm_bc, gffn_bc=gffn_bc,
        bgate=f(np.asarray(inp["b_gate"], np.float32).reshape(2, 1, 6144)), ident=ident, triu=triu, emask=em)
    maps = []
    nq = max(NSQ, 1)
    for (pb, s0) in cores:
        m = dict(shared)
        m["xp"] = f(np.asarray(inp["x_prompt"])[pb, :max(NPT, 1) * 128])
        m["memp"] = f(np.asarray(inp["mem_prompt"])[pb])
        sl = slice(s0, s0 + nq)
        m["xsm"] = f(np.asarray(inp["x_sample"])[sl].reshape(nq * 8, D))
        ck = np.asarray(inp["cache_win_k"], np.float32)[:, sl]
        m["cwk"] = f(ck.reshape(2, nq, 128, 256))
        m["cwkT"] = f(ck.transpose(0, 1, 4, 3, 2))
        m["cwv"] = f(np.asarray(inp["cache_win_v"], np.float32)[:, sl].reshape(2, nq, 128, 256))
        ssm = np.asarray(inp["state_ssm"], np.float32)[:, sl]
        m["sst"] = f(ssm.reshape(2, nq, 2048, 128).transpose(0, 1, 3, 2))
        cv = np.asarray(inp["state_conv"], np.float32)[:, sl]
        m["scv"] = f(cv.reshape(2, nq, 3, 24, 128).transpose(0, 1, 4, 3, 2))
        mk = np.asarray(inp["cache_mem_k"], np.float32)[:, sl].reshape(2, nq, 256, 16, 128)
        m["cmkT"] = f(mk.transpose(0, 1, 4, 3, 2))
        m["cmv"] = f(np.asarray(inp["cache_mem_v"], np.float32)[:, sl].reshape(2, nq, 256, D))
        maps.append(m)
    return maps


_NC_CACHE = {}


def kernel(**inputs):
    NPT, NSQ = 16, 4
    key = (NPT, NSQ)
    if key not in _NC_CACHE:
        _NC_CACHE[key] = build(NPT, NSQ)
    nc = _NC_CACHE[key]
    cores = [(c % 4, 4 * c) for c in range(8)]
    maps = make_in_maps(inputs, NPT, NSQ, cores)
    res = run_bass_kernel_spmd(nc, maps, core_ids=list(range(8))).results
    y_prompt = np.stack([res[b]["yp"] for b in range(4)]).astype(np.float32)
    y_sample = np.concatenate([res[c]["ys"].reshape(4, 8, D) for c in range(8)]).astype(np.float32)

    def pstack(name, shape):
        return np.stack([np.stack([res[b][name][l].reshape(shape) for b in range(4)]) for l in range(2)]).astype(np.float32)

    def sstack(name, shape):
        return np.stack([np.concatenate([res[c][name][l].reshape((4,) + shape) for c in range(8)]) for l in range(2)]).astype(np.float32)

    return (y_prompt, y_sample,
            pstack("pwk", (128, 4, 64)), pstack("pwv", (128, 4, 64)),
            pstack("pssm", (32, 64, 128)), pstack("pconv", (3, 3072)),
            pstack("pmk", (256, 4, 512)), pstack("pmv", (256, 4, 512)),
            sstack("swk", (128, 4, 64)), sstack("swv", (128, 4, 64)),
            sstack("sssm", (32, 64, 128)), sstack("sconv", (3, 3072)))
```

```python
import numpy as np
from contextlib import ExitStack
import concourse.bass as bass
import concourse.mybir as mybir
from concourse.bass_utils import run_bass_kernel_spmd

F32 = mybir.dt.float32
BF16 = mybir.dt.bfloat16
I32 = mybir.dt.int32
U32 = mybir.dt.uint32
ALU = mybir.AluOpType
AF = mybir.ActivationFunctionType
AX = mybir.AxisListType

SELF_SYNC = True
SEM_EPOCH = 30000
D = 2048
IN_DIM = 15904
OFF_Z, OFF_XBC, OFF_DT, OFF_Q, OFF_K, OFF_V, OFF_QM, OFF_G = 0, 2048, 5120, 5152, 7200, 7456, 7712, 9760
EPS = 1e-6
GSZ = 3200


class Res:
    __slots__ = ("name", "w", "r")

    def __init__(self, name):
        self.name = name
        self.w = None
        self.r = {}


class KB:
    def __init__(self, nc, stack, n_dma_sems=20):
        self.nc = nc
        self.stack = stack
        self.engs = {"pe": nc.tensor, "dve": nc.vector, "act": nc.scalar,
                     "pool": nc.gpsimd, "sp": nc.sync}
        self.sem = {}
        self.semidx = {}
        self.cnt = {}
        for k in self.engs:
            self._new_sem(k)
        self.waited = {k: {} for k in self.engs}
        self.dma_sems = [stack.enter_context(nc.semaphore("dq%d" % i)) for i in range(n_dma_sems)]
        self.dma_n = 0
        self.dma_tgt = [0] * n_dma_sems
        self.ninst = 0

    def _new_sem(self, k):
        i = self.semidx.get(k, -1) + 1
        self.semidx[k] = i
        self.sem[(k, i)] = self.stack.enter_context(self.nc.semaphore("s_%s_%d" % (k, i)))
        self.cnt[k] = 0

    def _semof(self, src):
        if src[0] == "e":
            return self.sem[(src[1], src[2])]
        return self.dma_sems[src[1]]

    def _collect(self, reads, writes):
        deps = {}

        def add(s, c):
            if deps.get(s, 0) < c:
                deps[s] = c
        for r in reads:
            if r.w:
                add(*r.w)
        for w in writes:
            if w.w:
                add(*w.w)
            for s, c in w.r.items():
                add(s, c)
        return deps

    def _emit_waits(self, eng, deps):
        e = self.engs[eng]
        for s, c in deps.items():
            if s[0] == "e" and s[1] == eng:
                if eng in ("pe", "sp") or not SELF_SYNC:
                    continue
            if self.waited[eng].get(s, 0) >= c:
                continue
            e.wait_ge(self._semof(s), c)
            self.waited[eng][s] = c

    def op(self, eng, emit, reads=(), writes=()):
        deps = self._collect(reads, writes)
        self._emit_waits(eng, deps)
        ins = emit(self.engs[eng])
        if self.cnt[eng] >= SEM_EPOCH:
            self._new_sem(eng)
        self.cnt[eng] += 1
        key = ("e", eng, self.semidx[eng])
        ins.then_inc(self.sem[(eng, self.semidx[eng])], 1)
        c = self.cnt[eng]
        for r in reads:
            r.r[key] = c
        for w in writes:
            w.w = (key, c)
            w.r = {}
        self.ninst += 1
        return ins

    def dma(self, q, out, in_, reads=(), writes=(), indirect=None, **kw):
        deps = self._collect(reads, writes)
        slot = self.dma_n % len(self.dma_sems)
        self.dma_n += 1
        if self.dma_tgt[slot] > 0:
            deps[("d", slot)] = max(deps.get(("d", slot), 0), self.dma_tgt[slot])
        self._emit_waits(q, deps)
        e = self.engs[q]
        if indirect is not None:
            ins = e.indirect_dma_start(out, None, in_, indirect, **kw)
        else:
            ins = e.dma_start(out, in_, **kw)
        self.dma_tgt[slot] += 16
        ins.then_inc(self.dma_sems[slot], 16)
        key = ("d", slot)
        c = self.dma_tgt[slot]
        for r in reads:
            r.r[key] = c
        for w in writes:
            w.w = (key, c)
            w.r = {}
        self.ninst += 1
        return ins

    def dma_barrier(self, eng):
        e = self.engs[eng]
        for i, t in enumerate(self.dma_tgt):
            if t > 0 and self.waited[eng].get(("d", i), 0) < t:
                e.wait_ge(self.dma_sems[i], t)
                self.waited[eng][("d", i)] = t

    def wait_all(self, eng, ress):
        deps = {}
        for r in ress:
            if r.w and deps.get(r.w[0], 0) < r.w[1]:
                deps[r.w[0]] = r.w[1]
            for s, c in r.r.items():
                if deps.get(s, 0) < c:
                    deps[s] = c
        e = self.engs[eng]
        for s, c in deps.items():
            e.wait_ge(self._semof(s), c)


def build(NPT, NSQ, n_layers=2, do_peer=True):
    nc = bass.Bass("TRN2", target_bir_lowering=False)

    def din(name, shape, dt=F32):
        return nc.dram_tensor(name, list(shape), dt, kind="ExternalInput").ap()

    def dout(name, shape):
        return nc.dram_tensor(name, list(shape), F32, kind="ExternalOutput").ap()

    xp = din("xp", [max(NPT, 1) * 128, D])
    xsm = din("xsm", [max(NSQ, 1) * 8, D])
    cwk = din("cwk", [2, max(NSQ, 1), 128, 256])
    cwv = din("cwv", [2, max(NSQ, 1), 128, 256])
    cwkT = din("cwkT", [2, max(NSQ, 1), 64, 4, 128])
    sst = din("sst", [2, max(NSQ, 1), 128, 2048])
    scv = din("scv", [2, max(NSQ, 1), 128, 24, 3])
    cmkT = din("cmkT", [2, max(NSQ, 1), 128, 16, 256])
    cmv = din("cmv", [2, max(NSQ, 1), 256, 2048])
    memp = din("memp", [256, D])
    w_in = din("w_in", [2, D, IN_DIM])
    w_mem_kv = din("w_mem_kv", [2, D, 4096])
    w_o_ssm = din("w_o_ssm", [2, D, D])
    w_o_swa = din("w_o_swa", [2, D, D])
    w_o_mem = din("w_o_mem", [2, D, D])
    w_out = din("w_out", [2, D, D])
    w_peer_q = din("w_peer_q", [2, D, D])
    peer_u = din("peer_u", [2 * 16384, D])
    peer_v = din("peer_v", [2 * 16384, D])
    skT_d = din("skT", [2, 128, 16, 128])
    NPC = 2 * 200
    pcols_d = din("pcols", [128, NPC])
    NPR = 2 * 384
    prow_d = din("prow", [128, NPR])
    gkm_bc = din("gkm_bc", [2, 128, D])
    gffn_bc = din("gffn_bc", [2, 128, D])
    bgate = din("bgate", [2, 1, 6144])
    ident_d = din("ident", [128, 128])
    triu_d = din("triu", [128, 128])
    emask_d = din("emask", [4, 128, 2048])

    yp = dout("yp", [max(NPT, 1) * 128, D])
    ys = dout("ys", [max(NSQ, 1) * 8, D])
    pwk = dout("pwk", [2, 128, 256])
    pwv = dout("pwv", [2, 128, 256])
    pssm = dout("pssm", [2, 2048, 128])
    pconv = dout("pconv", [2, 3, 3072])
    pmk = dout("pmk", [2, 256, D])
    pmv = dout("pmv", [2, 256, D])
    swk = dout("swk", [2, max(NSQ, 1), 128, 256])
    swv = dout("swv", [2, max(NSQ, 1), 128, 256])
    sssm = dout("sssm", [2, max(NSQ, 1), 2048, 128])
    sconv = dout("sconv", [2, max(NSQ, 1), 3, 3072])
    pmkT = nc.dram_tensor("pmkT", [2, 128, 16, 256], F32, kind="Internal").ap()
    NBLK = 52
    wsc = nc.dram_tensor("wsc", [2 * NBLK, 128, 8192], BF16, kind="Internal").ap()
    usc = nc.dram_tensor("usc", [2 * 16384, D], BF16, kind="Internal").ap()
    vsc = nc.dram_tensor("vsc", [2 * 16384, D], BF16, kind="Internal").ap()
    BID = {"in": 0, "dt": 31, "ossm": 32, "oswa": 36, "omem": 44, "out": 48}

    out_res = []

    def ores(name):
        r = Res(name)
        out_res.append(r)
        return r

    with ExitStack() as st:
        kb = KB(nc, st)

        def sb(name, shape, dt=F32):
            t = st.enter_context(nc.sbuf_tensor(name, list(shape), dt))
            return t, Res(name)

        def V(fn, r=(), w=()):
            return kb.op("dve", fn, reads=r, writes=w)

        def A(fn, r=(), w=()):
            return kb.op("act", fn, reads=r, writes=w)

        def P(fn, r=(), w=()):
            return kb.op("pe", fn, reads=r, writes=w)

        def G(fn, r=(), w=()):
            return kb.op("pool", fn, reads=r, writes=w)

        X, Xr = sb("X", [128, D])
        XN, XNr = sb("XN", [128, D])
        HT, HTr = sb("HT", [128, 16, 128])
        HTb = HT[:, 0:8, :].rearrange("p c t -> p (c t)").bitcast(BF16).rearrange("p (c t) -> p c t", t=128)
        MG, MGr = sb("MG", [128, D])
        WB = [sb("WB%d" % i, [128, 4096]) for i in range(4)]
        Gb = [sb("G%d" % i, [128, GSZ if i == 1 else 3072]) for i in range(6)]
        S = [sb("S%d" % l, [128, D]) for l in range(2)]
        CT = [sb("CT%d" % l, [128, 24, 3]) for l in range(2)]
        KTP = [sb("KTP%d" % l, [64, 4, 128]) for l in range(2)]
        VP = [sb("VP%d" % l, [128, 256]) for l in range(2)]
        KTC, KTCr = sb("KTC", [64, 4, 128])
        ident, identr = sb("ident_s", [128, 128])
        triu, triur = sb("triu_s", [128, 128])
        ones, onesr = sb("ones_s", [128, 128])
        pcols, pcolsr = sb("pcols_s", [128, NPC])
        prow, prowr = sb("prow_s", [128, NPR])
        epst, epstr = sb("epst", [128, 1])
        onec, onecr = sb("onec", [128, 1])
        iot16, iot16r = sb("iot16", [128, 16])
        SM = {}
        for nm, w_ in [("ss", 40), ("rs", 40), ("dt", 32), ("dA", 32), ("cum", 32), ("ncum", 32), ("ecum", 32),
                       ("cl", 32), ("wend", 32), ("A", 32), ("esink", 32), ("bg", 512), ("mx", 8), ("sm", 8),
                       ("vtop", 256), ("itopf", 256), ("t8", 8)]:
            SM[nm] = sb("sm_" + nm, [128, w_])
        itop_u, itop_ur = sb("itop_u", [128, 256], U32)
        pos_u, pos_ur = sb("pos_u", [128, 128], U32)
        idx_i, idx_ir = sb("idx_i", [128, 128], I32)
        PS = [(st.enter_context(nc.psum_tensor("ps%d" % i, [128, 512], F32)), Res("ps%d" % i)) for i in range(8)]

        pmv_r = [Res("pmv%d" % l) for l in range(2)]
        pmkT_r = [Res("pmkT%d" % l) for l in range(2)]
        ores_pmk = ores("pmk")

        kb.dma("sp", ident[:], ident_d, writes=[identr])
        kb.dma("sp", triu[:], triu_d, writes=[triur])
        kb.dma("sp", pcols[:], pcols_d, writes=[pcolsr])
        kb.dma("sp", prow[:], prow_d, writes=[prowr])
        G(lambda e: e.memset(ones[:], 1.0), w=[onesr])
        G(lambda e: e.memset(epst[:], EPS), w=[epstr])
        G(lambda e: e.memset(onec[:], 1.0), w=[onecr])
        G(lambda e: e.iota(iot16[:], [[1, 16]], base=0, channel_multiplier=0, allow_small_or_imprecise_dtypes=True),
          w=[iot16r])

        def pc(l, off, n=1):
            return pcols[:, l * 200 + off: l * 200 + off + n]

        def pr(l, off, n):
            return prow[:, l * 384 + off: l * 384 + off + n]

        wctr = [0]

        def proj(lhs_fn, nk, kp, wsrc, ncols, T, ps_ap, ps_res, lhs_res, extra=None):
            for c0 in range(0, ncols, 256):
                n_ = min(256, ncols - c0)
                i = wctr[0] % 4
                wctr[0] += 1
                wt, wr = WB[i]
                wv = wt[0:kp, 0:nk * n_].rearrange("p (c n) -> p c n", n=n_)
                kb.dma("sp", wv, wsrc[:, c0:c0 + n_].rearrange("(c p) n -> p c n", p=kp), writes=[wr])
                for c in range(nk):
                    P(lambda e: e.matmul(ps_ap[:, c0:c0 + n_], lhs_fn(c), wv[:, c, :], start=(c == 0), stop=(c == nk - 1)),
                      r=[lhs_res, wr], w=[ps_res])

        def projb(lhs_fn, nk, kp, bid, ncols, T, ps_ap, ps_res, lhs_res, extra=None):
            i = wctr[0] % 4
            wctr[0] += 1
            wt, wr = WB[i]
            wv = wt[0:kp, :].bitcast(BF16)[:, 0:nk * ncols].rearrange("p (c n) -> p c n", n=ncols)
            kb.dma("sp", wt[0:kp, :].bitcast(BF16)[:, 0:nk * ncols], wsc[bid][0:kp, 0:nk * ncols], writes=[wr])
            for c in range(nk):
                last = (c == nk - 1) and extra is None
                P(lambda e: e.matmul(ps_ap, lhs_fn(c), wv[:, c, :], start=(c == 0), stop=last),
                  r=[lhs_res, wr], w=[ps_res])
            if extra is not None:
                P(lambda e: e.matmul(ps_ap, extra[0], extra[1], start=False, stop=True),
                  r=extra[2], w=[ps_res])

        def precast_all():
            jobs = []
            for l in range(n_layers):
                base = l * NBLK

                def std(src, c0, bid):
                    us = []
                    for hf in range(2):
                        v = src[hf * 1024:(hf + 1) * 1024, c0:c0 + 512].rearrange("(c p) n -> p c n", p=128)
                        us.append((v, 8, 512, hf * 4096))
                    jobs.append((128, us, ("w", bid, 8192)))
                cols = [OFF_Z + i * 512 for i in range(4)] + [OFF_XBC + i * 512 for i in range(6)] + \
                       [OFF_Q + i * 512 for i in range(4)] + [OFF_K] + [OFF_QM + i * 512 for i in range(4)] + \
                       [OFF_G + i * 512 for i in range(12)]
                for i, c0 in enumerate(cols):
                    std(w_in[l], c0, base + BID["in"] + i)
                jobs.append((128, [(w_in[l][:, OFF_DT:OFF_DT + 32].rearrange("(c p) n -> p c n", p=128), 16, 32, 0)],
                             ("w", base + BID["dt"], 512)))
                for nm, wm in (("ossm", w_o_ssm), ("omem", w_o_mem), ("out", w_out)):
                    for i in range(4):
                        std(wm[l], i * 512, base + BID[nm] + i)
                for i in range(8):
                    us = []
                    for hf in range(2):
                        v = w_o_swa[l][hf * 1024:(hf + 1) * 1024, i * 256:(i + 1) * 256].rearrange("(c p) n -> p c n", p=64)
                        us.append((v, 16, 256, hf * 4096))
                    jobs.append((64, us, ("w", base + BID["oswa"] + i, 8192)))
            if do_peer:
                for (src, dst) in ((peer_u, usc), (peer_v, vsc)):
                    for blk in range(64 * n_layers):
                        r0 = blk * 256
                        jobs.append((128, [(src[r0:r0 + 256, :].rearrange("(a p) d -> p a d", p=128), 2, D, 0)],
                                     ("t", dst[r0:r0 + 256, :].rearrange("(a p) d -> p a d", p=128), 4096)))
            units = []
            for ji, (kp, us, dst) in enumerate(jobs):
                for ui, u in enumerate(us):
                    units.append((ji, kp, u, ui == len(us) - 1, dst))

            def issue_in(k):
                ji, kp, (src, a_, n_, off), last, dst = units[k]
                wt, wr = WB[k % 2]
                kb.dma("sp", wt[0:kp, 0:a_ * n_].rearrange("p (c n) -> p c n", n=n_), src, writes=[wr])
            if units:
                issue_in(0)
            for k in range(len(units)):
                if k + 1 < len(units):
                    issue_in(k + 1)
                ji, kp, (src, a_, n_, off), last, dst = units[k]
                wt, wr = WB[k % 2]
                stg, stgr = WB[2 + ji % 2]
                stb = stg[0:kp, :].bitcast(BF16)
                dstv = stb[:, off:off + a_ * n_]
                srcv = wt[0:kp, 0:a_ * n_]
                e_ = k % 3
                if e_ == 0:
                    V(lambda e: e.tensor_copy(dstv, srcv), r=[wr], w=[stgr])
                elif e_ == 1:
                    A(lambda e: e.copy(dstv, srcv), r=[wr], w=[stgr])
                else:
                    G(lambda e: e.tensor_copy(dstv, srcv), r=[wr], w=[stgr])
                if last:
                    if dst[0] == "w":
                        kb.dma("sp", wsc[dst[1]][0:kp, 0:dst[2]], stb[:, 0:dst[2]], reads=[stgr], writes=[Res("wsc_tmp")])
                    else:
                        kb.dma("sp", dst[1], stb[:, 0:dst[2]].rearrange("p (a d) -> p a d", d=D), reads=[stgr],
                               writes=[Res("tsc_tmp")])
            kb.dma_barrier("sp")

        INB = {"z": 0, "xbc": 4, "q": 10, "kv": 14, "qm": 15, "g": 19}

        def rms_rstd(ss_ap, rs_ap, n, T, ssr, rsr):
            A(lambda e: e.activation(rs_ap, ss_ap, AF.Sqrt, bias=epst[:T, :], scale=1.0 / n), r=[ssr, epstr], w=[rsr])
            V(lambda e: e.reciprocal(rs_ap, rs_ap), r=[rsr], w=[rsr])

        def norm_T(src, srcr, T, gcol_fn, dst, dstr, scale=None):
            ss, ssr = SM["ss"]
            rs, rsr = SM["rs"]
            V(lambda e: e.memset(ss[:T, 0:1], 0.0), w=[ssr])
            A(lambda e: e.activation(XN[:T, :], src[:T, :], AF.Square, accum_out=ss[:T, 0:1]), r=[srcr], w=[XNr, ssr])
            rms_rstd(ss[:T, 0:1], rs[:T, 0:1], D, T, ssr, rsr)
            A(lambda e: e.activation(XN[:T, :], src[:T, :], AF.Copy, scale=rs[:T, 0:1]), r=[srcr, rsr], w=[XNr])
            transp(XN, XNr, T, 16, gcol_fn, dst, dstr)

        tctr = [0]

        def transp(src, srcr, T, nch, gcol_fn, dst, dstr, width=128, src_off=0):
            for c in range(nch):
                b = tctr[0] % 2
                tctr[0] += 1
                pt, ptr = PS[b]
                P(lambda e: e.transpose(pt[:width, 0:T], src[:T, src_off + c * width: src_off + (c + 1) * width],
                                        ident[:T, :T]), r=[srcr, identr], w=[ptr])
                if gcol_fn is None:
                    if c % 2 == 0:
                        V(lambda e: e.tensor_copy(dst[:width, c, 0:T], pt[:width, 0:T]), r=[ptr], w=[dstr])
                    else:
                        A(lambda e: e.copy(dst[:width, c, 0:T], pt[:width, 0:T]), r=[ptr], w=[dstr])
                else:
                    V(lambda e: e.tensor_scalar(dst[:width, c, 0:T], pt[:width, 0:T], gcol_fn(c), None, ALU.mult),
                      r=[ptr, pcolsr], w=[dstr])

        def mem_precompute(l):
            g0, g0r = Gb[0]
            g1, g1r = Gb[1]
            g2, g2r = Gb[2]
            kT = g2[:, 0:2048].rearrange("p (c t) -> p c t", t=128)
            kb.dma("sp", g1[:, 0:D], gkm_bc[l], writes=[g1r])
            for mt in range(2):
                kb.dma("sp", X[:, :], memp[mt * 128:(mt + 1) * 128, :], writes=[Xr])
                norm_T(X, Xr, 128, lambda c: pc(l, 32 + c), HT, HTr)
                for blk in range(8):
                    pt, ptr = PS[2 + blk % 2]
                    proj(lambda c: HT[:, c, :], 16, 128, w_mem_kv[l][:, blk * 512:(blk + 1) * 512], 512, 128,
                         pt[:, :], ptr, HTr)
                    if blk < 4:
                        ss, ssr = SM["ss"]
                        rs, rsr = SM["rs"]
                        V(lambda e: e.memset(ss[:, 1:2], 0.0), w=[ssr])
                        A(lambda e: e.activation(g0[:, blk * 512:(blk + 1) * 512], pt[:, :], AF.Square,
                                                 accum_out=ss[:, 1:2]), r=[ptr], w=[g0r, ssr])
                        rms_rstd(ss[:, 1:2], rs[:, 1:2], 512, 128, ssr, rsr)
                        V(lambda e: e.scalar_tensor_tensor(g0[:, blk * 512:(blk + 1) * 512], pt[:, :], rs[:, 1:2],
                                                           g1[:, blk * 512:(blk + 1) * 512], ALU.mult, ALU.mult),
                          r=[ptr, rsr, g1r], w=[g0r])
                    else:
                        A(lambda e: e.copy(MG[:, (blk - 4) * 512:(blk - 3) * 512], pt[:, :]), r=[ptr], w=[MGr])
                kb.dma("sp", pmk[l][mt * 128:(mt + 1) * 128, :], g0[:, 0:D], reads=[g0r], writes=[ores_pmk])
                kb.dma("sp", pmv[l][mt * 128:(mt + 1) * 128, :], MG[:, :], reads=[MGr], writes=[pmv_r[l]])
                transp(g0, g0r, 128, 16, None, kT, g2r)
                kb.dma("sp", pmkT[l][:, :, mt * 128:(mt + 1) * 128], kT, reads=[g2r], writes=[pmkT_r[l]])

        def tile_layer(T, l, has_prev, memK_ap, memK_r, memV_ap, memV_r, conv_out_ap, conv_out_r, win_out=None, run_peer=True):
            Sl, Slr = S[l]
            CTl, CTlr = CT[l]
            KTPl, KTPlr = KTP[l]
            VPl, VPlr = VP[l]
            ss, ssr = SM["ss"]
            rs, rsr = SM["rs"]
            g0, g0r = Gb[0]
            g1, g1r = Gb[1]
            g2, g2r = Gb[2]
            g3, g3r = Gb[3]
            g4, g4r = Gb[4]
            g5, g5r = Gb[5]
            bg, bgr = SM["bg"]

            wb0 = l * NBLK
            norm_T(X, Xr, T, lambda c: pc(l, c), HTb, HTr)

            def gate_merge(br, blk, br_ps, br_psr, first):
                pg, pgr = PS[4 + blk % 2]
                col = OFF_G + br * 2048 + blk * 512
                kb.dma("sp", bg[0:1, :], bgate[l][:, br * 2048 + blk * 512: br * 2048 + (blk + 1) * 512], writes=[bgr])
                projb(lambda c: HTb[:, c, :T], 16, 128, wb0 + BID["in"] + INB["g"] + br * 4 + blk, 512, T, pg[:T, :], pgr, HTr,
                      extra=(ones[0:1, 0:T], bg[0:1, 0:512], [onesr, bgr]))
                gs = g1[:T, 2600:3112]
                A(lambda e: e.activation(gs, pg[:T, :], AF.Sigmoid), r=[pgr], w=[g1r])
                mgs = MG[:T, blk * 512:(blk + 1) * 512]
                if first:
                    V(lambda e: e.tensor_tensor(mgs, gs, br_ps, ALU.mult), r=[g1r, br_psr], w=[MGr])
                else:
                    V(lambda e: e.tensor_tensor(gs, gs, br_ps, ALU.mult), r=[g1r, br_psr], w=[g1r])
                    V(lambda e: e.tensor_tensor(mgs, mgs, gs, ALU.add), r=[g1r, MGr], w=[MGr])

            def out_branch(br, lhs_fn, nk, kp, bid0, lhs_res, ncols):
                for blk in range(4):
                    pb, pbr = PS[6 + blk % 2]
                    projb(lhs_fn, nk, kp, bid0 + blk, 512, T, pb[:T, :], pbr, lhs_res)
                    gate_merge(br, blk, pb[:T, :], pbr, br == 0)

            xraw = g0
            for blk in range(6):
                pt, ptr = PS[2 + blk % 2]
                projb(lambda c: HTb[:, c, :T], 16, 128, wb0 + BID["in"] + INB["xbc"] + blk, 512, T,
                      pt[:T, :], ptr, HTr)
                if blk % 2 == 0:
                    A(lambda e: e.copy(xraw[:T, blk * 512:(blk + 1) * 512], pt[:T, :]), r=[ptr], w=[g0r])
                else:
                    V(lambda e: e.tensor_copy(xraw[:T, blk * 512:(blk + 1) * 512], pt[:T, :]), r=[ptr], w=[g0r])
            if conv_out_ap is not None:
                kb.dma("sp", conv_out_ap, xraw[T - 3:T, 0:3072], reads=[g0r], writes=[conv_out_r])
            xbcT = g1[:, 0:24 * 131].rearrange("p (c t) -> p c t", t=131)
            V(lambda e: e.tensor_copy(xbcT[:, :, 0:3], CTl[:, :, :]), r=[CTlr], w=[g1r])
            for c in range(24):
                b = tctr[0] % 2
                tctr[0] += 1
                pt, ptr = PS[b]
                P(lambda e: e.transpose(pt[:, 0:T], xraw[:T, c * 128:(c + 1) * 128], ident[:T, :T]),
                  r=[g0r, identr], w=[ptr])
                if c % 2 == 0:
                    V(lambda e: e.tensor_copy(xbcT[:, c, 3:3 + T], pt[:, 0:T]), r=[ptr], w=[g1r])
                else:
                    A(lambda e: e.copy(xbcT[:, c, 3:3 + T], pt[:, 0:T]), r=[ptr], w=[g1r])
            V(lambda e: e.tensor_copy(CTl[:, :, :], xbcT[:, :, T:T + 3]), r=[g1r], w=[CTlr])
            xact = g2[:, 0:24 * 128].rearrange("p (c t) -> p c t", t=128)
            for c in range(24):
                V(lambda e: e.tensor_scalar(xact[:, c, 0:T], xbcT[:, c, 0:T], pc(l, 68 + c), pc(l, 164 + c),
                                            ALU.mult, ALU.add), r=[g1r, pcolsr], w=[g2r])
                for k in range(1, 4):
                    V(lambda e: e.scalar_tensor_tensor(xact[:, c, 0:T], xbcT[:, c, k:k + T], pc(l, 68 + k * 24 + c),
                                                       xact[:, c, 0:T], ALU.mult, ALU.add),
                      r=[g1r, g2r, pcolsr], w=[g2r])
            A(lambda e: e.activation(xact[:, :, 0:T], xact[:, :, 0:T], AF.Silu), r=[g2r], w=[g2r])
            xs = g3
            xs3 = g3[:, 0:2048].rearrange("p (c t) -> p c t", t=128)
            transp_fm(xact, g2r, T, 0, 16, xs3, g3r)
            bt3 = g4[:, 0:512].rearrange("p (c t) -> p c t", t=128)
            transp_fm(xact, g2r, T, 16, 4, bt3, g4r)
            dt, dtr = SM["dt"]
            dA, dAr = SM["dA"]
            Aneg, Anegr = SM["A"]
            pt, ptr = PS[2]
            projb(lambda c: HTb[:, c, :T], 16, 128, wb0 + BID["dt"], 32, T, pt[:T, 0:32], ptr, HTr)
            V(lambda e: e.tensor_tensor(dt[:T, :], pt[:T, 0:32], pr(l, 0, 32)[:T, :], ALU.add), r=[ptr, prowr], w=[dtr])
            A(lambda e: e.activation(dt[:T, :], dt[:T, :], AF.Exp), r=[dtr], w=[dtr])
            A(lambda e: e.activation(dt[:T, :], dt[:T, :], AF.Ln, bias=onec[:T, :], scale=1.0), r=[dtr, onecr], w=[dtr])
            A(lambda e: e.activation(Aneg[:, :], pr(l, 32, 32), AF.Exp), r=[prowr], w=[Anegr])
            V(lambda e: e.scalar_tensor_tensor(dA[:T, :], dt[:T, :], -1.0, Aneg[:T, :], ALU.mult, ALU.mult),
              r=[dtr, Anegr], w=[dAr])
            cum, cumr = SM["cum"]
            cl, clr = SM["cl"]
            ecum, ecumr = SM["ecum"]
            wend, wendr = SM["wend"]
            pt, ptr = PS[3]
            P(lambda e: e.matmul(pt[:T, 0:32], triu[:T, :T], dA[:T, :], start=True, stop=True), r=[triur, dAr], w=[ptr])
            P(lambda e: e.matmul(pt[:, 32:64], ones[:T, :], dA[:T, :], start=True, stop=True), r=[onesr, dAr], w=[ptr])
            V(lambda e: e.tensor_copy(cum[:T, :], pt[:T, 0:32]), r=[ptr], w=[cumr])
            V(lambda e: e.tensor_copy(cl[:, :], pt[:, 32:64]), r=[ptr], w=[clr])
            A(lambda e: e.activation(ecum[:T, :], cum[:T, :], AF.Exp), r=[cumr], w=[ecumr])
            V(lambda e: e.tensor_tensor(wend[:T, :], cl[:T, :], cum[:T, :], ALU.subtract), r=[clr, cumr], w=[wendr])
            A(lambda e: e.activation(wend[:T, :], wend[:T, :], AF.Exp), r=[wendr], w=[wendr])
            V(lambda e: e.tensor_tensor(wend[:T, :], wend[:T, :], dt[:T, :], ALU.mult), r=[wendr, dtr], w=[wendr])
            cbm = g4[:, 512:1024].rearrange("p (c t) -> p c t", t=128)
            pt, ptr = PS[2]
            for gq in range(4):
                P(lambda e: e.matmul(pt[:T, gq * 128: gq * 128 + T], xact[:, 16 + gq, 0:T], xact[:, 20 + gq, 0:T],
                                     start=True, stop=True), r=[g2r], w=[ptr])
            for gq in range(4):
                V(lambda e: e.tensor_tensor(cbm[:T, gq, 0:T], pt[:T, gq * 128: gq * 128 + T], triu[:T, :T], ALU.mult),
                  r=[ptr, triur], w=[g4r])
            ysb = g5
            for hf in range(2):
                Rp = g0[:, 0:2048].rearrange("p (h t) -> p h t", t=128)
                for hh in range(16):
                    h = hf * 16 + hh
                    V(lambda e: e.tensor_scalar(Rp[:T, hh, 0:T], triu[:T, :T], dA[:T, h:h + 1], None, ALU.mult),
                      r=[triur, dAr], w=[g0r])
                MT = g1[:, 0:2048].rearrange("p (h t) -> p h t", t=128)
                for q4 in range(4):
                    pt, ptr = PS[4 + q4]
                    for hq in range(4):
                        hh = q4 * 4 + hq
                        P(lambda e: e.matmul(pt[:T, hq * 128: hq * 128 + T], ones[:T, :T], Rp[:T, hh, 0:T],
                                             start=True, stop=True), r=[onesr, g0r], w=[ptr])
                    for hq in range(4):
                        hh = q4 * 4 + hq
                        h = hf * 16 + hh
                        V(lambda e: e.tensor_scalar(MT[:T, hh, 0:T], pt[:T, hq * 128: hq * 128 + T], cum[:T, h:h + 1], 0.0,
                                                    ALU.subtract, ALU.min), r=[ptr, cumr], w=[g1r])
                A(lambda e: e.activation(MT[:T, :, 0:T], MT[:T, :, 0:T], AF.Exp), r=[g1r], w=[g1r])
                for hh in range(16):
                    h = hf * 16 + hh
                    V(lambda e: e.scalar_tensor_tensor(MT[:T, hh, 0:T], MT[:T, hh, 0:T], dt[:T, h:h + 1],
                                                       cbm[:T, h // 8, 0:T], ALU.mult, ALU.mult),
                      r=[g1r, dtr, g4r], w=[g1r])
                for hh in range(16):
                    h = hf * 16 + hh
                    pt, ptr = PS[hh // 8]
                    P(lambda e: e.matmul(pt[:T, (hh % 8) * 64:(hh % 8 + 1) * 64], MT[:T, hh, 0:T],
                                         xs[:T, h * 64:(h + 1) * 64], start=True, stop=True), r=[g1r, g3r], w=[ptr])
                for gq in range(2):
                    gg = hf * 2 + gq
                    pt, ptr = PS[2 + gq]
                    P(lambda e: e.matmul(pt[:T, :], xact[:, 20 + gg, 0:T], Sl[:, gg * 512:(gg + 1) * 512],
                                         start=True, stop=True), r=[g2r, Slr], w=[ptr])
                for gq in range(2):
                    gg = hf * 2 + gq
                    pi, pir = PS[gq]
                    pst, pstr = PS[2 + gq]
                    yv = ysb[:T, gg * 512:(gg + 1) * 512].rearrange("p (h d) -> p h d", d=64)
                    V(lambda e: e.tensor_tensor(yv, pst[:T, :].rearrange("p (h d) -> p h d", d=64),
                                                ecum[:T, gg * 8:(gg + 1) * 8].unsqueeze(2).to_broadcast([T, 8, 64]),
                                                ALU.mult), r=[pstr, ecumr], w=[g5r])
                    V(lambda e: e.tensor_tensor(ysb[:T, gg * 512:(gg + 1) * 512], ysb[:T, gg * 512:(gg + 1) * 512],
                                                pi[:T, :], ALU.add), r=[pir, g5r], w=[g5r])
            xw = g0
            V(lambda e: e.tensor_tensor(xw[:T, 0:2048].rearrange("p (h d) -> p h d", d=64),
                                        xs[:T, 0:2048].rearrange("p (h d) -> p h d", d=64),
                                        pr(l, 64, 32)[:T, :].unsqueeze(2).to_broadcast([T, 32, 64]), ALU.mult),
              r=[g3r, prowr], w=[g0r])
            V(lambda e: e.tensor_tensor(ysb[:T, 0:2048], ysb[:T, 0:2048], xw[:T, 0:2048], ALU.add), r=[g0r, g5r], w=[g5r])
            V(lambda e: e.tensor_tensor(xw[:T, 0:2048].rearrange("p (h d) -> p h d", d=64),
                                        xs[:T, 0:2048].rearrange("p (h d) -> p h d", d=64),
                                        wend[:T, :].unsqueeze(2).to_broadcast([T, 32, 64]), ALU.mult),
              r=[g3r, wendr], w=[g0r])
            A(lambda e: e.activation(cl[:, :], cl[:, :], AF.Exp), r=[clr], w=[clr])
            for gg in range(4):
                pt, ptr = PS[4 + gg]
                P(lambda e: e.matmul(pt[:, :], bt3[:T, gg, :], xw[:T, gg * 512:(gg + 1) * 512], start=True, stop=True),
                  r=[g4r, g0r], w=[ptr])
            V(lambda e: e.tensor_tensor(Sl[:, :].rearrange("p (h d) -> p h d", d=64),
                                        Sl[:, :].rearrange("p (h d) -> p h d", d=64),
                                        cl[:, :].unsqueeze(2).to_broadcast([128, 32, 64]), ALU.mult),
              r=[Slr, clr], w=[Slr])
            for gg in range(4):
                pt, ptr = PS[4 + gg]
                V(lambda e: e.tensor_tensor(Sl[:, gg * 512:(gg + 1) * 512], Sl[:, gg * 512:(gg + 1) * 512], pt[:, :],
                                            ALU.add), r=[Slr, ptr], w=[Slr])
            for blk in range(4):
                pt, ptr = PS[blk % 2]
                projb(lambda c: HTb[:, c, :T], 16, 128, wb0 + BID["in"] + INB["z"] + blk, 512, T,
                      pt[:T, :], ptr, HTr)
                zs = g1[:T, 0:512]
                A(lambda e: e.activation(zs, pt[:T, :], AF.Silu), r=[ptr], w=[g1r])
                V(lambda e: e.tensor_tensor(ysb[:T, blk * 512:(blk + 1) * 512], ysb[:T, blk * 512:(blk + 1) * 512], zs,
                                            ALU.mult), r=[g1r, g5r], w=[g5r])
                V(lambda e: e.memset(ss[:T, 4 + blk:5 + blk], 0.0), w=[ssr])
                A(lambda e: e.activation(zs, ysb[:T, blk * 512:(blk + 1) * 512], AF.Square,
                                         accum_out=ss[:T, 4 + blk:5 + blk]), r=[g5r], w=[g1r, ssr])
            rms_rstd(ss[:T, 4:8], rs[:T, 4:8], 512, T, ssr, rsr)
            V(lambda e: e.tensor_tensor(ysb[:T, 0:2048].rearrange("p (g d) -> p g d", d=512),
                                        ysb[:T, 0:2048].rearrange("p (g d) -> p g d", d=512),
                                        rs[:T, 4:8].unsqueeze(2).to_broadcast([T, 4, 512]), ALU.mult),
              r=[g5r, rsr], w=[g5r])
            yT = g2[:, 0:1024].bitcast(BF16).rearrange("p (c t) -> p c t", t=128)
            transp(ysb, g5r, T, 16, lambda c: pc(l, 48 + c), yT, g2r)
            out_branch(0, lambda c: yT[:, c, 0:T], 16, 128, wb0 + BID["ossm"], g2r, 512)

            qsb = g0
            for blk in range(4):
                pt, ptr = PS[2 + blk % 2]
                projb(lambda c: HTb[:, c, :T], 16, 128, wb0 + BID["in"] + INB["q"] + blk, 512, T,
                      pt[:T, :], ptr, HTr)
                A(lambda e: e.copy(qsb[:T, blk * 512:(blk + 1) * 512], pt[:T, :]), r=[ptr], w=[g0r])
            pt, ptr = PS[2]
            projb(lambda c: HTb[:, c, :T], 16, 128, wb0 + BID["in"] + INB["kv"], 512, T, pt[:T, :], ptr, HTr)
            kv = g1
            A(lambda e: e.copy(kv[:T, 0:512], pt[:T, :]), r=[ptr], w=[g1r])
            sq = g2
            V(lambda e: e.tensor_tensor(sq[:T, 0:2048], qsb[:T, 0:2048], qsb[:T, 0:2048], ALU.mult), r=[g0r], w=[g2r])
            V(lambda e: e.tensor_reduce(ss[:T, 8:40], sq[:T, 0:2048].rearrange("p (h d) -> p h d", d=64), AX.X, ALU.add),
              r=[g2r], w=[ssr])
            rms_rstd(ss[:T, 8:40], rs[:T, 8:40], 64, T, ssr, rsr)
            V(lambda e: e.tensor_tensor(qsb[:T, 0:2048].rearrange("p (h d) -> p h d", d=64),
                                        qsb[:T, 0:2048].rearrange("p (h d) -> p h d", d=64),
                                        rs[:T, 8:40].unsqueeze(2).to_broadcast([T, 32, 64]), ALU.mult),
              r=[g0r, rsr], w=[g0r])
            V(lambda e: e.tensor_tensor(sq[:T, 0:256], kv[:T, 0:256], kv[:T, 0:256], ALU.mult), r=[g1r], w=[g2r])
            V(lambda e: e.tensor_reduce(ss[:T, 0:4], sq[:T, 0:256].rearrange("p (h d) -> p h d", d=64), AX.X, ALU.add),
              r=[g2r], w=[ssr])
            rms_rstd(ss[:T, 0:4], rs[:T, 0:4], 64, T, ssr, rsr)
            ktok = kv[:T, 512:768]
            V(lambda e: e.tensor_tensor(ktok.rearrange("p (h d) -> p h d", d=64),
                                        kv[:T, 0:256].rearrange("p (h d) -> p h d", d=64),
                                        rs[:T, 0:4].unsqueeze(2).to_broadcast([T, 4, 64]), ALU.mult), r=[g1r, rsr], w=[g1r])
            V(lambda e: e.tensor_tensor(ktok, ktok, pr(l, 128, 256)[:T, :], ALU.mult), r=[g1r, prowr], w=[g1r])
            transp(kv, g1r, T, 4, None, KTC, KTCr, width=64, src_off=512)
            if win_out is not None:
                win_out(kv, g1r)
            esink, esinkr = SM["esink"]
            A(lambda e: e.activation(esink[:, :], pr(l, 96, 32), AF.Exp), r=[prowr], w=[esinkr])
            nT = 8 * T
            for gq in range(4):
                qT = g2[:64, 0:1024].rearrange("p (h t) -> p h t", t=128)
                transp(qsb, g0r, T, 8, lambda c: pc(l, 188)[:64, :], qT, g2r, width=64, src_off=gq * 512)
                Em = g3
                kb.dma("sp", Em[:, 0:2048], emask_d[gq], writes=[g3r])
                Em4 = g3[:, 0:2048].rearrange("p (b h q) -> p b h q", b=2, h=8)
                PT = g4[:, 0:2048].rearrange("p (b h q) -> p b h q", b=2, h=8)
                blocks = ([0] if has_prev else []) + [1]
                for bi, kbk in enumerate(blocks):
                    nk = 128 if kbk == 0 else T
                    for hb in range(2):
                        pt, ptr = PS[2 + hb]
                        for hq in range(4):
                            hh = hb * 4 + hq
                            if kbk == 0:
                                P(lambda e: e.matmul(pt[:nk, hq * 128: hq * 128 + T], KTPl[:64, gq, 0:nk], qT[:64, hh, 0:T],
                                                     start=True, stop=True), r=[KTPlr, g2r], w=[ptr])
                            else:
                                P(lambda e: e.matmul(pt[:nk, hq * 128: hq * 128 + T], KTC[:64, gq, 0:nk], qT[:64, hh, 0:T],
                                                     start=True, stop=True), r=[KTCr, g2r], w=[ptr])
                        pv = pt[:nk, :].rearrange("p (h q) -> p h q", q=128)[:, :, 0:T]
                        A(lambda e: e.activation(PT[:nk, kbk, hb * 4:(hb + 1) * 4, 0:T], pv, AF.Exp, scale=0.125),
                          r=[ptr], w=[g4r])
                        V(lambda e: e.tensor_tensor(PT[:nk, kbk, hb * 4:(hb + 1) * 4, 0:T],
                                                    PT[:nk, kbk, hb * 4:(hb + 1) * 4, 0:T],
                                                    Em4[:nk, kbk, hb * 4:(hb + 1) * 4, 0:T], ALU.mult),
                          r=[g4r, g3r], w=[g4r])
                for hb in range(2):
                    po, por = PS[4 + hb]
                    pd, pdr = PS[6 + hb]
                    for hq in range(4):
                        hh = hb * 4 + hq
                        for bi, kbk in enumerate(blocks):
                            nk = 128 if kbk == 0 else T
                            if kbk == 0:
                                vsrc, vr = VPl[:nk, gq * 64:(gq + 1) * 64], VPlr
                            else:
                                vsrc, vr = kv[:nk, 256 + gq * 64: 256 + (gq + 1) * 64], g1r
                            P(lambda e: e.matmul(po[:64, hq * 128: hq * 128 + T], vsrc, PT[:nk, kbk, hh, 0:T],
                                                 start=(bi == 0), stop=(bi == len(blocks) - 1)), r=[vr, g4r], w=[por])
                            P(lambda e: e.matmul(pd[:64, hq * 128: hq * 128 + T], ones[:nk, 0:64], PT[:nk, kbk, hh, 0:T],
                                                 start=(bi == 0), stop=(bi == len(blocks) - 1)), r=[onesr, g4r], w=[pdr])
                    oT = g5[:64, 0:2048].bitcast(BF16).rearrange("p (h t) -> p h t", t=128)
                    oTr = g5r
                    hbase = gq * 8 + hb * 4
                    dn = g1[:64, 1024:1536].rearrange("p (h t) -> p h t", t=128)
                    V(lambda e: e.tensor_tensor(dn[:, :, 0:T], pd[:64, :].rearrange("p (h q) -> p h q", q=128)[:, :, 0:T],
                                                esink[:64, gq * 8 + hb * 4: gq * 8 + hb * 4 + 4].unsqueeze(2).to_broadcast([64, 4, T]),
                                                ALU.add), r=[pdr, esinkr], w=[g1r])
                    V(lambda e: e.reciprocal(dn[:, :, 0:T], dn[:, :, 0:T]), r=[g1r], w=[g1r])
                    V(lambda e: e.tensor_tensor(oT[:, hbase:hbase + 4, 0:T],
                                                po[:64, :].rearrange("p (h q) -> p h q", q=128)[:, :, 0:T],
                                                dn[:, :, 0:T], ALU.mult), r=[por, g1r], w=[oTr])
            if T == 128:
                V(lambda e: e.tensor_copy(KTPl[:, :, :], KTC[:, :, :]), r=[KTCr], w=[KTPlr])
                V(lambda e: e.tensor_copy(VPl[:, :], kv[:, 256:512]), r=[g1r], w=[VPlr])
            win_src = (kv, g1r)

            oTa = g5[:64, 0:2048].bitcast(BF16).rearrange("p (h t) -> p h t", t=128)
            for blk in range(4):
                pb, pbr = PS[6 + blk % 2]
                for sub in range(2):
                    projb(lambda c: oTa[:, c, 0:T], 32, 64, wb0 + BID["oswa"] + blk * 2 + sub, 256, T,
                          pb[:T, sub * 256:(sub + 1) * 256], pbr, g5r)
                gate_merge(1, blk, pb[:T, :], pbr, False)

            qm = g0
            for blk in range(4):
                pt, ptr = PS[2 + blk % 2]
                projb(lambda c: HTb[:, c, :T], 16, 128, wb0 + BID["in"] + INB["qm"] + blk, 512, T,
                      pt[:T, :], ptr, HTr)
                V(lambda e: e.memset(ss[:T, blk:blk + 1], 0.0), w=[ssr])
                A(lambda e: e.activation(qm[:T, blk * 512:(blk + 1) * 512], pt[:T, :], AF.Square,
                                         accum_out=ss[:T, blk:blk + 1]), r=[ptr], w=[g0r, ssr])
                rms_rstd(ss[:T, blk:blk + 1], rs[:T, blk:blk + 1], 512, T, ssr, rsr)
                A(lambda e: e.activation(qm[:T, blk * 512:(blk + 1) * 512], pt[:T, :], AF.Copy, scale=rs[:T, blk:blk + 1]),
                  r=[ptr, rsr], w=[g0r])
            qmT = g1[:, 0:2048].rearrange("p (c t) -> p c t", t=128)
            transp(qm, g0r, T, 16, lambda c: pc(l, 64 + c % 4), qmT, g1r)
            omT = g5[:, 0:1024].bitcast(BF16).rearrange("p (c t) -> p c t", t=128)
            mx, mxr = SM["mx"]
            smm, smr = SM["sm"]
            for hm in range(4):
                KTh = g2[:, 0:1024].rearrange("p (c m) -> p c m", m=256)
                Vh = g3[:, 0:1024].rearrange("p (b d) -> p b d", d=512)
                kb.dma("sp", KTh, memK_ap[:, hm * 4:(hm + 1) * 4, :], reads=[memK_r], writes=[g2r])
                kb.dma("sp", Vh, memV_ap[:, hm * 512:(hm + 1) * 512].rearrange("(b p) d -> p b d", p=128),
                       reads=[memV_r], writes=[g3r])
                pt, ptr = PS[2]
                for c in range(4):
                    P(lambda e: e.matmul(pt[:T, 0:256], qmT[:, hm * 4 + c, 0:T], KTh[:, c, :], start=(c == 0), stop=(c == 3)),
                      r=[g1r, g2r], w=[ptr])
                V(lambda e: e.tensor_reduce(mx[:T, 0:1], pt[:T, 0:256], AX.X, ALU.max), r=[ptr], w=[mxr])
                V(lambda e: e.tensor_scalar(mx[:T, 0:1], mx[:T, 0:1], -(512 ** -0.5), None, ALU.mult), r=[mxr], w=[mxr])
                Pm = g4[:, 0:256]
                V(lambda e: e.memset(smm[:T, 0:1], 0.0), w=[smr])
                A(lambda e: e.activation(Pm[:T, :], pt[:T, 0:256], AF.Exp, bias=mx[:T, 0:1], scale=512 ** -0.5,
                                         accum_out=smm[:T, 0:1]), r=[ptr, mxr], w=[g4r, smr])
                V(lambda e: e.reciprocal(smm[:T, 0:1], smm[:T, 0:1]), r=[smr], w=[smr])
                V(lambda e: e.tensor_scalar(Pm[:T, :], Pm[:T, :], smm[:T, 0:1], None, ALU.mult), r=[g4r, smr], w=[g4r])
                PmT = g4[:, 512:768].rearrange("p (c t) -> p c t", t=128)
                transp(g4, g4r, T, 2, None, PmT, g4r)
                pt2, pt2r = PS[3]
                for dc in range(4):
                    for mc in range(2):
                        P(lambda e: e.matmul(pt2[:, dc * 128: dc * 128 + T], Vh[:, mc, dc * 128:(dc + 1) * 128],
                                             PmT[:, mc, 0:T], start=(mc == 0), stop=(mc == 1)), r=[g3r, g4r], w=[pt2r])
                A(lambda e: e.copy(omT[:, hm * 4:(hm + 1) * 4, 0:T],
                                   pt2[:, :].rearrange("p (c t) -> p c t", t=128)[:, :, 0:T]), r=[pt2r], w=[g5r])
            out_branch(2, lambda c: omT[:, c, 0:T], 16, 128, wb0 + BID["omem"], g5r, 512)

            mT = g2[:, 0:1024].bitcast(BF16).rearrange("p (c t) -> p c t", t=128)
            transp(MG, MGr, T, 16, None, mT, g2r)
            for blk in range(4):
                pt, ptr = PS[2 + blk % 2]
                projb(lambda c: mT[:, c, 0:T], 16, 128, wb0 + BID["out"] + blk, 512, T, pt[:T, :], ptr, g2r)
                V(lambda e: e.tensor_tensor(X[:T, blk * 512:(blk + 1) * 512], X[:T, blk * 512:(blk + 1) * 512], pt[:T, :],
                                            ALU.add), r=[ptr, Xr], w=[Xr])
            if do_peer and run_peer:
                peer(T, l, X, Xr)
            return win_src

        def transp_fm(src3, srcr, T, c0, nch, dst3, dstr):
            for c in range(nch):
                b = tctr[0] % 2
                tctr[0] += 1
                pt, ptr = PS[b]
                P(lambda e: e.transpose(pt[:T, 0:128], src3[:, c0 + c, 0:T], ident[:, :]), r=[srcr, identr], w=[ptr])
                if c % 2 == 0:
                    V(lambda e: e.tensor_copy(dst3[:T, c, :], pt[:T, 0:128]), r=[ptr], w=[dstr])
                else:
                    A(lambda e: e.copy(dst3[:T, c, :], pt[:T, 0:128]), r=[ptr], w=[dstr])

        def peer(T, l, Xt, Xtr):
            ss, ssr = SM["ss"]
            rs, rsr = SM["rs"]
            g0, g0r = Gb[0]
            g1, g1r = Gb[1]
            g2, g2r = Gb[2]
            g3, g3r = Gb[3]
            g4, g4r = Gb[4]
            g5, g5r = Gb[5]
            for i_, nm_ in enumerate(["k1", "k2", "posf", "idxf", "gates", "acol", "wcol", "sc16"]):
                SM[nm_] = (g0[:, 2048 + i_ * 128: 2048 + (i_ + 1) * 128], g0r)
            norm_T(Xt, Xtr, T, lambda c: pc(l, 16 + c), HT, HTr)
            h2b = g0[:, 0:1024].bitcast(BF16)
            kb.dma("sp", g5[:, 0:D], gffn_bc[l], writes=[g5r])
            V(lambda e: e.tensor_tensor(h2b[:T, :], g5[:T, 0:D], XN[:T, :], ALU.mult), r=[g5r, XNr], w=[g0r])
            for blk in range(4):
                pt, ptr = PS[2 + blk % 2]
                proj(lambda c: HT[:, c, :T], 16, 128, w_peer_q[l][:, blk * 512:(blk + 1) * 512], 512, T, pt[:T, :], ptr, HTr)
                A(lambda e: e.copy(g1[:T, blk * 512:(blk + 1) * 512], pt[:T, :]), r=[ptr], w=[g1r])
            qT = g2[:, 0:2048].rearrange("p (c t) -> p c t", t=128)
            transp(g1, g1r, T, 16, None, qT, g2r)
            skT = g3[:, 0:2048].rearrange("p (c n) -> p c n", n=128)
            kb.dma("sp", skT, skT_d[l], writes=[g3r])
            sc = g4[:, 0:2048].rearrange("p (c n) -> p c n", n=128)
            for q4 in range(4):
                pt, ptr = PS[4 + q4]
                for j in range(4):
                    hc = q4 * 4 + j
                    P(lambda e: e.matmul(pt[:T, j * 128:(j + 1) * 128], qT[:, hc, 0:T], skT[:, hc, :], start=True, stop=True),
                      r=[g2r, g3r], w=[ptr])
                A(lambda e: e.copy(g4[:T, q4 * 512:(q4 + 1) * 512], pt[:T, :]), r=[ptr], w=[g4r])
            sc2 = g5[:, 0:2048].rearrange("p (c n) -> p c n", n=128)
            vtop, vtopr = SM["vtop"]
            itopf, itopfr = SM["itopf"]
            for hc in range(16):
                V(lambda e: e.max(vtop[:T, hc * 16: hc * 16 + 8], sc[:T, hc, :]), r=[g4r], w=[vtopr])
                V(lambda e: e.max_index(itop_u[:T, hc * 16: hc * 16 + 8], vtop[:T, hc * 16: hc * 16 + 8], sc[:T, hc, :]),
                  r=[g4r, vtopr], w=[itop_ur])
                V(lambda e: e.match_replace(sc2[:T, hc, :], vtop[:T, hc * 16: hc * 16 + 8], sc[:T, hc, :], -1e30),
                  r=[g4r, vtopr], w=[g5r])
                V(lambda e: e.max(vtop[:T, hc * 16 + 8: hc * 16 + 16], sc2[:T, hc, :]), r=[g5r], w=[vtopr])
                V(lambda e: e.max_index(itop_u[:T, hc * 16 + 8: hc * 16 + 16], vtop[:T, hc * 16 + 8: hc * 16 + 16],
                                        sc2[:T, hc, :]), r=[g5r, vtopr], w=[itop_ur])
            V(lambda e: e.tensor_copy(itopf[:T, :], itop_u[:T, :]), r=[itop_ur], w=[itopfr])
            cand = g1[:, 0:2048].rearrange("p (h a b) -> p h a b", h=8, a=16)
            cand2 = g3[:, 0:2048].rearrange("p (h n) -> p h n", n=256)
            v4 = vtop[:T, :].rearrange("p (h c k) -> p h c k", h=8, c=2)
            V(lambda e: e.tensor_tensor(cand[:T], v4[:, :, 0, :].unsqueeze(3).to_broadcast([T, 8, 16, 16]),
                                        v4[:, :, 1, :].unsqueeze(2).to_broadcast([T, 8, 16, 16]), ALU.add),
              r=[vtopr], w=[g1r])
            candf = g1[:, 0:2048].rearrange("p (h n) -> p h n", n=256)
            sc16, sc16r = SM["sc16"]
            for h in range(8):
                V(lambda e: e.max(sc16[:T, h * 16: h * 16 + 8], candf[:T, h, :]), r=[g1r], w=[sc16r])
                V(lambda e: e.max_index(pos_u[:T, h * 16: h * 16 + 8], sc16[:T, h * 16: h * 16 + 8], candf[:T, h, :]),
                  r=[g1r, sc16r], w=[pos_ur])
                V(lambda e: e.match_replace(cand2[:T, h, :], sc16[:T, h * 16: h * 16 + 8], candf[:T, h, :], -1e30),
                  r=[g1r, sc16r], w=[g3r])
                V(lambda e: e.max(sc16[:T, h * 16 + 8: h * 16 + 16], cand2[:T, h, :]), r=[g3r], w=[sc16r])
                V(lambda e: e.max_index(pos_u[:T, h * 16 + 8: h * 16 + 16], sc16[:T, h * 16 + 8: h * 16 + 16],
                                        cand2[:T, h, :]), r=[g3r, sc16r], w=[pos_ur])
            k1, k1r = SM["k1"]
            k2, k2r = SM["k2"]
            V(lambda e: e.tensor_single_scalar(idx_i[:T, :], pos_u[:T, :].bitcast(I32), 4, ALU.logical_shift_right), r=[pos_ur], w=[idx_ir])
            V(lambda e: e.tensor_copy(k1[:T, :], idx_i[:T, :]), r=[idx_ir], w=[k1r])
            V(lambda e: e.tensor_single_scalar(idx_i[:T, :], pos_u[:T, :].bitcast(I32), 15, ALU.bitwise_and), r=[pos_ur], w=[idx_ir])
            V(lambda e: e.tensor_copy(k2[:T, :], idx_i[:T, :]), r=[idx_ir], w=[k2r])
            oh = g4[:, 0:2048].rearrange("p (h k j) -> p h k j", h=8, k=16)
            i4 = itopf[:T, :].rearrange("p (h c k) -> p h c k", h=8, c=2)
            idxf, idxfr = SM["idxf"]
            posf, posfr = SM["posf"]
            for (kk, kkr, ci, dst) in ((k1, k1r, 0, idxf), (k2, k2r, 1, posf)):
                V(lambda e: e.tensor_tensor(oh[:T], kk[:T, :].rearrange("p (h k) -> p h k", h=8).unsqueeze(3).to_broadcast([T, 8, 16, 16]),
                                            iot16[:T, :].unsqueeze(1).unsqueeze(1).to_broadcast([T, 8, 16, 16]), ALU.is_equal),
                  r=[kkr, iot16r], w=[g4r])
                V(lambda e: e.tensor_tensor(oh[:T], oh[:T], i4[:, :, ci, :].unsqueeze(2).to_broadcast([T, 8, 16, 16]), ALU.mult),
                  r=[g4r, itopfr], w=[g4r])
                V(lambda e: e.tensor_reduce(dst[:T, :], g4[:T, 0:2048].rearrange("p (a j) -> p a j", j=16), AX.X, ALU.add),
                  r=[g4r], w=[idxfr if ci == 0 else posfr])
            V(lambda e: e.scalar_tensor_tensor(idxf[:T, :], idxf[:T, :], 128.0, posf[:T, :], ALU.mult, ALU.add),
              r=[idxfr, posfr], w=[idxfr])
            if l > 0:
                V(lambda e: e.tensor_scalar(idxf[:T, :], idxf[:T, :], float(l * 16384), None, ALU.add), r=[idxfr], w=[idxfr])
            V(lambda e: e.tensor_copy(idx_i[:T, :], idxf[:T, :]), r=[idxfr], w=[idx_ir])
            gates, gatesr = SM["gates"]
            t8, t8r = SM["t8"]
            s3 = sc16[:T, :].rearrange("p (h k) -> p h k", k=16)
            g3v = gates[:T, :].rearrange("p (h k) -> p h k", k=16)
            V(lambda e: e.tensor_tensor(g3v, s3, s3[:, :, 0:1].to_broadcast([T, 8, 16]), ALU.subtract), r=[sc16r], w=[gatesr])
            A(lambda e: e.activation(gates[:T, :], gates[:T, :], AF.Exp), r=[gatesr], w=[gatesr])
            V(lambda e: e.tensor_reduce(t8[:T, :], g3v, AX.X, ALU.add), r=[gatesr], w=[t8r])
            V(lambda e: e.reciprocal(t8[:T, :], t8[:T, :]), r=[t8r], w=[t8r])
            V(lambda e: e.tensor_tensor(g3v, g3v, t8[:T, :].unsqueeze(2).to_broadcast([T, 8, 16]), ALU.mult),
              r=[gatesr, t8r], w=[gatesr])
            acol, acolr = SM["acol"]
            wcol, wcolr = SM["wcol"]
            V(lambda e: e.memset(acol[:T, :], 0.0), w=[acolr])
            gbufs = []
            for k_ in range(3):
                for i_ in range(1, 5):
                    r_ = Res("gs%d_%d" % (i_, k_))
                    r_.w = Gb[i_][1].w
                    r_.r = dict(Gb[i_][1].r)
                    gbufs.append((Gb[i_][0][:, k_ * 1024:(k_ + 1) * 1024], r_))
            NGB = len(gbufs)
            for s in range(128):
                gb, gbr = gbufs[s % NGB]
                gbv = gb[:, 0:1024].bitcast(BF16)
                kb.dma("pool", gbv[:T, :], usc, reads=[idx_ir], writes=[gbr],
                       indirect=bass.IndirectOffsetOnAxis(idx_i[:T, s:s + 1], 0))
                V(lambda e: e.scalar_tensor_tensor(gbv[:T, :], gbv[:T, :], 1.0, h2b[:T, :], ALU.mult, ALU.mult,
                                                   accum_out=acol[:T, s:s + 1]), r=[gbr, g0r], w=[gbr, acolr])
            A(lambda e: e.activation(wcol[:T, :], acol[:T, :], AF.Gelu), r=[acolr], w=[wcolr])
            V(lambda e: e.tensor_tensor(wcol[:T, :], wcol[:T, :], gates[:T, :], ALU.mult), r=[wcolr, gatesr], w=[wcolr])
            NDS = 8
            dres = [Res("diag%d" % i) for i in range(NDS)]
            for s in range(128):
                gb, gbr = gbufs[s % NGB]
                gbv = gb[:, 0:1024].bitcast(BF16)
                kb.dma("pool", gbv[:T, :], vsc, reads=[idx_ir], writes=[gbr],
                       indirect=bass.IndirectOffsetOnAxis(idx_i[:T, s:s + 1], 0))
                ds_ = s % NDS
                dv = g5[:, 0:1024].bitcast(BF16)[:T, ds_ * 128: ds_ * 128 + T]
                wl = [dres[ds_]] + ([g5r] if (s < NDS or s >= 128 - NDS) else [])
                V(lambda e: e.tensor_scalar(dv, ident[:T, :T], wcol[:T, s:s + 1], None, ALU.mult),
                  r=[identr, wcolr], w=wl)
                for q in range(4):
                    pq, pqr = PS[4 + q]
                    rl = [dres[ds_], gbr] + ([g5r] if s >= 128 - NDS else [])
                    P(lambda e: e.matmul(pq[:T, :], dv, gbv[:T, q * 512:(q + 1) * 512], start=(s == 0), stop=(s == 127)),
                      r=rl, w=[pqr])
            for q in range(4):
                pq, pqr = PS[4 + q]
                V(lambda e: e.tensor_tensor(Xt[:T, q * 512:(q + 1) * 512], Xt[:T, q * 512:(q + 1) * 512], pq[:T, :], ALU.add),
                  r=[pqr, Xtr], w=[Xtr])
            for j_, (gb_, r_) in enumerate(gbufs):
                gr_ = Gb[1 + j_ % 4][1]
                for src_, c_ in ([r_.w] if r_.w else []) + list(r_.r.items()):
                    if gr_.r.get(src_, 0) < c_:
                        gr_.r[src_] = c_

        def ssm_out(l, dst_ap, dst_r):
            Sl, Slr = S[l]
            g0, g0r = Gb[0]
            so = g0[:, 0:2048].rearrange("p (c n) -> p c n", n=128)
            transp(Sl, Slr, 128, 16, None, so, g0r)
            kb.dma("sp", dst_ap.rearrange("(c p) n -> p c n", p=128), so, reads=[g0r], writes=[dst_r])

        r_yp, r_ys = ores("yp"), ores("ys")
        r_pw, r_ps, r_pc = ores("pw"), ores("pssm"), ores("pconv")
        r_sw, r_ss, r_scv = ores("sw"), ores("sssm"), ores("sconv")

        precast_all()
        if NPT > 0:
            for l in range(n_layers):
                mem_precompute(l)
                V(lambda e: e.memset(S[l][0][:, :], 0.0), w=[S[l][1]])
                V(lambda e: e.memset(CT[l][0][:, :, :], 0.0), w=[CT[l][1]])
            for ti in range(NPT):
                kb.dma("sp", X[:, :], xp[ti * 128:(ti + 1) * 128, :], writes=[Xr])
                for l in range(n_layers):
                    last = ti == NPT - 1
                    def wo(kv, kvr, l=l):
                        kb.dma("sp", pwk[l], kv[:, 512:768], reads=[kvr], writes=[r_pw])
                        kb.dma("sp", pwv[l], kv[:, 256:512], reads=[kvr], writes=[r_pw])
                    tile_layer(128, l, ti > 0, pmkT[l], pmkT_r[l], pmv[l], pmv_r[l],
                               pconv[l] if last else None, r_pc, wo if last else None)
                    if last:
                        ssm_out(l, pssm[l], r_ps)
                kb.dma("sp", yp[ti * 128:(ti + 1) * 128, :], X[:, :], reads=[Xr], writes=[r_yp])
        cres = Res("cin")
        TS = 8 * NSQ
        XBt, XBr = None, None
        for l in range(n_layers if NSQ > 0 else 0):
            XBt, XBr = S[1] if l == 0 else S[0]
            for sq in range(NSQ):
                if l == 0:
                    kb.dma("sp", X[0:8, :], xsm[sq * 8:(sq + 1) * 8, :], writes=[Xr])
                else:
                    kb.dma("sp", X[0:8, :], XBt[sq * 8:(sq + 1) * 8, :], reads=[XBr], writes=[Xr])
                kb.dma("sp", S[l][0][:, :], sst[l, sq], writes=[S[l][1]])
                kb.dma("sp", CT[l][0][:, :, :], scv[l, sq], writes=[CT[l][1]])
                kb.dma("sp", KTP[l][0][:, :, :], cwkT[l, sq], writes=[KTP[l][1]])
                kb.dma("sp", VP[l][0][:, :], cwv[l, sq], writes=[VP[l][1]])

                def wo(kv, kvr, l=l, sq=sq):
                    kb.dma("sp", swk[l, sq, 0:120, :], cwk[l, sq, 8:128, :], writes=[r_sw])
                    kb.dma("sp", swv[l, sq, 0:120, :], cwv[l, sq, 8:128, :], writes=[r_sw])
                    kb.dma("sp", swk[l, sq, 120:128, :], kv[0:8, 512:768], reads=[kvr], writes=[r_sw])
                    kb.dma("sp", swv[l, sq, 120:128, :], kv[0:8, 256:512], reads=[kvr], writes=[r_sw])
                tile_layer(8, l, True, cmkT[l, sq], cres, cmv[l, sq], cres, sconv[l, sq], r_scv, wo, run_peer=False)
                kb.dma("sp", XBt[sq * 8:(sq + 1) * 8, :], X[0:8, :], reads=[Xr], writes=[XBr])
                ssm_out(l, sssm[l, sq], r_ss)
            if do_peer:
                peer(TS, l, XBt, XBr)
            if l == 0 and n_layers > 1:
                V(lambda e: e.tensor_copy(S[0][0][0:TS, :], S[1][0][0:TS, :]), r=[S[1][1]], w=[S[0][1]])
        if NSQ > 0:
            kb.dma("sp", ys[0:TS, :], XBt[0:TS, :], reads=[XBr], writes=[r_ys])
        kb.wait_all("sp", out_res)
        build.ninst = kb.ninst
    return nc


def _consts():
    ident = np.eye(128, dtype=np.float32)
    j = np.arange(128)
    triu = (j[:, None] <= j[None, :]).astype(np.float32)
    slopes = np.exp2(-8.0 * np.arange(1, 33, dtype=np.float32) / 32).astype(np.float32)
    k = np.arange(128)[:, None].astype(np.float32)
    q = np.arange(128)[None, :].astype(np.float32)
    em = np.zeros((4, 128, 2, 8, 128), np.float32)
    for h in range(32):
        d0 = q + 128 - k
        d1 = q - k
        em[h // 8, :, 0, h % 8, :] = np.where((d0 >= 0) & (d0 <= 128), np.exp(-slopes[h] * d0), 0.0)
        em[h // 8, :, 1, h % 8, :] = np.where((d1 >= 0) & (d1 <= 128), np.exp(-slopes[h] * d1), 0.0)
    return ident, triu, em.reshape(4, 128, 2048)


def _col(v, n):
    return np.ascontiguousarray(np.asarray(v, np.float32).reshape(n, 128).T)


def make_in_maps(inp, NPT, NSQ, cores):
    f = lambda a: np.ascontiguousarray(np.asarray(a, dtype=np.float32))
    ident, triu, em = _consts()
    pcols = np.zeros((128, 400), np.float32)
    prow = np.zeros((128, 768), np.float32)
    for l in range(2):
        o = l * 200
        pcols[:, o:o + 16] = _col(inp["g_mix"][l], 16)
        pcols[:, o + 16:o + 32] = _col(inp["g_ffn"][l], 16)
        pcols[:, o + 32:o + 48] = _col(inp["g_mem"][l], 16)
        pcols[:, o + 48:o + 64] = _col(inp["g_ssd_norm"][l], 16)
        pcols[:, o + 64:o + 68] = _col(inp["g_qm"][l], 4)
        for k in range(4):
            pcols[:, o + 68 + k * 24:o + 68 + (k + 1) * 24] = _col(inp["conv_w"][l][k], 24)
        pcols[:, o + 164:o + 188] = _col(inp["conv_b"][l], 24)
        pcols[:64, o + 188] = np.asarray(inp["g_q"][l], np.float32)
        r = l * 384
        prow[:, r:r + 32] = np.asarray(inp["dt_bias"][l], np.float32)[None, :]
        prow[:, r + 32:r + 64] = np.asarray(inp["a_log"][l], np.float32)[None, :]
        prow[:, r + 64:r + 96] = np.asarray(inp["d_skip"][l], np.float32)[None, :]
        prow[:, r + 96:r + 128] = np.asarray(inp["attn_sinks"][l], np.float32)[None, :]
        prow[:, r + 128:r + 384] = np.tile(np.asarray(inp["g_k"][l], np.float32), 4)[None, :]
    gkm_bc = f(np.broadcast_to(np.tile(np.asarray(inp["g_km"], np.float32), (1, 4))[:, None, :], (2, 128, D)))
    gffn_bc = f(np.broadcast_to(np.asarray(inp["g_ffn"], np.float32)[:, None, :], (2, 128, D)))
    skT = f(np.asarray(inp["peer_sub_keys"], np.float32).reshape(2, 16, 128, 128).transpose(0, 3, 1, 2))
    shared = dict(
        w_in=f(inp["w_in"]), w_mem_kv=f(inp["w_mem_kv"]), w_o_ssm=f(inp["w_o_ssm"]), w_o_swa=f(inp["w_o_swa"]),
        w_o_mem=f(inp["w_o_mem"]), w_out=f(inp["w_out"]), w_peer_q=f(inp["w_peer_q"]), peer_u=f(inp["peer_u"]).reshape(2 * 16384, D),
        peer_v=f(inp["peer_v"]).reshape(2 * 16384, D), skT=skT, pcols=pcols, prow=prow, gkm_bc=gkm_bc, gffn_bc=gffn_bc,
        bgate=f(np.asarray(inp["b_gate"], np.float32).reshape(2, 1, 6144)), ident=ident, triu=triu, emask=em)
    maps = []
    nq = max(NSQ, 1)
    for (pb, s0) in cores:
        m = dict(shared)
        m["xp"] = f(np.asarray(inp["x_prompt"])[pb, :max(NPT, 1) * 128])
        m["memp"] = f(np.asarray(inp["mem_prompt"])[pb])
        sl = slice(s0, s0 + nq)
        m["xsm"] = f(np.asarray(inp["x_sample"])[sl].reshape(nq * 8, D))
        ck = np.asarray(inp["cache_win_k"], np.float32)[:, sl]
        m["cwk"] = f(ck.reshape(2, nq, 128, 256))
        m["cwkT"] = f(ck.transpose(0, 1, 4, 3, 2))
        m["cwv"] = f(np.asarray(inp["cache_win_v"], np.float32)[:, sl].reshape(2, nq, 128, 256))
        ssm = np.asarray(inp["state_ssm"], np.float32)[:, sl]
        m["sst"] = f(ssm.reshape(2, nq, 2048, 128).transpose(0, 1, 3, 2))
        cv = np.asarray(inp["state_conv"], np.float32)[:, sl]
        m["scv"] = f(cv.reshape(2, nq, 3, 24, 128).transpose(0, 1, 4, 3, 2))
        mk = np.asarray(inp["cache_mem_k"], np.float32)[:, sl].reshape(2, nq, 256, 16, 128)
        m["cmkT"] = f(mk.transpose(0, 1, 4, 3, 2))
        m["cmv"] = f(np.asarray(inp["cache_mem_v"], np.float32)[:, sl].reshape(2, nq, 256, D))
        maps.append(m)
    return maps


_NC_CACHE = {}


def kernel(**inputs):
    NPT, NSQ = 16, 4
    key = (NPT, NSQ)
    if key not in _NC_CACHE:
        _NC_CACHE[key] = build(NPT, NSQ)
    nc = _NC_CACHE[key]
    cores = [(c % 4, 4 * c) for c in range(8)]
    maps = make_in_maps(inputs, NPT, NSQ, cores)
    res = run_bass_kernel_spmd(nc, maps, core_ids=list(range(8))).results
    y_prompt = np.stack([res[b]["yp"] for b in range(4)]).astype(np.float32)
    y_sample = np.concatenate([res[c]["ys"].reshape(4, 8, D) for c in range(8)]).astype(np.float32)

    def pstack(name, shape):
        return np.stack([np.stack([res[b][name][l].reshape(shape) for b in range(4)]) for l in range(2)]).astype(np.float32)

    def sstack(name, shape):
        return np.stack([np.concatenate([res[c][name][l].reshape((4,) + shape) for c in range(8)]) for l in range(2)]).astype(np.float32)

    return (y_prompt, y_sample,
            pstack("pwk", (128, 4, 64)), pstack("pwv", (128, 4, 64)),
            pstack("pssm", (32, 64, 128)), pstack("pconv", (3, 3072)),
            pstack("pmk", (256, 4, 512)), pstack("pmv", (256, 4, 512)),
            sstack("swk", (128, 4, 64)), sstack("swv", (128, 4, 64)),
            sstack("sssm", (32, 64, 128)), sstack("sconv", (3, 3072)))
```

```python
import numpy as np
from contextlib import ExitStack
import concourse.bass as bass
import concourse.mybir as mybir
from concourse.bass_utils import run_bass_kernel_spmd

F32 = mybir.dt.float32
BF16 = mybir.dt.bfloat16
I32 = mybir.dt.int32
U32 = mybir.dt.uint32
ALU = mybir.AluOpType
AF = mybir.ActivationFunctionType
AX = mybir.AxisListType

SELF_SYNC = True
SEM_EPOCH = 30000
D = 2048
IN_DIM = 15904
OFF_Z, OFF_XBC, OFF_DT, OFF_Q, OFF_K, OFF_V, OFF_QM, OFF_G = 0, 2048, 5120, 5152, 7200, 7456, 7712, 9760
EPS = 1e-6
GSZ = 3200


class Res:
    __slots__ = ("name", "w", "r")

    def __init__(self, name):
        self.name = name
        self.w = None
        self.r = {}


class KB:
    def __init__(self, nc, stack, n_dma_sems=20):
        self.nc = nc
        self.stack = stack
        self.engs = {"pe": nc.tensor, "dve": nc.vector, "act": nc.scalar,
                     "pool": nc.gpsimd, "sp": nc.sync}
        self.sem = {}
        self.semidx = {}
        self.cnt = {}
        for k in self.engs:
            self._new_sem(k)
        self.waited = {k: {} for k in self.engs}
        self.dma_sems = [stack.enter_context(nc.semaphore("dq%d" % i)) for i in range(n_dma_sems)]
        self.dma_n = 0
        self.dma_tgt = [0] * n_dma_sems
        self.ninst = 0

    def _new_sem(self, k):
        i = self.semidx.get(k, -1) + 1
        self.semidx[k] = i
        self.sem[(k, i)] = self.stack.enter_context(self.nc.semaphore("s_%s_%d" % (k, i)))
        self.cnt[k] = 0

    def _semof(self, src):
        if src[0] == "e":
            return self.sem[(src[1], src[2])]
        return self.dma_sems[src[1]]

    def _collect(self, reads, writes):
        deps = {}

        def add(s, c):
            if deps.get(s, 0) < c:
                deps[s] = c
        for r in reads:
            if r.w:
                add(*r.w)
        for w in writes:
            if w.w:
                add(*w.w)
            for s, c in w.r.items():
                add(s, c)
        return deps

    def _emit_waits(self, eng, deps):
        e = self.engs[eng]
        for s, c in deps.items():
            if s[0] == "e" and s[1] == eng:
                if eng in ("pe", "sp") or not SELF_SYNC:
                    continue
            if self.waited[eng].get(s, 0) >= c:
                continue
            e.wait_ge(self._semof(s), c)
            self.waited[eng][s] = c

    def op(self, eng, emit, reads=(), writes=()):
        deps = self._collect(reads, writes)
        self._emit_waits(eng, deps)
        ins = emit(self.engs[eng])
        if self.cnt[eng] >= SEM_EPOCH:
            self._new_sem(eng)
        self.cnt[eng] += 1
        key = ("e", eng, self.semidx[eng])
        ins.then_inc(self.sem[(eng, self.semidx[eng])], 1)
        c = self.cnt[eng]
        for r in reads:
            r.r[key] = c
        for w in writes:
            w.w = (key, c)
            w.r = {}
        self.ninst += 1
        return ins

    def dma(self, q, out, in_, reads=(), writes=(), indirect=None, **kw):
        deps = self._collect(reads, writes)
        slot = self.dma_n % len(self.dma_sems)
        self.dma_n += 1
        if self.dma_tgt[slot] > 0:
            deps[("d", slot)] = max(deps.get(("d", slot), 0), self.dma_tgt[slot])
        self._emit_waits(q, deps)
        e = self.engs[q]
        if indirect is not None:
            ins = e.indirect_dma_start(out, None, in_, indirect, **kw)
        else:
            ins = e.dma_start(out, in_, **kw)
        self.dma_tgt[slot] += 16
        ins.then_inc(self.dma_sems[slot], 16)
        key = ("d", slot)
        c = self.dma_tgt[slot]
        for r in reads:
            r.r[key] = c
        for w in writes:
            w.w = (key, c)
            w.r = {}
        self.ninst += 1
        return ins

    def dma_barrier(self, eng):
        e = self.engs[eng]
        for i, t in enumerate(self.dma_tgt):
            if t > 0 and self.waited[eng].get(("d", i), 0) < t:
                e.wait_ge(self.dma_sems[i], t)
                self.waited[eng][("d", i)] = t

    def wait_all(self, eng, ress):
        deps = {}
        for r in ress:
            if r.w and deps.get(r.w[0], 0) < r.w[1]:
                deps[r.w[0]] = r.w[1]
            for s, c in r.r.items():
                if deps.get(s, 0) < c:
                    deps[s] = c
        e = self.engs[eng]
        for s, c in deps.items():
            e.wait_ge(self._semof(s), c)


def build(NPT, NSQ, n_layers=2, do_peer=True):
    nc = bass.Bass("TRN2", target_bir_lowering=False)

    def din(name, shape, dt=F32):
        return nc.dram_tensor(name, list(shape), dt, kind="ExternalInput").ap()

    def dout(name, shape):
        return nc.dram_tensor(name, list(shape), F32, kind="ExternalOutput").ap()

    xp = din("xp", [max(NPT, 1) * 128, D])
    xsm = din("xsm", [max(NSQ, 1) * 8, D])
    cwk = din("cwk", [2, max(NSQ, 1), 128, 256])
    cwv = din("cwv", [2, max(NSQ, 1), 128, 256])
    cwkT = din("cwkT", [2, max(NSQ, 1), 64, 4, 128])
    sst = din("sst", [2, max(NSQ, 1), 128, 2048])
    scv = din("scv", [2, max(NSQ, 1), 128, 24, 3])
    cmkT = din("cmkT", [2, max(NSQ, 1), 128, 16, 256])
    cmv = din("cmv", [2, max(NSQ, 1), 256, 2048])
    memp = din("memp", [256, D])
    w_in = din("w_in", [2, D, IN_DIM])
    w_mem_kv = din("w_mem_kv", [2, D, 4096])
    w_o_ssm = din("w_o_ssm", [2, D, D])
    w_o_swa = din("w_o_swa", [2, D, D])
    w_o_mem = din("w_o_mem", [2, D, D])
    w_out = din("w_out", [2, D, D])
    w_peer_q = din("w_peer_q", [2, D, D])
    peer_u = din("peer_u", [2 * 16384, D])
    peer_v = din("peer_v", [2 * 16384, D])
    skT_d = din("skT", [2, 128, 16, 128])
    NPC = 2 * 200
    pcols_d = din("pcols", [128, NPC])
    NPR = 2 * 384
    prow_d = din("prow", [128, NPR])
    gkm_bc = din("gkm_bc", [2, 128, D])
    gffn_bc = din("gffn_bc", [2, 128, D])
    bgate = din("bgate", [2, 1, 6144])
    ident_d = din("ident", [128, 128])
    triu_d = din("triu", [128, 128])
    emask_d = din("emask", [4, 128, 2048])

    yp = dout("yp", [max(NPT, 1) * 128, D])
    ys = dout("ys", [max(NSQ, 1) * 8, D])
    pwk = dout("pwk", [2, 128, 256])
    pwv = dout("pwv", [2, 128, 256])
    pssm = dout("pssm", [2, 2048, 128])
    pconv = dout("pconv", [2, 3, 3072])
    pmk = dout("pmk", [2, 256, D])
    pmv = dout("pmv", [2, 256, D])
    swk = dout("swk", [2, max(NSQ, 1), 128, 256])
    swv = dout("swv", [2, max(NSQ, 1), 128, 256])
    sssm = dout("sssm", [2, max(NSQ, 1), 2048, 128])
    sconv = dout("sconv", [2, max(NSQ, 1), 3, 3072])
    pmkT = nc.dram_tensor("pmkT", [2, 128, 16, 256], F32, kind="Internal").ap()
    NBLK = 52
    wsc = nc.dram_tensor("wsc", [2 * NBLK, 128, 8192], BF16, kind="Internal").ap()
    usc = nc.dram_tensor("usc", [2 * 16384, D], BF16, kind="Internal").ap()
    vsc = nc.dram_tensor("vsc", [2 * 16384, D], BF16, kind="Internal").ap()
    BID = {"in": 0, "dt": 31, "ossm": 32, "oswa": 36, "omem": 44, "out": 48}

    out_res = []

    def ores(name):
        r = Res(name)
        out_res.append(r)
        return r

    with ExitStack() as st:
        kb = KB(nc, st)

        def sb(name, shape, dt=F32):
            t = st.enter_context(nc.sbuf_tensor(name, list(shape), dt))
            return t, Res(name)

        def V(fn, r=(), w=()):
            return kb.op("dve", fn, reads=r, writes=w)

        def A(fn, r=(), w=()):
            return kb.op("act", fn, reads=r, writes=w)

        def P(fn, r=(), w=()):
            return kb.op("pe", fn, reads=r, writes=w)

        def G(fn, r=(), w=()):
            return kb.op("pool", fn, reads=r, writes=w)

        X, Xr = sb("X", [128, D])
        XN, XNr = sb("XN", [128, D])
        HT, HTr = sb("HT", [128, 16, 128])
        HTb = HT[:, 0:8, :].rearrange("p c t -> p (c t)").bitcast(BF16).rearrange("p (c t) -> p c t", t=128)
        MG, MGr = sb("MG", [128, D])
        WB = [sb("WB%d" % i, [128, 4096]) for i in range(4)]
        Gb = [sb("G%d" % i, [128, GSZ if i == 1 else 3072]) for i in range(6)]
        S = [sb("S%d" % l, [128, D]) for l in range(2)]
        CT = [sb("CT%d" % l, [128, 24, 3]) for l in range(2)]
        KTP = [sb("KTP%d" % l, [64, 4, 128]) for l in range(2)]
        VP = [sb("VP%d" % l, [128, 256]) for l in range(2)]
        KTC, KTCr = sb("KTC", [64, 4, 128])
        ident, identr = sb("ident_s", [128, 128])
        triu, triur = sb("triu_s", [128, 128])
        ones, onesr = sb("ones_s", [128, 128])
        pcols, pcolsr = sb("pcols_s", [128, NPC])
        prow, prowr = sb("prow_s", [128, NPR])
        epst, epstr = sb("epst", [128, 1])
        onec, onecr = sb("onec", [128, 1])
        iot16, iot16r = sb("iot16", [128, 16])
        SM = {}
        for nm, w_ in [("ss", 40), ("rs", 40), ("dt", 32), ("dA", 32), ("cum", 32), ("ncum", 32), ("ecum", 32),
                       ("cl", 32), ("wend", 32), ("A", 32), ("esink", 32), ("bg", 512), ("mx", 8), ("sm", 8),
                       ("vtop", 256), ("itopf", 256), ("t8", 8)]:
            SM[nm] = sb("sm_" + nm, [128, w_])
        itop_u, itop_ur = sb("itop_u", [128, 256], U32)
        pos_u, pos_ur = sb("pos_u", [128, 128], U32)
        idx_i, idx_ir = sb("idx_i", [128, 128], I32)
        PS = [(st.enter_context(nc.psum_tensor("ps%d" % i, [128, 512], F32)), Res("ps%d" % i)) for i in range(8)]

        pmv_r = [Res("pmv%d" % l) for l in range(2)]
        pmkT_r = [Res("pmkT%d" % l) for l in range(2)]
        ores_pmk = ores("pmk")

        kb.dma("sp", ident[:], ident_d, writes=[identr])
        kb.dma("sp", triu[:], triu_d, writes=[triur])
        kb.dma("sp", pcols[:], pcols_d, writes=[pcolsr])
        kb.dma("sp", prow[:], prow_d, writes=[prowr])
        G(lambda e: e.memset(ones[:], 1.0), w=[onesr])
        G(lambda e: e.memset(epst[:], EPS), w=[epstr])
        G(lambda e: e.memset(onec[:], 1.0), w=[onecr])
        G(lambda e: e.iota(iot16[:], [[1, 16]], base=0, channel_multiplier=0, allow_small_or_imprecise_dtypes=True),
          w=[iot16r])

        def pc(l, off, n=1):
            return pcols[:, l * 200 + off: l * 200 + off + n]

        def pr(l, off, n):
            return prow[:, l * 384 + off: l * 384 + off + n]

        wctr = [0]

        def proj(lhs_fn, nk, kp, wsrc, ncols, T, ps_ap, ps_res, lhs_res, extra=None):
            for c0 in range(0, ncols, 256):
                n_ = min(256, ncols - c0)
                i = wctr[0] % 4
                wctr[0] += 1
                wt, wr = WB[i]
                wv = wt[0:kp, 0:nk * n_].rearrange("p (c n) -> p c n", n=n_)
                kb.dma("sp", wv, wsrc[:, c0:c0 + n_].rearrange("(c p) n -> p c n", p=kp), writes=[wr])
                for c in range(nk):
                    P(lambda e: e.matmul(ps_ap[:, c0:c0 + n_], lhs_fn(c), wv[:, c, :], start=(c == 0), stop=(c == nk - 1)),
                      r=[lhs_res, wr], w=[ps_res])

        def projb(lhs_fn, nk, kp, bid, ncols, T, ps_ap, ps_res, lhs_res, extra=None):
            i = wctr[0] % 4
            wctr[0] += 1
            wt, wr = WB[i]
            wv = wt[0:kp, :].bitcast(BF16)[:, 0:nk * ncols].rearrange("p (c n) -> p c n", n=ncols)
            kb.dma("sp", wt[0:kp, :].bitcast(BF16)[:, 0:nk * ncols], wsc[bid][0:kp, 0:nk * ncols], writes=[wr])
            for c in range(nk):
                last = (c == nk - 1) and extra is None
                P(lambda e: e.matmul(ps_ap, lhs_fn(c), wv[:, c, :], start=(c == 0), stop=last),
                  r=[lhs_res, wr], w=[ps_res])
            if extra is not None:
                P(lambda e: e.matmul(ps_ap, extra[0], extra[1], start=False, stop=True),
                  r=extra[2], w=[ps_res])

        def precast_all():
            jobs = []
            for l in range(n_layers):
                base = l * NBLK

                def std(src, c0, bid):
                    us = []
                    for hf in range(2):
                        v = src[hf * 1024:(hf + 1) * 1024, c0:c0 + 512].rearrange("(c p) n -> p c n", p=128)
                        us.append((v, 8, 512, hf * 4096))
                    jobs.append((128, us, ("w", bid, 8192)))
                cols = [OFF_Z + i * 512 for i in range(4)] + [OFF_XBC + i * 512 for i in range(6)] + \
                       [OFF_Q + i * 512 for i in range(4)] + [OFF_K] + [OFF_QM + i * 512 for i in range(4)] + \
                       [OFF_G + i * 512 for i in range(12)]
                for i, c0 in enumerate(cols):
                    std(w_in[l], c0, base + BID["in"] + i)
                jobs.append((128, [(w_in[l][:, OFF_DT:OFF_DT + 32].rearrange("(c p) n -> p c n", p=128), 16, 32, 0)],
                             ("w", base + BID["dt"], 512)))
                for nm, wm in (("ossm", w_o_ssm), ("omem", w_o_mem), ("out", w_out)):
                    for i in range(4):
                        std(wm[l], i * 512, base + BID[nm] + i)
                for i in range(8):
                    us = []
                    for hf in range(2):
                        v = w_o_swa[l][hf * 1024:(hf + 1) * 1024, i * 256:(i + 1) * 256].rearrange("(c p) n -> p c n", p=64)
                        us.append((v, 16, 256, hf * 4096))
                    jobs.append((64, us, ("w", base + BID["oswa"] + i, 8192)))
            if do_peer:
                for (src, dst) in ((peer_u, usc), (peer_v, vsc)):
                    for blk in range(64 * n_layers):
                        r0 = blk * 256
                        jobs.append((128, [(src[r0:r0 + 256, :].rearrange("(a p) d -> p a d", p=128), 2, D, 0)],
                                     ("t", dst[r0:r0 + 256, :].rearrange("(a p) d -> p a d", p=128), 4096)))
            units = []
            for ji, (kp, us, dst) in enumerate(jobs):
                for ui, u in enumerate(us):
                    units.append((ji, kp, u, ui == len(us) - 1, dst))

            def issue_in(k):
                ji, kp, (src, a_, n_, off), last, dst = units[k]
                wt, wr = WB[k % 2]
                kb.dma("sp", wt[0:kp, 0:a_ * n_].rearrange("p (c n) -> p c n", n=n_), src, writes=[wr])
            if units:
                issue_in(0)
            for k in range(len(units)):
                if k + 1 < len(units):
                    issue_in(k + 1)
                ji, kp, (src, a_, n_, off), last, dst = units[k]
                wt, wr = WB[k % 2]
                stg, stgr = WB[2 + ji % 2]
                stb = stg[0:kp, :].bitcast(BF16)
                dstv = stb[:, off:off + a_ * n_]
                srcv = wt[0:kp, 0:a_ * n_]
                e_ = k % 3
                if e_ == 0:
                    V(lambda e: e.tensor_copy(dstv, srcv), r=[wr], w=[stgr])
                elif e_ == 1:
                    A(lambda e: e.copy(dstv, srcv), r=[wr], w=[stgr])
                else:
                    G(lambda e: e.tensor_copy(dstv, srcv), r=[wr], w=[stgr])
                if last:
                    if dst[0] == "w":
                        kb.dma("sp", wsc[dst[1]][0:kp, 0:dst[2]], stb[:, 0:dst[2]], reads=[stgr], writes=[Res("wsc_tmp")])
                    else:
                        kb.dma("sp", dst[1], stb[:, 0:dst[2]].rearrange("p (a d) -> p a d", d=D), reads=[stgr],
                               writes=[Res("tsc_tmp")])
            kb.dma_barrier("sp")

        INB = {"z": 0, "xbc": 4, "q": 10, "kv": 14, "qm": 15, "g": 19}

        def rms_rstd(ss_ap, rs_ap, n, T, ssr, rsr):
            A(lambda e: e.activation(rs_ap, ss_ap, AF.Sqrt, bias=epst[:T, :], scale=1.0 / n), r=[ssr, epstr], w=[rsr])
            V(lambda e: e.reciprocal(rs_ap, rs_ap), r=[rsr], w=[rsr])

        def norm_T(src, srcr, T, gcol_fn, dst, dstr, scale=None):
            ss, ssr = SM["ss"]
            rs, rsr = SM["rs"]
            V(lambda e: e.memset(ss[:T, 0:1], 0.0), w=[ssr])
            A(lambda e: e.activation(XN[:T, :], src[:T, :], AF.Square, accum_out=ss[:T, 0:1]), r=[srcr], w=[XNr, ssr])
            rms_rstd(ss[:T, 0:1], rs[:T, 0:1], D, T, ssr, rsr)
            A(lambda e: e.activation(XN[:T, :], src[:T, :], AF.Copy, scale=rs[:T, 0:1]), r=[srcr, rsr], w=[XNr])
            transp(XN, XNr, T, 16, gcol_fn, dst, dstr)

        tctr = [0]

        def transp(src, srcr, T, nch, gcol_fn, dst, dstr, width=128, src_off=0):
            for c in range(nch):
                b = tctr[0] % 2
                tctr[0] += 1
                pt, ptr = PS[b]
                P(lambda e: e.transpose(pt[:width, 0:T], src[:T, src_off + c * width: src_off + (c + 1) * width],
                                        ident[:T, :T]), r=[srcr, identr], w=[ptr])
                if gcol_fn is None:
                    if c % 2 == 0:
                        V(lambda e: e.tensor_copy(dst[:width, c, 0:T], pt[:width, 0:T]), r=[ptr], w=[dstr])
                    else:
                        A(lambda e: e.copy(dst[:width, c, 0:T], pt[:width, 0:T]), r=[ptr], w=[dstr])
                else:
                    V(lambda e: e.tensor_scalar(dst[:width, c, 0:T], pt[:width, 0:T], gcol_fn(c), None, ALU.mult),
                      r=[ptr, pcolsr], w=[dstr])

        def mem_precompute(l):
            g0, g0r = Gb[0]
            g1, g1r = Gb[1]
            g2, g2r = Gb[2]
            kT = g2[:, 0:2048].rearrange("p (c t) -> p c t", t=128)
            kb.dma("sp", g1[:, 0:D], gkm_bc[l], writes=[g1r])
            for mt in range(2):
                kb.dma("sp", X[:, :], memp[mt * 128:(mt + 1) * 128, :], writes=[Xr])
                norm_T(X, Xr, 128, lambda c: pc(l, 32 + c), HT, HTr)
                for blk in range(8):
                    pt, ptr = PS[2 + blk % 2]
                    proj(lambda c: HT[:, c, :], 16, 128, w_mem_kv[l][:, blk * 512:(blk + 1) * 512], 512, 128,
                         pt[:, :], ptr, HTr)
                    if blk < 4:
                        ss, ssr = SM["ss"]
                        rs, rsr = SM["rs"]
                        V(lambda e: e.memset(ss[:, 1:2], 0.0), w=[ssr])
                        A(lambda e: e.activation(g0[:, blk * 512:(blk + 1) * 512], pt[:, :], AF.Square,
                                                 accum_out=ss[:, 1:2]), r=[ptr], w=[g0r, ssr])
                        rms_rstd(ss[:, 1:2], rs[:, 1:2], 512, 128, ssr, rsr)
                        V(lambda e: e.scalar_tensor_tensor(g0[:, blk * 512:(blk + 1) * 512], pt[:, :], rs[:, 1:2],
                                                           g1[:, blk * 512:(blk + 1) * 512], ALU.mult, ALU.mult),
                          r=[ptr, rsr, g1r], w=[g0r])
                    else:
                        A(lambda e: e.copy(MG[:, (blk - 4) * 512:(blk - 3) * 512], pt[:, :]), r=[ptr], w=[MGr])
                kb.dma("sp", pmk[l][mt * 128:(mt + 1) * 128, :], g0[:, 0:D], reads=[g0r], writes=[ores_pmk])
                kb.dma("sp", pmv[l][mt * 128:(mt + 1) * 128, :], MG[:, :], reads=[MGr], writes=[pmv_r[l]])
                transp(g0, g0r, 128, 16, None, kT, g2r)
                kb.dma("sp", pmkT[l][:, :, mt * 128:(mt + 1) * 128], kT, reads=[g2r], writes=[pmkT_r[l]])

        def tile_layer(T, l, has_prev, memK_ap, memK_r, memV_ap, memV_r, conv_out_ap, conv_out_r, win_out=None, run_peer=True):
            Sl, Slr = S[l]
            CTl, CTlr = CT[l]
            KTPl, KTPlr = KTP[l]
            VPl, VPlr = VP[l]
            ss, ssr = SM["ss"]
            rs, rsr = SM["rs"]
            g0, g0r = Gb[0]
            g1, g1r = Gb[1]
            g2, g2r = Gb[2]
            g3, g3r = Gb[3]
            g4, g4r = Gb[4]
            g5, g5r = Gb[5]
            bg, bgr = SM["bg"]

            wb0 = l * NBLK
            norm_T(X, Xr, T, lambda c: pc(l, c), HTb, HTr)

            def gate_merge(br, blk, br_ps, br_psr, first):
                pg, pgr = PS[4 + blk % 2]
                col = OFF_G + br * 2048 + blk * 512
                kb.dma("sp", bg[0:1, :], bgate[l][:, br * 2048 + blk * 512: br * 2048 + (blk + 1) * 512], writes=[bgr])
                projb(lambda c: HTb[:, c, :T], 16, 128, wb0 + BID["in"] + INB["g"] + br * 4 + blk, 512, T, pg[:T, :], pgr, HTr,
                      extra=(ones[0:1, 0:T], bg[0:1, 0:512], [onesr, bgr]))
                gs = g1[:T, 2600:3112]
                A(lambda e: e.activation(gs, pg[:T, :], AF.Sigmoid), r=[pgr], w=[g1r])
                mgs = MG[:T, blk * 512:(blk + 1) * 512]
                if first:
                    V(lambda e: e.tensor_tensor(mgs, gs, br_ps, ALU.mult), r=[g1r, br_psr], w=[MGr])
                else:
                    V(lambda e: e.tensor_tensor(gs, gs, br_ps, ALU.mult), r=[g1r, br_psr], w=[g1r])
                    V(lambda e: e.tensor_tensor(mgs, mgs, gs, ALU.add), r=[g1r, MGr], w=[MGr])

            def out_branch(br, lhs_fn, nk, kp, bid0, lhs_res, ncols):
                for blk in range(4):
                    pb, pbr = PS[6 + blk % 2]
                    projb(lhs_fn, nk, kp, bid0 + blk, 512, T, pb[:T, :], pbr, lhs_res)
                    gate_merge(br, blk, pb[:T, :], pbr, br == 0)

            xraw = g0
            for blk in range(6):
                pt, ptr = PS[2 + blk % 2]
                projb(lambda c: HTb[:, c, :T], 16, 128, wb0 + BID["in"] + INB["xbc"] + blk, 512, T,
                      pt[:T, :], ptr, HTr)
                if blk % 2 == 0:
                    A(lambda e: e.copy(xraw[:T, blk * 512:(blk + 1) * 512], pt[:T, :]), r=[ptr], w=[g0r])
                else:
                    V(lambda e: e.tensor_copy(xraw[:T, blk * 512:(blk + 1) * 512], pt[:T, :]), r=[ptr], w=[g0r])
            if conv_out_ap is not None:
                kb.dma("sp", conv_out_ap, xraw[T - 3:T, 0:3072], reads=[g0r], writes=[conv_out_r])
            xbcT = g1[:, 0:24 * 131].rearrange("p (c t) -> p c t", t=131)
            V(lambda e: e.tensor_copy(xbcT[:, :, 0:3], CTl[:, :, :]), r=[CTlr], w=[g1r])
            for c in range(24):
                b = tctr[0] % 2
                tctr[0] += 1
                pt, ptr = PS[b]
                P(lambda e: e.transpose(pt[:, 0:T], xraw[:T, c * 128:(c + 1) * 128], ident[:T, :T]),
                  r=[g0r, identr], w=[ptr])
                if c % 2 == 0:
                    V(lambda e: e.tensor_copy(xbcT[:, c, 3:3 + T], pt[:, 0:T]), r=[ptr], w=[g1r])
                else:
                    A(lambda e: e.copy(xbcT[:, c, 3:3 + T], pt[:, 0:T]), r=[ptr], w=[g1r])
            V(lambda e: e.tensor_copy(CTl[:, :, :], xbcT[:, :, T:T + 3]), r=[g1r], w=[CTlr])
            xact = g2[:, 0:24 * 128].rearrange("p (c t) -> p c t", t=128)
            for c in range(24):
                V(lambda e: e.tensor_scalar(xact[:, c, 0:T], xbcT[:, c, 0:T], pc(l, 68 + c), pc(l, 164 + c),
                                            ALU.mult, ALU.add), r=[g1r, pcolsr], w=[g2r])
                for k in range(1, 4):
                    V(lambda e: e.scalar_tensor_tensor(xact[:, c, 0:T], xbcT[:, c, k:k + T], pc(l, 68 + k * 24 + c),
                                                       xact[:, c, 0:T], ALU.mult, ALU.add),
                      r=[g1r, g2r, pcolsr], w=[g2r])
            A(lambda e: e.activation(xact[:, :, 0:T], xact[:, :, 0:T], AF.Silu), r=[g2r], w=[g2r])
            xs = g3
            xs3 = g3[:, 0:2048].rearrange("p (c t) -> p c t", t=128)
            transp_fm(xact, g2r, T, 0, 16, xs3, g3r)
            bt3 = g4[:, 0:512].rearrange("p (c t) -> p c t", t=128)
            transp_fm(xact, g2r, T, 16, 4, bt3, g4r)
            dt, dtr = SM["dt"]
            dA, dAr = SM["dA"]
            Aneg, Anegr = SM["A"]
            pt, ptr = PS[2]
            projb(lambda c: HTb[:, c, :T], 16, 128, wb0 + BID["dt"], 32, T, pt[:T, 0:32], ptr, HTr)
            V(lambda e: e.tensor_tensor(dt[:T, :], pt[:T, 0:32], pr(l, 0, 32)[:T, :], ALU.add), r=[ptr, prowr], w=[dtr])
            A(lambda e: e.activation(dt[:T, :], dt[:T, :], AF.Exp), r=[dtr], w=[dtr])
            A(lambda e: e.activation(dt[:T, :], dt[:T, :], AF.Ln, bias=onec[:T, :], scale=1.0), r=[dtr, onecr], w=[dtr])
            A(lambda e: e.activation(Aneg[:, :], pr(l, 32, 32), AF.Exp), r=[prowr], w=[Anegr])
            V(lambda e: e.scalar_tensor_tensor(dA[:T, :], dt[:T, :], -1.0, Aneg[:T, :], ALU.mult, ALU.mult),
              r=[dtr, Anegr], w=[dAr])
            cum, cumr = SM["cum"]
            cl, clr = SM["cl"]
            ecum, ecumr = SM["ecum"]
            wend, wendr = SM["wend"]
            pt, ptr = PS[3]
            P(lambda e: e.matmul(pt[:T, 0:32], triu[:T, :T], dA[:T, :], start=True, stop=True), r=[triur, dAr], w=[ptr])
            P(lambda e: e.matmul(pt[:, 32:64], ones[:T, :], dA[:T, :], start=True, stop=True), r=[onesr, dAr], w=[ptr])
            V(lambda e: e.tensor_copy(cum[:T, :], pt[:T, 0:32]), r=[ptr], w=[cumr])
            V(lambda e: e.tensor_copy(cl[:, :], pt[:, 32:64]), r=[ptr], w=[clr])
            A(lambda e: e.activation(ecum[:T, :], cum[:T, :], AF.Exp), r=[cumr], w=[ecumr])
            V(lambda e: e.tensor_tensor(wend[:T, :], cl[:T, :], cum[:T, :], ALU.subtract), r=[clr, cumr], w=[wendr])
            A(lambda e: e.activation(wend[:T, :], wend[:T, :], AF.Exp), r=[wendr], w=[wendr])
            V(lambda e: e.tensor_tensor(wend[:T, :], wend[:T, :], dt[:T, :], ALU.mult), r=[wendr, dtr], w=[wendr])
            cbm = g4[:, 512:1024].rearrange("p (c t) -> p c t", t=128)
            pt, ptr = PS[2]
            for gq in range(4):
                P(lambda e: e.matmul(pt[:T, gq * 128: gq * 128 + T], xact[:, 16 + gq, 0:T], xact[:, 20 + gq, 0:T],
                                     start=True, stop=True), r=[g2r], w=[ptr])
            for gq in range(4):
                V(lambda e: e.tensor_tensor(cbm[:T, gq, 0:T], pt[:T, gq * 128: gq * 128 + T], triu[:T, :T], ALU.mult),
                  r=[ptr, triur], w=[g4r])
            ysb = g5
            for hf in range(2):
                Rp = g0[:, 0:2048].rearrange("p (h t) -> p h t", t=128)
                for hh in range(16):
                    h = hf * 16 + hh
                    V(lambda e: e.tensor_scalar(Rp[:T, hh, 0:T], triu[:T, :T], dA[:T, h:h + 1], None, ALU.mult),
                      r=[triur, dAr], w=[g0r])
                MT = g1[:, 0:2048].rearrange("p (h t) -> p h t", t=128)
                for q4 in range(4):
                    pt, ptr = PS[4 + q4]
                    for hq in range(4):
                        hh = q4 * 4 + hq
                        P(lambda e: e.matmul(pt[:T, hq * 128: hq * 128 + T], ones[:T, :T], Rp[:T, hh, 0:T],
                                             start=True, stop=True), r=[onesr, g0r], w=[ptr])
                    for hq in range(4):
                        hh = q4 * 4 + hq
                        h = hf * 16 + hh
                        V(lambda e: e.tensor_scalar(MT[:T, hh, 0:T], pt[:T, hq * 128: hq * 128 + T], cum[:T, h:h + 1], 0.0,
                                                    ALU.subtract, ALU.min), r=[ptr, cumr], w=[g1r])
                A(lambda e: e.activation(MT[:T, :, 0:T], MT[:T, :, 0:T], AF.Exp), r=[g1r], w=[g1r])
                for hh in range(16):
                    h = hf * 16 + hh
                    V(lambda e: e.scalar_tensor_tensor(MT[:T, hh, 0:T], MT[:T, hh, 0:T], dt[:T, h:h + 1],
                                                       cbm[:T, h // 8, 0:T], ALU.mult, ALU.mult),
                      r=[g1r, dtr, g4r], w=[g1r])
                for hh in range(16):
                    h = hf * 16 + hh
                    pt, ptr = PS[hh // 8]
                    P(lambda e: e.matmul(pt[:T, (hh % 8) * 64:(hh % 8 + 1) * 64], MT[:T, hh, 0:T],
                                         xs[:T, h * 64:(h + 1) * 64], start=True, stop=True), r=[g1r, g3r], w=[ptr])
                for gq in range(2):
                    gg = hf * 2 + gq
                    pt, ptr = PS[2 + gq]
                    P(lambda e: e.matmul(pt[:T, :], xact[:, 20 + gg, 0:T], Sl[:, gg * 512:(gg + 1) * 512],
                                         start=True, stop=True), r=[g2r, Slr], w=[ptr])
                for gq in range(2):
                    gg = hf * 2 + gq
                    pi, pir = PS[gq]
                    pst, pstr = PS[2 + gq]
                    yv = ysb[:T, gg * 512:(gg + 1) * 512].rearrange("p (h d) -> p h d", d=64)
                    V(lambda e: e.tensor_tensor(yv, pst[:T, :].rearrange("p (h d) -> p h d", d=64),
                                                ecum[:T, gg * 8:(gg + 1) * 8].unsqueeze(2).to_broadcast([T, 8, 64]),
                                                ALU.mult), r=[pstr, ecumr], w=[g5r])
                    V(lambda e: e.tensor_tensor(ysb[:T, gg * 512:(gg + 1) * 512], ysb[:T, gg * 512:(gg + 1) * 512],
                                                pi[:T, :], ALU.add), r=[pir, g5r], w=[g5r])
            xw = g0
            V(lambda e: e.tensor_tensor(xw[:T, 0:2048].rearrange("p (h d) -> p h d", d=64),
                                        xs[:T, 0:2048].rearrange("p (h d) -> p h d", d=64),
                                        pr(l, 64, 32)[:T, :].unsqueeze(2).to_broadcast([T, 32, 64]), ALU.mult),
              r=[g3r, prowr], w=[g0r])
            V(lambda e: e.tensor_tensor(ysb[:T, 0:2048], ysb[:T, 0:2048], xw[:T, 0:2048], ALU.add), r=[g0r, g5r], w=[g5r])
            V(lambda e: e.tensor_tensor(xw[:T, 0:2048].rearrange("p (h d) -> p h d", d=64),
                                        xs[:T, 0:2048].rearrange("p (h d) -> p h d", d=64),
                                        wend[:T, :].unsqueeze(2).to_broadcast([T, 32, 64]), ALU.mult),
              r=[g3r, wendr], w=[g0r])
            A(lambda e: e.activation(cl[:, :], cl[:, :], AF.Exp), r=[clr], w=[clr])
            for gg in range(4):
                pt, ptr = PS[4 + gg]
                P(lambda e: e.matmul(pt[:, :], bt3[:T, gg, :], xw[:T, gg * 512:(gg + 1) * 512], start=True, stop=True),
                  r=[g4r, g0r], w=[ptr])
            V(lambda e: e.tensor_tensor(Sl[:, :].rearrange("p (h d) -> p h d", d=64),
                                        Sl[:, :].rearrange("p (h d) -> p h d", d=64),
                                        cl[:, :].unsqueeze(2).to_broadcast([128, 32, 64]), ALU.mult),
              r=[Slr, clr], w=[Slr])
            for gg in range(4):
                pt, ptr = PS[4 + gg]
                V(lambda e: e.tensor_tensor(Sl[:, gg * 512:(gg + 1) * 512], Sl[:, gg * 512:(gg + 1) * 512], pt[:, :],
                                            ALU.add), r=[Slr, ptr], w=[Slr])
            for blk in range(4):
                pt, ptr = PS[blk % 2]
                projb(lambda c: HTb[:, c, :T], 16, 128, wb0 + BID["in"] + INB["z"] + blk, 512, T,
                      pt[:T, :], ptr, HTr)
                zs = g1[:T, 0:512]
                A(lambda e: e.activation(zs, pt[:T, :], AF.Silu), r=[ptr], w=[g1r])
                V(lambda e: e.tensor_tensor(ysb[:T, blk * 512:(blk + 1) * 512], ysb[:T, blk * 512:(blk + 1) * 512], zs,
                                            ALU.mult), r=[g1r, g5r], w=[g5r])
                V(lambda e: e.memset(ss[:T, 4 + blk:5 + blk], 0.0), w=[ssr])
                A(lambda e: e.activation(zs, ysb[:T, blk * 512:(blk + 1) * 512], AF.Square,
                                         accum_out=ss[:T, 4 + blk:5 + blk]), r=[g5r], w=[g1r, ssr])
            rms_rstd(ss[:T, 4:8], rs[:T, 4:8], 512, T, ssr, rsr)
            V(lambda e: e.tensor_tensor(ysb[:T, 0:2048].rearrange("p (g d) -> p g d", d=512),
                                        ysb[:T, 0:2048].rearrange("p (g d) -> p g d", d=512),
                                        rs[:T, 4:8].unsqueeze(2).to_broadcast([T, 4, 512]), ALU.mult),
              r=[g5r, rsr], w=[g5r])
            yT = g2[:, 0:1024].bitcast(BF16).rearrange("p (c t) -> p c t", t=128)
            transp(ysb, g5r, T, 16, lambda c: pc(l, 48 + c), yT, g2r)
            out_branch(0, lambda c: yT[:, c, 0:T], 16, 128, wb0 + BID["ossm"], g2r, 512)

            qsb = g0
            for blk in range(4):
                pt, ptr = PS[2 + blk % 2]
                projb(lambda c: HTb[:, c, :T], 16, 128, wb0 + BID["in"] + INB["q"] + blk, 512, T,
                      pt[:T, :], ptr, HTr)
                A(lambda e: e.copy(qsb[:T, blk * 512:(blk + 1) * 512], pt[:T, :]), r=[ptr], w=[g0r])
            pt, ptr = PS[2]
            projb(lambda c: HTb[:, c, :T], 16, 128, wb0 + BID["in"] + INB["kv"], 512, T, pt[:T, :], ptr, HTr)
            kv = g1
            A(lambda e: e.copy(kv[:T, 0:512], pt[:T, :]), r=[ptr], w=[g1r])
            sq = g2
            V(lambda e: e.tensor_tensor(sq[:T, 0:2048], qsb[:T, 0:2048], qsb[:T, 0:2048], ALU.mult), r=[g0r], w=[g2r])
            V(lambda e: e.tensor_reduce(ss[:T, 8:40], sq[:T, 0:2048].rearrange("p (h d) -> p h d", d=64), AX.X, ALU.add),
              r=[g2r], w=[ssr])
            rms_rstd(ss[:T, 8:40], rs[:T, 8:40], 64, T, ssr, rsr)
            V(lambda e: e.tensor_tensor(qsb[:T, 0:2048].rearrange("p (h d) -> p h d", d=64),
                                        qsb[:T, 0:2048].rearrange("p (h d) -> p h d", d=64),
                                        rs[:T, 8:40].unsqueeze(2).to_broadcast([T, 32, 64]), ALU.mult),
              r=[g0r, rsr], w=[g0r])
            V(lambda e: e.tensor_tensor(sq[:T, 0:256], kv[:T, 0:256], kv[:T, 0:256], ALU.mult), r=[g1r], w=[g2r])
            V(lambda e: e.tensor_reduce(ss[:T, 0:4], sq[:T, 0:256].rearrange("p (h d) -> p h d", d=64), AX.X, ALU.add),
              r=[g2r], w=[ssr])
            rms_rstd(ss[:T, 0:4], rs[:T, 0:4], 64, T, ssr, rsr)
            ktok = kv[:T, 512:768]
            V(lambda e: e.tensor_tensor(ktok.rearrange("p (h d) -> p h d", d=64),
                                        kv[:T, 0:256].rearrange("p (h d) -> p h d", d=64),
                                        rs[:T, 0:4].unsqueeze(2).to_broadcast([T, 4, 64]), ALU.mult), r=[g1r, rsr], w=[g1r])
            V(lambda e: e.tensor_tensor(ktok, ktok, pr(l, 128, 256)[:T, :], ALU.mult), r=[g1r, prowr], w=[g1r])
            transp(kv, g1r, T, 4, None, KTC, KTCr, width=64, src_off=512)
            if win_out is not None:
                win_out(kv, g1r)
            esink, esinkr = SM["esink"]
            A(lambda e: e.activation(esink[:, :], pr(l, 96, 32), AF.Exp), r=[prowr], w=[esinkr])
            nT = 8 * T
            for gq in range(4):
                qT = g2[:64, 0:1024].rearrange("p (h t) -> p h t", t=128)
                transp(qsb, g0r, T, 8, lambda c: pc(l, 188)[:64, :], qT, g2r, width=64, src_off=gq * 512)
                Em = g3
                kb.dma("sp", Em[:, 0:2048], emask_d[gq], writes=[g3r])
                Em4 = g3[:, 0:2048].rearrange("p (b h q) -> p b h q", b=2, h=8)
                PT = g4[:, 0:2048].rearrange("p (b h q) -> p b h q", b=2, h=8)
                blocks = ([0] if has_prev else []) + [1]
                for bi, kbk in enumerate(blocks):
                    nk = 128 if kbk == 0 else T
                    for hb in range(2):
                        pt, ptr = PS[2 + hb]
                        for hq in range(4):
                            hh = hb * 4 + hq
                            if kbk == 0:
                                P(lambda e: e.matmul(pt[:nk, hq * 128: hq * 128 + T], KTPl[:64, gq, 0:nk], qT[:64, hh, 0:T],
                                                     start=True, stop=True), r=[KTPlr, g2r], w=[ptr])
                            else:
                                P(lambda e: e.matmul(pt[:nk, hq * 128: hq * 128 + T], KTC[:64, gq, 0:nk], qT[:64, hh, 0:T],
                                                     start=True, stop=True), r=[KTCr, g2r], w=[ptr])
                        pv = pt[:nk, :].rearrange("p (h q) -> p h q", q=128)[:, :, 0:T]
                        A(lambda e: e.activation(PT[:nk, kbk, hb * 4:(hb + 1) * 4, 0:T], pv, AF.Exp, scale=0.125),
                          r=[ptr], w=[g4r])
                        V(lambda e: e.tensor_tensor(PT[:nk, kbk, hb * 4:(hb + 1) * 4, 0:T],
                                                    PT[:nk, kbk, hb * 4:(hb + 1) * 4, 0:T],
                                                    Em4[:nk, kbk, hb * 4:(hb + 1) * 4, 0:T], ALU.mult),
                          r=[g4r, g3r], w=[g4r])
                for hb in range(2):
                    po, por = PS[4 + hb]
                    pd, pdr = PS[6 + hb]
                    for hq in range(4):
                        hh = hb * 4 + hq
                        for bi, kbk in enumerate(blocks):
                            nk = 128 if kbk == 0 else T
                            if kbk == 0:
                                vsrc, vr = VPl[:nk, gq * 64:(gq + 1) * 64], VPlr
                            else:
                                vsrc, vr = kv[:nk, 256 + gq * 64: 256 + (gq + 1) * 64], g1r
                            P(lambda e: e.matmul(po[:64, hq * 128: hq * 128 + T], vsrc, PT[:nk, kbk, hh, 0:T],
                                                 start=(bi == 0), stop=(bi == len(blocks) - 1)), r=[vr, g4r], w=[por])
                            P(lambda e: e.matmul(pd[:64, hq * 128: hq * 128 + T], ones[:nk, 0:64], PT[:nk, kbk, hh, 0:T],
                                                 start=(bi == 0), stop=(bi == len(blocks) - 1)), r=[onesr, g4r], w=[pdr])
                    oT = g5[:64, 0:2048].bitcast(BF16).rearrange("p (h t) -> p h t", t=128)
                    oTr = g5r
                    hbase = gq * 8 + hb * 4
                    dn = g1[:64, 1024:1536].rearrange("p (h t) -> p h t", t=128)
                    V(lambda e: e.tensor_tensor(dn[:, :, 0:T], pd[:64, :].rearrange("p (h q) -> p h q", q=128)[:, :, 0:T],
                                                esink[:64, gq * 8 + hb * 4: gq * 8 + hb * 4 + 4].unsqueeze(2).to_broadcast([64, 4, T]),
                                                ALU.add), r=[pdr, esinkr], w=[g1r])
                    V(lambda e: e.reciprocal(dn[:, :, 0:T], dn[:, :, 0:T]), r=[g1r], w=[g1r])
                    V(lambda e: e.tensor_tensor(oT[:, hbase:hbase + 4, 0:T],
                                                po[:64, :].rearrange("p (h q) -> p h q", q=128)[:, :, 0:T],
                                                dn[:, :, 0:T], ALU.mult), r=[por, g1r], w=[oTr])
            if T == 128:
                V(lambda e: e.tensor_copy(KTPl[:, :, :], KTC[:, :, :]), r=[KTCr], w=[KTPlr])
                V(lambda e: e.tensor_copy(VPl[:, :], kv[:, 256:512]), r=[g1r], w=[VPlr])
            win_src = (kv, g1r)

            oTa = g5[:64, 0:2048].bitcast(BF16).rearrange("p (h t) -> p h t", t=128)
            for blk in range(4):
                pb, pbr = PS[6 + blk % 2]
                for sub in range(2):
                    projb(lambda c: oTa[:, c, 0:T], 32, 64, wb0 + BID["oswa"] + blk * 2 + sub, 256, T,
                          pb[:T, sub * 256:(sub + 1) * 256], pbr, g5r)
                gate_merge(1, blk, pb[:T, :], pbr, False)

            qm = g0
            for blk in range(4):
                pt, ptr = PS[2 + blk % 2]
                projb(lambda c: HTb[:, c, :T], 16, 128, wb0 + BID["in"] + INB["qm"] + blk, 512, T,
                      pt[:T, :], ptr, HTr)
                V(lambda e: e.memset(ss[:T, blk:blk + 1], 0.0), w=[ssr])
                A(lambda e: e.activation(qm[:T, blk * 512:(blk + 1) * 512], pt[:T, :], AF.Square,
                                         accum_out=ss[:T, blk:blk + 1]), r=[ptr], w=[g0r, ssr])
                rms_rstd(ss[:T, blk:blk + 1], rs[:T, blk:blk + 1], 512, T, ssr, rsr)
                A(lambda e: e.activation(qm[:T, blk * 512:(blk + 1) * 512], pt[:T, :], AF.Copy, scale=rs[:T, blk:blk + 1]),
                  r=[ptr, rsr], w=[g0r])
            qmT = g1[:, 0:2048].rearrange("p (c t) -> p c t", t=128)
            transp(qm, g0r, T, 16, lambda c: pc(l, 64 + c % 4), qmT, g1r)
            omT = g5[:, 0:1024].bitcast(BF16).rearrange("p (c t) -> p c t", t=128)
            mx, mxr = SM["mx"]
            smm, smr = SM["sm"]
            for hm in range(4):
                KTh = g2[:, 0:1024].rearrange("p (c m) -> p c m", m=256)
                Vh = g3[:, 0:1024].rearrange("p (b d) -> p b d", d=512)
                kb.dma("sp", KTh, memK_ap[:, hm * 4:(hm + 1) * 4, :], reads=[memK_r], writes=[g2r])
                kb.dma("sp", Vh, memV_ap[:, hm * 512:(hm + 1) * 512].rearrange("(b p) d -> p b d", p=128),
                       reads=[memV_r], writes=[g3r])
                pt, ptr = PS[2]
                for c in range(4):
                    P(lambda e: e.matmul(pt[:T, 0:256], qmT[:, hm * 4 + c, 0:T], KTh[:, c, :], start=(c == 0), stop=(c == 3)),
                      r=[g1r, g2r], w=[ptr])
                V(lambda e: e.tensor_reduce(mx[:T, 0:1], pt[:T, 0:256], AX.X, ALU.max), r=[ptr], w=[mxr])
                V(lambda e: e.tensor_scalar(mx[:T, 0:1], mx[:T, 0:1], -(512 ** -0.5), None, ALU.mult), r=[mxr], w=[mxr])
                Pm = g4[:, 0:256]
                V(lambda e: e.memset(smm[:T, 0:1], 0.0), w=[smr])
                A(lambda e: e.activation(Pm[:T, :], pt[:T, 0:256], AF.Exp, bias=mx[:T, 0:1], scale=512 ** -0.5,
                                         accum_out=smm[:T, 0:1]), r=[ptr, mxr], w=[g4r, smr])
                V(lambda e: e.reciprocal(smm[:T, 0:1], smm[:T, 0:1]), r=[smr], w=[smr])
                V(lambda e: e.tensor_scalar(Pm[:T, :], Pm[:T, :], smm[:T, 0:1], None, ALU.mult), r=[g4r, smr], w=[g4r])
                PmT = g4[:, 512:768].rearrange("p (c t) -> p c t", t=128)
                transp(g4, g4r, T, 2, None, PmT, g4r)
                pt2, pt2r = PS[3]
                for dc in range(4):
                    for mc in range(2):
                        P(lambda e: e.matmul(pt2[:, dc * 128: dc * 128 + T], Vh[:, mc, dc * 128:(dc + 1) * 128],
                                             PmT[:, mc, 0:T], start=(mc == 0), stop=(mc == 1)), r=[g3r, g4r], w=[pt2r])
                A(lambda e: e.copy(omT[:, hm * 4:(hm + 1) * 4, 0:T],
                                   pt2[:, :].rearrange("p (c t) -> p c t", t=128)[:, :, 0:T]), r=[pt2r], w=[g5r])
            out_branch(2, lambda c: omT[:, c, 0:T], 16, 128, wb0 + BID["omem"], g5r, 512)

            mT = g2[:, 0:1024].bitcast(BF16).rearrange("p (c t) -> p c t", t=128)
            transp(MG, MGr, T, 16, None, mT, g2r)
            for blk in range(4):
                pt, ptr = PS[2 + blk % 2]
                projb(lambda c: mT[:, c, 0:T], 16, 128, wb0 + BID["out"] + blk, 512, T, pt[:T, :], ptr, g2r)
                V(lambda e: e.tensor_tensor(X[:T, blk * 512:(blk + 1) * 512], X[:T, blk * 512:(blk + 1) * 512], pt[:T, :],
                                            ALU.add), r=[ptr, Xr], w=[Xr])
            if do_peer and run_peer:
                peer(T, l, X, Xr)
            return win_src

        def transp_fm(src3, srcr, T, c0, nch, dst3, dstr):
            for c in range(nch):
                b = tctr[0] % 2
                tctr[0] += 1
                pt, ptr = PS[b]
                P(lambda e: e.transpose(pt[:T, 0:128], src3[:, c0 + c, 0:T], ident[:, :]), r=[srcr, identr], w=[ptr])
                if c % 2 == 0:
                    V(lambda e: e.tensor_copy(dst3[:T, c, :], pt[:T, 0:128]), r=[ptr], w=[dstr])
                else:
                    A(lambda e: e.copy(dst3[:T, c, :], pt[:T, 0:128]), r=[ptr], w=[dstr])

        def peer(T, l, Xt, Xtr):
            ss, ssr = SM["ss"]
            rs, rsr = SM["rs"]
            g0, g0r = Gb[0]
            g1, g1r = Gb[1]
            g2, g2r = Gb[2]
            g3, g3r = Gb[3]
            g4, g4r = Gb[4]
            g5, g5r = Gb[5]
            for i_, nm_ in enumerate(["k1", "k2", "posf", "idxf", "gates", "acol", "wcol", "sc16"]):
                SM[nm_] = (g0[:, 2048 + i_ * 128: 2048 + (i_ + 1) * 128], g0r)
            norm_T(Xt, Xtr, T, lambda c: pc(l, 16 + c), HT, HTr)
            h2b = g0[:, 0:1024].bitcast(BF16)
            kb.dma("sp", g5[:, 0:D], gffn_bc[l], writes=[g5r])
            V(lambda e: e.tensor_tensor(h2b[:T, :], g5[:T, 0:D], XN[:T, :], ALU.mult), r=[g5r, XNr], w=[g0r])
            for blk in range(4):
                pt, ptr = PS[2 + blk % 2]
                proj(lambda c: HT[:, c, :T], 16, 128, w_peer_q[l][:, blk * 512:(blk + 1) * 512], 512, T, pt[:T, :], ptr, HTr)
                A(lambda e: e.copy(g1[:T, blk * 512:(blk + 1) * 512], pt[:T, :]), r=[ptr], w=[g1r])
            qT = g2[:, 0:2048].rearrange("p (c t) -> p c t", t=128)
            transp(g1, g1r, T, 16, None, qT, g2r)
            skT = g3[:, 0:2048].rearrange("p (c n) -> p c n", n=128)
            kb.dma("sp", skT, skT_d[l], writes=[g3r])
            sc = g4[:, 0:2048].rearrange("p (c n) -> p c n", n=128)
            for q4 in range(4):
                pt, ptr = PS[4 + q4]
                for j in range(4):
                    hc = q4 * 4 + j
                    P(lambda e: e.matmul(pt[:T, j * 128:(j + 1) * 128], qT[:, hc, 0:T], skT[:, hc, :], start=True, stop=True),
                      r=[g2r, g3r], w=[ptr])
                A(lambda e: e.copy(g4[:T, q4 * 512:(q4 + 1) * 512], pt[:T, :]), r=[ptr], w=[g4r])
            sc2 = g5[:, 0:2048].rearrange("p (c n) -> p c n", n=128)
            vtop, vtopr = SM["vtop"]
            itopf, itopfr = SM["itopf"]
            for hc in range(16):
                V(lambda e: e.max(vtop[:T, hc * 16: hc * 16 + 8], sc[:T, hc, :]), r=[g4r], w=[vtopr])
                V(lambda e: e.max_index(itop_u[:T, hc * 16: hc * 16 + 8], vtop[:T, hc * 16: hc * 16 + 8], sc[:T, hc, :]),
                  r=[g4r, vtopr], w=[itop_ur])
                V(lambda e: e.match_replace(sc2[:T, hc, :], vtop[:T, hc * 16: hc * 16 + 8], sc[:T, hc, :], -1e30),
                  r=[g4r, vtopr], w=[g5r])
                V(lambda e: e.max(vtop[:T, hc * 16 + 8: hc * 16 + 16], sc2[:T, hc, :]), r=[g5r], w=[vtopr])
                V(lambda e: e.max_index(itop_u[:T, hc * 16 + 8: hc * 16 + 16], vtop[:T, hc * 16 + 8: hc * 16 + 16],
                                        sc2[:T, hc, :]), r=[g5r, vtopr], w=[itop_ur])
            V(lambda e: e.tensor_copy(itopf[:T, :], itop_u[:T, :]), r=[itop_ur], w=[itopfr])
            cand = g1[:, 0:2048].rearrange("p (h a b) -> p h a b", h=8, a=16)
            cand2 = g3[:, 0:2048].rearrange("p (h n) -> p h n", n=256)
            v4 = vtop[:T, :].rearrange("p (h c k) -> p h c k", h=8, c=2)
            V(lambda e: e.tensor_tensor(cand[:T], v4[:, :, 0, :].unsqueeze(3).to_broadcast([T, 8, 16, 16]),
                                        v4[:, :, 1, :].unsqueeze(2).to_broadcast([T, 8, 16, 16]), ALU.add),
              r=[vtopr], w=[g1r])
            candf = g1[:, 0:2048].rearrange("p (h n) -> p h n", n=256)
            sc16, sc16r = SM["sc16"]
            for h in range(8):
                V(lambda e: e.max(sc16[:T, h * 16: h * 16 + 8], candf[:T, h, :]), r=[g1r], w=[sc16r])
                V(lambda e: e.max_index(pos_u[:T, h * 16: h * 16 + 8], sc16[:T, h * 16: h * 16 + 8], candf[:T, h, :]),
                  r=[g1r, sc16r], w=[pos_ur])
                V(lambda e: e.match_replace(cand2[:T, h, :], sc16[:T, h * 16: h * 16 + 8], candf[:T, h, :], -1e30),
                  r=[g1r, sc16r], w=[g3r])
                V(lambda e: e.max(sc16[:T, h * 16 + 8: h * 16 + 16], cand2[:T, h, :]), r=[g3r], w=[sc16r])
                V(lambda e: e.max_index(pos_u[:T, h * 16 + 8: h * 16 + 16], sc16[:T, h * 16 + 8: h * 16 + 16],
                                        cand2[:T, h, :]), r=[g3r, sc16r], w=[pos_ur])
            k1, k1r = SM["k1"]
            k2, k2r = SM["k2"]
            V(lambda e: e.tensor_single_scalar(idx_i[:T, :], pos_u[:T, :].bitcast(I32), 4, ALU.logical_shift_right), r=[pos_ur], w=[idx_ir])
            V(lambda e: e.tensor_copy(k1[:T, :], idx_i[:T, :]), r=[idx_ir], w=[k1r])
            V(lambda e: e.tensor_single_scalar(idx_i[:T, :], pos_u[:T, :].bitcast(I32), 15, ALU.bitwise_and), r=[pos_ur], w=[idx_ir])
            V(lambda e: e.tensor_copy(k2[:T, :], idx_i[:T, :]), r=[idx_ir], w=[k2r])
            oh = g4[:, 0:2048].rearrange("p (h k j) -> p h k j", h=8, k=16)
            i4 = itopf[:T, :].rearrange("p (h c k) -> p h c k", h=8, c=2)
            idxf, idxfr = SM["idxf"]
            posf, posfr = SM["posf"]
            for (kk, kkr, ci, dst) in ((k1, k1r, 0, idxf), (k2, k2r, 1, posf)):
                V(lambda e: e.tensor_tensor(oh[:T], kk[:T, :].rearrange("p (h k) -> p h k", h=8).unsqueeze(3).to_broadcast([T, 8, 16, 16]),
                                            iot16[:T, :].unsqueeze(1).unsqueeze(1).to_broadcast([T, 8, 16, 16]), ALU.is_equal),
                  r=[kkr, iot16r], w=[g4r])
                V(lambda e: e.tensor_tensor(oh[:T], oh[:T], i4[:, :, ci, :].unsqueeze(2).to_broadcast([T, 8, 16, 16]), ALU.mult),
                  r=[g4r, itopfr], w=[g4r])
                V(lambda e: e.tensor_reduce(dst[:T, :], g4[:T, 0:2048].rearrange("p (a j) -> p a j", j=16), AX.X, ALU.add),
                  r=[g4r], w=[idxfr if ci == 0 else posfr])
            V(lambda e: e.scalar_tensor_tensor(idxf[:T, :], idxf[:T, :], 128.0, posf[:T, :], ALU.mult, ALU.add),
              r=[idxfr, posfr], w=[idxfr])
            if l > 0:
                V(lambda e: e.tensor_scalar(idxf[:T, :], idxf[:T, :], float(l * 16384), None, ALU.add), r=[idxfr], w=[idxfr])
            V(lambda e: e.tensor_copy(idx_i[:T, :], idxf[:T, :]), r=[idxfr], w=[idx_ir])
            gates, gatesr = SM["gates"]
            t8, t8r = SM["t8"]
            s3 = sc16[:T, :].rearrange("p (h k) -> p h k", k=16)
            g3v = gates[:T, :].rearrange("p (h k) -> p h k", k=16)
            V(lambda e: e.tensor_tensor(g3v, s3, s3[:, :, 0:1].to_broadcast([T, 8, 16]), ALU.subtract), r=[sc16r], w=[gatesr])
            A(lambda e: e.activation(gates[:T, :], gates[:T, :], AF.Exp), r=[gatesr], w=[gatesr])
            V(lambda e: e.tensor_reduce(t8[:T, :], g3v, AX.X, ALU.add), r=[gatesr], w=[t8r])
            V(lambda e: e.reciprocal(t8[:T, :], t8[:T, :]), r=[t8r], w=[t8r])
            V(lambda e: e.tensor_tensor(g3v, g3v, t8[:T, :].unsqueeze(2).to_broadcast([T, 8, 16]), ALU.mult),
              r=[gatesr, t8r], w=[gatesr])
            acol, acolr = SM["acol"]
            wcol, wcolr = SM["wcol"]
            V(lambda e: e.memset(acol[:T, :], 0.0), w=[acolr])
            acres = [Res("ac%d" % i) for i in range(128)]
            for r_ in acres:
                r_.w = acolr.w
            gbufs = []
            for k_ in range(3):
                for i_ in range(1, 5):
                    r_ = Res("gs%d_%d" % (i_, k_))
                    r_.w = Gb[i_][1].w
                    r_.r = dict(Gb[i_][1].r)
                    gbufs.append((Gb[i_][0][:, k_ * 1024:(k_ + 1) * 1024], r_))
            NGB = len(gbufs)
            for s in range(128):
                gb, gbr = gbufs[s % NGB]
                gbv = gb[:, 0:1024].bitcast(BF16)
                kb.dma("pool", gbv[:T, :], usc, reads=[idx_ir], writes=[gbr],
                       indirect=bass.IndirectOffsetOnAxis(idx_i[:T, s:s + 1], 0))
                V(lambda e: e.scalar_tensor_tensor(gbv[:T, :], gbv[:T, :], 1.0, h2b[:T, :], ALU.mult, ALU.mult,
                                                   accum_out=acol[:T, s:s + 1]), r=[gbr, g0r], w=[gbr, acres[s]])
            A(lambda e: e.activation(wcol[:T, :], acol[:T, :], AF.Gelu), r=[acolr] + acres, w=[wcolr])
            V(lambda e: e.tensor_tensor(wcol[:T, :], wcol[:T, :], gates[:T, :], ALU.mult), r=[wcolr, gatesr], w=[wcolr])
            NDS = 8
            dres = [Res("diag%d" % i) for i in range(NDS)]
            for s in range(128):
                gb, gbr = gbufs[s % NGB]
                gbv = gb[:, 0:1024].bitcast(BF16)
                kb.dma("pool", gbv[:T, :], vsc, reads=[idx_ir], writes=[gbr],
                       indirect=bass.IndirectOffsetOnAxis(idx_i[:T, s:s + 1], 0))
                ds_ = s % NDS
                dv = g5[:, 0:1024].bitcast(BF16)[:T, ds_ * 128: ds_ * 128 + T]
                wl = [dres[ds_]] + ([g5r] if (s < NDS or s >= 128 - NDS) else [])
                V(lambda e: e.tensor_scalar(dv, ident[:T, :T], wcol[:T, s:s + 1], None, ALU.mult),
                  r=[identr, wcolr], w=wl)
                for q in range(4):
                    pq, pqr = PS[4 + q]
                    rl = [dres[ds_], gbr] + ([g5r] if s >= 128 - NDS else [])
                    P(lambda e: e.matmul(pq[:T, :], dv, gbv[:T, q * 512:(q + 1) * 512], start=(s == 0), stop=(s == 127)),
                      r=rl, w=[pqr])
            for q in range(4):
                pq, pqr = PS[4 + q]
                V(lambda e: e.tensor_tensor(Xt[:T, q * 512:(q + 1) * 512], Xt[:T, q * 512:(q + 1) * 512], pq[:T, :], ALU.add),
                  r=[pqr, Xtr], w=[Xtr])
            for j_, (gb_, r_) in enumerate(gbufs):
                gr_ = Gb[1 + j_ % 4][1]
                for src_, c_ in ([r_.w] if r_.w else []) + list(r_.r.items()):
                    if gr_.r.get(src_, 0) < c_:
                        gr_.r[src_] = c_

        def ssm_out(l, dst_ap, dst_r):
            Sl, Slr = S[l]
            g0, g0r = Gb[0]
            so = g0[:, 0:2048].rearrange("p (c n) -> p c n", n=128)
            transp(Sl, Slr, 128, 16, None, so, g0r)
            kb.dma("sp", dst_ap.rearrange("(c p) n -> p c n", p=128), so, reads=[g0r], writes=[dst_r])

        r_yp, r_ys = ores("yp"), ores("ys")
        r_pw, r_ps, r_pc = ores("pw"), ores("pssm"), ores("pconv")
        r_sw, r_ss, r_scv = ores("sw"), ores("sssm"), ores("sconv")

        precast_all()
        if NPT > 0:
            for l in range(n_layers):
                mem_precompute(l)
                V(lambda e: e.memset(S[l][0][:, :], 0.0), w=[S[l][1]])
                V(lambda e: e.memset(CT[l][0][:, :, :], 0.0), w=[CT[l][1]])
            for ti in range(NPT):
                kb.dma("sp", X[:, :], xp[ti * 128:(ti + 1) * 128, :], writes=[Xr])
                for l in range(n_layers):
                    last = ti == NPT - 1
                    def wo(kv, kvr, l=l):
                        kb.dma("sp", pwk[l], kv[:, 512:768], reads=[kvr], writes=[r_pw])
                        kb.dma("sp", pwv[l], kv[:, 256:512], reads=[kvr], writes=[r_pw])
                    tile_layer(128, l, ti > 0, pmkT[l], pmkT_r[l], pmv[l], pmv_r[l],
                               pconv[l] if last else None, r_pc, wo if last else None)
                    if last:
                        ssm_out(l, pssm[l], r_ps)
                kb.dma("sp", yp[ti * 128:(ti + 1) * 128, :], X[:, :], reads=[Xr], writes=[r_yp])
        cres = Res("cin")
        TS = 8 * NSQ
        XBt, XBr = None, None
        for l in range(n_layers if NSQ > 0 else 0):
            XBt, XBr = S[1] if l == 0 else S[0]
            for sq in range(NSQ):
                if l == 0:
                    kb.dma("sp", X[0:8, :], xsm[sq * 8:(sq + 1) * 8, :], writes=[Xr])
                else:
                    kb.dma("sp", X[0:8, :], XBt[sq * 8:(sq + 1) * 8, :], reads=[XBr], writes=[Xr])
                kb.dma("sp", S[l][0][:, :], sst[l, sq], writes=[S[l][1]])
                kb.dma("sp", CT[l][0][:, :, :], scv[l, sq], writes=[CT[l][1]])
                kb.dma("sp", KTP[l][0][:, :, :], cwkT[l, sq], writes=[KTP[l][1]])
                kb.dma("sp", VP[l][0][:, :], cwv[l, sq], writes=[VP[l][1]])

                def wo(kv, kvr, l=l, sq=sq):
                    kb.dma("sp", swk[l, sq, 0:120, :], cwk[l, sq, 8:128, :], writes=[r_sw])
                    kb.dma("sp", swv[l, sq, 0:120, :], cwv[l, sq, 8:128, :], writes=[r_sw])
                    kb.dma("sp", swk[l, sq, 120:128, :], kv[0:8, 512:768], reads=[kvr], writes=[r_sw])
                    kb.dma("sp", swv[l, sq, 120:128, :], kv[0:8, 256:512], reads=[kvr], writes=[r_sw])
                tile_layer(8, l, True, cmkT[l, sq], cres, cmv[l, sq], cres, sconv[l, sq], r_scv, wo, run_peer=False)
                kb.dma("sp", XBt[sq * 8:(sq + 1) * 8, :], X[0:8, :], reads=[Xr], writes=[XBr])
                ssm_out(l, sssm[l, sq], r_ss)
            if do_peer:
                peer(TS, l, XBt, XBr)
            if l == 0 and n_layers > 1:
                V(lambda e: e.tensor_copy(S[0][0][0:TS, :], S[1][0][0:TS, :]), r=[S[1][1]], w=[S[0][1]])
        if NSQ > 0:
            kb.dma("sp", ys[0:TS, :], XBt[0:TS, :], reads=[XBr], writes=[r_ys])
        kb.wait_all("sp", out_res)
        build.ninst = kb.ninst
    return nc


def _consts():
    ident = np.eye(128, dtype=np.float32)
    j = np.arange(128)
    triu = (j[:, None] <= j[None, :]).astype(np.float32)
    slopes = np.exp2(-8.0 * np.arange(1, 33, dtype=np.float32) / 32).astype(np.float32)
    k = np.arange(128)[:, None].astype(np.float32)
    q = np.arange(128)[None, :].astype(np.float32)
    em = np.zeros((4, 128, 2, 8, 128), np.float32)
    for h in range(32):
        d0 = q + 128 - k
        d1 = q - k
        em[h // 8, :, 0, h % 8, :] = np.where((d0 >= 0) & (d0 <= 128), np.exp(-slopes[h] * d0), 0.0)
        em[h // 8, :, 1, h % 8, :] = np.where((d1 >= 0) & (d1 <= 128), np.exp(-slopes[h] * d1), 0.0)
    return ident, triu, em.reshape(4, 128, 2048)


def _col(v, n):
    return np.ascontiguousarray(np.asarray(v, np.float32).reshape(n, 128).T)


def make_in_maps(inp, NPT, NSQ, cores):
    f = lambda a: np.ascontiguousarray(np.asarray(a, dtype=np.float32))
    ident, triu, em = _consts()
    pcols = np.zeros((128, 400), np.float32)
    prow = np.zeros((128, 768), np.float32)
    for l in range(2):
        o = l * 200
        pcols[:, o:o + 16] = _col(inp["g_mix"][l], 16)
        pcols[:, o + 16:o + 32] = _col(inp["g_ffn"][l], 16)
        pcols[:, o + 32:o + 48] = _col(inp["g_mem"][l], 16)
        pcols[:, o + 48:o + 64] = _col(inp["g_ssd_norm"][l], 16)
        pcols[:, o + 64:o + 68] = _col(inp["g_qm"][l], 4)
        for k in range(4):
            pcols[:, o + 68 + k * 24:o + 68 + (k + 1) * 24] = _col(inp["conv_w"][l][k], 24)
        pcols[:, o + 164:o + 188] = _col(inp["conv_b"][l], 24)
        pcols[:64, o + 188] = np.asarray(inp["g_q"][l], np.float32)
        r = l * 384
        prow[:, r:r + 32] = np.asarray(inp["dt_bias"][l], np.float32)[None, :]
        prow[:, r + 32:r + 64] = np.asarray(inp["a_log"][l], np.float32)[None, :]
        prow[:, r + 64:r + 96] = np.asarray(inp["d_skip"][l], np.float32)[None, :]
        prow[:, r + 96:r + 128] = np.asarray(inp["attn_sinks"][l], np.float32)[None, :]
        prow[:, r + 128:r + 384] = np.tile(np.asarray(inp["g_k"][l], np.float32), 4)[None, :]
    gkm_bc = f(np.broadcast_to(np.tile(np.asarray(inp["g_km"], np.float32), (1, 4))[:, None, :], (2, 128, D)))
    gffn_bc = f(np.broadcast_to(np.asarray(inp["g_ffn"], np.float32)[:, None, :], (2, 128, D)))
    skT = f(np.asarray(inp["peer_sub_keys"], np.float32).reshape(2, 16, 128, 128).transpose(0, 3, 1, 2))
    shared = dict(
        w_in=f(inp["w_in"]), w_mem_kv=f(inp["w_mem_kv"]), w_o_ssm=f(inp["w_o_ssm"]), w_o_swa=f(inp["w_o_swa"]),
        w_o_mem=f(inp["w_o_mem"]), w_out=f(inp["w_out"]), w_peer_q=f(inp["w_peer_q"]), peer_u=f(inp["peer_u"]).reshape(2 * 16384, D),
        peer_v=f(inp["peer_v"]).reshape(2 * 16384, D), skT=skT, pcols=pcols, prow=prow, gkm_bc=gkm_bc, gffn_bc=gffn_bc,
        bgate=f(np.asarray(inp["b_gate"], np.float32).reshape(2, 1, 6144)), ident=ident, triu=triu, emask=em)
    maps = []
    nq = max(NSQ, 1)
    for (pb, s0) in cores:
        m = dict(shared)
        m["xp"] = f(np.asarray(inp["x_prompt"])[pb, :max(NPT, 1) * 128])
        m["memp"] = f(np.asarray(inp["mem_prompt"])[pb])
        sl = slice(s0, s0 + nq)
        m["xsm"] = f(np.asarray(inp["x_sample"])[sl].reshape(nq * 8, D))
        ck = np.asarray(inp["cache_win_k"], np.float32)[:, sl]
        m["cwk"] = f(ck.reshape(2, nq, 128, 256))
        m["cwkT"] = f(ck.transpose(0, 1, 4, 3, 2))
        m["cwv"] = f(np.asarray(inp["cache_win_v"], np.float32)[:, sl].reshape(2, nq, 128, 256))
        ssm = np.asarray(inp["state_ssm"], np.float32)[:, sl]
        m["sst"] = f(ssm.reshape(2, nq, 2048, 128).transpose(0, 1, 3, 2))
        cv = np.asarray(inp["state_conv"], np.float32)[:, sl]
        m["scv"] = f(cv.reshape(2, nq, 3, 24, 128).transpose(0, 1, 4, 3, 2))
        mk = np.asarray(inp["cache_mem_k"], np.float32)[:, sl].reshape(2, nq, 256, 16, 128)
        m["cmkT"] = f(mk.transpose(0, 1, 4, 3, 2))
        m["cmv"] = f(np.asarray(inp["cache_mem_v"], np.float32)[:, sl].reshape(2, nq, 256, D))
        maps.append(m)
    return maps


_NC_CACHE = {}


def kernel(**inputs):
    NPT, NSQ = 16, 4
    key = (NPT, NSQ)
    if key not in _NC_CACHE:
        _NC_CACHE[key] = build(NPT, NSQ)
    nc = _NC_CACHE[key]
    cores = [(c % 4, 4 * c) for c in range(8)]
    maps = make_in_maps(inputs, NPT, NSQ, cores)
    res = run_bass_kernel_spmd(nc, maps, core_ids=list(range(8))).results
    y_prompt = np.stack([res[b]["yp"] for b in range(4)]).astype(np.float32)
    y_sample = np.concatenate([res[c]["ys"].reshape(4, 8, D) for c in range(8)]).astype(np.float32)

    def pstack(name, shape):
        return np.stack([np.stack([res[b][name][l].reshape(shape) for b in range(4)]) for l in range(2)]).astype(np.float32)

    def sstack(name, shape):
        return np.stack([np.concatenate([res[c][name][l].reshape((4,) + shape) for c in range(8)]) for l in range(2)]).astype(np.float32)

    return (y_prompt, y_sample,
            pstack("pwk", (128, 4, 64)), pstack("pwv", (128, 4, 64)),
            pstack("pssm", (32, 64, 128)), pstack("pconv", (3, 3072)),
            pstack("pmk", (256, 4, 512)), pstack("pmv", (256, 4, 512)),
            sstack("swk", (128, 4, 64)), sstack("swv", (128, 4, 64)),
            sstack("sssm", (32, 64, 128)), sstack("sconv", (3, 3072)))
```

```python
import numpy as np
from contextlib import ExitStack
import concourse.bass as bass
import concourse.mybir as mybir
from concourse.bass_utils import run_bass_kernel_spmd

F32 = mybir.dt.float32
BF16 = mybir.dt.bfloat16
I32 = mybir.dt.int32
U32 = mybir.dt.uint32
ALU = mybir.AluOpType
AF = mybir.ActivationFunctionType
AX = mybir.AxisListType

SELF_SYNC = True
SEM_EPOCH = 30000
D = 2048
IN_DIM = 15904
OFF_Z, OFF_XBC, OFF_DT, OFF_Q, OFF_K, OFF_V, OFF_QM, OFF_G = 0, 2048, 5120, 5152, 7200, 7456, 7712, 9760
EPS = 1e-6
GSZ = 3200


class Res:
    __slots__ = ("name", "w", "r")

    def __init__(self, name):
        self.name = name
        self.w = None
        self.r = {}


class KB:
    def __init__(self, nc, stack, n_dma_sems=20):
        self.nc = nc
        self.stack = stack
        self.engs = {"pe": nc.tensor, "dve": nc.vector, "act": nc.scalar,
                     "pool": nc.gpsimd, "sp": nc.sync}
        self.sem = {}
        self.semidx = {}
        self.cnt = {}
        for k in self.engs:
            self._new_sem(k)
        self.waited = {k: {} for k in self.engs}
        self.dma_sems = [stack.enter_context(nc.semaphore("dq%d" % i)) for i in range(n_dma_sems)]
        self.dma_n = 0
        self.dma_tgt = [0] * n_dma_sems
        self.ninst = 0

    def _new_sem(self, k):
        i = self.semidx.get(k, -1) + 1
        self.semidx[k] = i
        self.sem[(k, i)] = self.stack.enter_context(self.nc.semaphore("s_%s_%d" % (k, i)))
        self.cnt[k] = 0

    def _semof(self, src):
        if src[0] == "e":
            return self.sem[(src[1], src[2])]
        return self.dma_sems[src[1]]

    def _collect(self, reads, writes):
        deps = {}

        def add(s, c):
            if deps.get(s, 0) < c:
                deps[s] = c
        for r in reads:
            if r.w:
                add(*r.w)
        for w in writes:
            if w.w:
                add(*w.w)
            for s, c in w.r.items():
                add(s, c)
        return deps

    def _emit_waits(self, eng, deps):
        e = self.engs[eng]
        for s, c in deps.items():
            if s[0] == "e" and s[1] == eng:
                if eng in ("pe", "sp") or not SELF_SYNC:
                    continue
            if self.waited[eng].get(s, 0) >= c:
                continue
            e.wait_ge(self._semof(s), c)
            self.waited[eng][s] = c

    def op(self, eng, emit, reads=(), writes=()):
        deps = self._collect(reads, writes)
        self._emit_waits(eng, deps)
        ins = emit(self.engs[eng])
        if self.cnt[eng] >= SEM_EPOCH:
            self._new_sem(eng)
        self.cnt[eng] += 1
        key = ("e", eng, self.semidx[eng])
        ins.then_inc(self.sem[(eng, self.semidx[eng])], 1)
        c = self.cnt[eng]
        for r in reads:
            r.r[key] = c
        for w in writes:
            w.w = (key, c)
            w.r = {}
        self.ninst += 1
        return ins

    def dma(self, q, out, in_, reads=(), writes=(), indirect=None, **kw):
        deps = self._collect(reads, writes)
        slot = self.dma_n % len(self.dma_sems)
        self.dma_n += 1
        if self.dma_tgt[slot] > 0:
            deps[("d", slot)] = max(deps.get(("d", slot), 0), self.dma_tgt[slot])
        self._emit_waits(q, deps)
        e = self.engs[q]
        if indirect is not None:
            ins = e.indirect_dma_start(out, None, in_, indirect, **kw)
        else:
            ins = e.dma_start(out, in_, **kw)
        self.dma_tgt[slot] += 16
        ins.then_inc(self.dma_sems[slot], 16)
        key = ("d", slot)
        c = self.dma_tgt[slot]
        for r in reads:
            r.r[key] = c
        for w in writes:
            w.w = (key, c)
            w.r = {}
        self.ninst += 1
        return ins

    def dma_barrier(self, eng):
        e = self.engs[eng]
        for i, t in enumerate(self.dma_tgt):
            if t > 0 and self.waited[eng].get(("d", i), 0) < t:
                e.wait_ge(self.dma_sems[i], t)
                self.waited[eng][("d", i)] = t

    def wait_all(self, eng, ress):
        deps = {}
        for r in ress:
            if r.w and deps.get(r.w[0], 0) < r.w[1]:
                deps[r.w[0]] = r.w[1]
            for s, c in r.r.items():
                if deps.get(s, 0) < c:
                    deps[s] = c
        e = self.engs[eng]
        for s, c in deps.items():
            e.wait_ge(self._semof(s), c)


def build(NPT, NSQ, n_layers=2, do_peer=True):
    nc = bass.Bass("TRN2", target_bir_lowering=False)

    def din(name, shape, dt=F32):
        return nc.dram_tensor(name, list(shape), dt, kind="ExternalInput").ap()

    def dout(name, shape):
        return nc.dram_tensor(name, list(shape), F32, kind="ExternalOutput").ap()

    xp = din("xp", [max(NPT, 1) * 128, D])
    xsm = din("xsm", [max(NSQ, 1) * 8, D])
    cwk = din("cwk", [2, max(NSQ, 1), 128, 256])
    cwv = din("cwv", [2, max(NSQ, 1), 128, 256])
    cwkT = din("cwkT", [2, max(NSQ, 1), 64, 4, 128])
    sst = din("sst", [2, max(NSQ, 1), 128, 2048])
    scv = din("scv", [2, max(NSQ, 1), 128, 24, 3])
    cmkT = din("cmkT", [2, max(NSQ, 1), 128, 16, 256])
    cmv = din("cmv", [2, max(NSQ, 1), 256, 2048])
    memp = din("memp", [256, D])
    w_in = din("w_in", [2, D, IN_DIM])
    w_mem_kv = din("w_mem_kv", [2, D, 4096])
    w_o_ssm = din("w_o_ssm", [2, D, D])
    w_o_swa = din("w_o_swa", [2, D, D])
    w_o_mem = din("w_o_mem", [2, D, D])
    w_out = din("w_out", [2, D, D])
    w_peer_q = din("w_peer_q", [2, D, D])
    peer_u = din("peer_u", [2 * 16384, D])
    peer_v = din("peer_v", [2 * 16384, D])
    skT_d = din("skT", [2, 128, 16, 128])
    NPC = 2 * 200
    pcols_d = din("pcols", [128, NPC])
    NPR = 2 * 384
    prow_d = din("prow", [128, NPR])
    gkm_bc = din("gkm_bc", [2, 128, D])
    gffn_bc = din("gffn_bc", [2, 128, D])
    bgate = din("bgate", [2, 1, 6144])
    ident_d = din("ident", [128, 128])
    triu_d = din("triu", [128, 128])
    emask_d = din("emask", [4, 128, 2048])

    yp = dout("yp", [max(NPT, 1) * 128, D])
    ys = dout("ys", [max(NSQ, 1) * 8, D])
    pwk = dout("pwk", [2, 128, 256])
    pwv = dout("pwv", [2, 128, 256])
    pssm = dout("pssm", [2, 2048, 128])
    pconv = dout("pconv", [2, 3, 3072])
    pmk = dout("pmk", [2, 256, D])
    pmv = dout("pmv", [2, 256, D])
    swk = dout("swk", [2, max(NSQ, 1), 128, 256])
    swv = dout("swv", [2, max(NSQ, 1), 128, 256])
    sssm = dout("sssm", [2, max(NSQ, 1), 2048, 128])
    sconv = dout("sconv", [2, max(NSQ, 1), 3, 3072])
    pmkT = nc.dram_tensor("pmkT", [2, 128, 16, 256], F32, kind="Internal").ap()
    NBLK = 52
    wsc = nc.dram_tensor("wsc", [2 * NBLK, 128, 8192], BF16, kind="Internal").ap()
    usc = nc.dram_tensor("usc", [2 * 16384, D], BF16, kind="Internal").ap()
    vsc = nc.dram_tensor("vsc", [2 * 16384, D], BF16, kind="Internal").ap()
    BID = {"in": 0, "dt": 31, "ossm": 32, "oswa": 36, "omem": 44, "out": 48}

    out_res = []

    def ores(name):
        r = Res(name)
        out_res.append(r)
        return r

    with ExitStack() as st:
        kb = KB(nc, st)

        def sb(name, shape, dt=F32):
            t = st.enter_context(nc.sbuf_tensor(name, list(shape), dt))
            return t, Res(name)

        def V(fn, r=(), w=()):
            return kb.op("dve", fn, reads=r, writes=w)

        def A(fn, r=(), w=()):
            return kb.op("act", fn, reads=r, writes=w)

        def P(fn, r=(), w=()):
            return kb.op("pe", fn, reads=r, writes=w)

        def G(fn, r=(), w=()):
            return kb.op("pool", fn, reads=r, writes=w)

        X, Xr = sb("X", [128, D])
        XN, XNr = sb("XN", [128, D])
        HT, HTr = sb("HT", [128, 16, 128])
        HTb = HT[:, 0:8, :].rearrange("p c t -> p (c t)").bitcast(BF16).rearrange("p (c t) -> p c t", t=128)
        MG, MGr = sb("MG", [128, D])
        WB = [sb("WB%d" % i, [128, 4096]) for i in range(4)]
        Gb = [sb("G%d" % i, [128, GSZ if i == 1 else 3072]) for i in range(6)]
        S = [sb("S%d" % l, [128, D]) for l in range(2)]
        CT = [sb("CT%d" % l, [128, 24, 3]) for l in range(2)]
        KTP = [sb("KTP%d" % l, [64, 4, 128]) for l in range(2)]
        VP = [sb("VP%d" % l, [128, 256]) for l in range(2)]
        KTC, KTCr = sb("KTC", [64, 4, 128])
        ident, identr = sb("ident_s", [128, 128])
        triu, triur = sb("triu_s", [128, 128])
        ones, onesr = sb("ones_s", [128, 128])
        pcols, pcolsr = sb("pcols_s", [128, NPC])
        prow, prowr = sb("prow_s", [128, NPR])
        epst, epstr = sb("epst", [128, 1])
        onec, onecr = sb("onec", [128, 1])
        iot16, iot16r = sb("iot16", [128, 16])
        SM = {}
        for nm, w_ in [("ss", 40), ("rs", 40), ("dt", 32), ("dA", 32), ("cum", 32), ("ncum", 32), ("ecum", 32),
                       ("cl", 32), ("wend", 32), ("A", 32), ("esink", 32), ("bg", 512), ("mx", 8), ("sm", 8),
                       ("vtop", 256), ("itopf", 256), ("t8", 8)]:
            SM[nm] = sb("sm_" + nm, [128, w_])
        itop_u, itop_ur = sb("itop_u", [128, 256], U32)
        pos_u, pos_ur = sb("pos_u", [128, 128], U32)
        idx_i, idx_ir = sb("idx_i", [128, 128], I32)
        PS = [(st.enter_context(nc.psum_tensor("ps%d" % i, [128, 512], F32)), Res("ps%d" % i)) for i in range(8)]

        pmv_r = [Res("pmv%d" % l) for l in range(2)]
        pmkT_r = [Res("pmkT%d" % l) for l in range(2)]
        ores_pmk = ores("pmk")

        kb.dma("sp", ident[:], ident_d, writes=[identr])
        kb.dma("sp", triu[:], triu_d, writes=[triur])
        kb.dma("sp", pcols[:], pcols_d, writes=[pcolsr])
        kb.dma("sp", prow[:], prow_d, writes=[prowr])
        G(lambda e: e.memset(ones[:], 1.0), w=[onesr])
        G(lambda e: e.memset(epst[:], EPS), w=[epstr])
        G(lambda e: e.memset(onec[:], 1.0), w=[onecr])
        G(lambda e: e.iota(iot16[:], [[1, 16]], base=0, channel_multiplier=0, allow_small_or_imprecise_dtypes=True),
          w=[iot16r])

        def pc(l, off, n=1):
            return pcols[:, l * 200 + off: l * 200 + off + n]

        def pr(l, off, n):
            return prow[:, l * 384 + off: l * 384 + off + n]

        wctr = [0]

        def proj(lhs_fn, nk, kp, wsrc, ncols, T, ps_ap, ps_res, lhs_res, extra=None):
            for c0 in range(0, ncols, 256):
                n_ = min(256, ncols - c0)
                i = wctr[0] % 4
                wctr[0] += 1
                wt, wr = WB[i]
                wv = wt[0:kp, 0:nk * n_].rearrange("p (c n) -> p c n", n=n_)
                kb.dma("sp", wv, wsrc[:, c0:c0 + n_].rearrange("(c p) n -> p c n", p=kp), writes=[wr])
                for c in range(nk):
                    P(lambda e: e.matmul(ps_ap[:, c0:c0 + n_], lhs_fn(c), wv[:, c, :], start=(c == 0), stop=(c == nk - 1)),
                      r=[lhs_res, wr], w=[ps_res])

        def projb(lhs_fn, nk, kp, bid, ncols, T, ps_ap, ps_res, lhs_res, extra=None):
            i = wctr[0] % 4
            wctr[0] += 1
            wt, wr = WB[i]
            wv = wt[0:kp, :].bitcast(BF16)[:, 0:nk * ncols].rearrange("p (c n) -> p c n", n=ncols)
            kb.dma("sp", wt[0:kp, :].bitcast(BF16)[:, 0:nk * ncols], wsc[bid][0:kp, 0:nk * ncols], writes=[wr])
            for c in range(nk):
                last = (c == nk - 1) and extra is None
                P(lambda e: e.matmul(ps_ap, lhs_fn(c), wv[:, c, :], start=(c == 0), stop=last),
                  r=[lhs_res, wr], w=[ps_res])
            if extra is not None:
                P(lambda e: e.matmul(ps_ap, extra[0], extra[1], start=False, stop=True),
                  r=extra[2], w=[ps_res])

        def precast_all():
            jobs = []
            for l in range(n_layers):
                base = l * NBLK

                def std(src, c0, bid):
                    us = []
                    for hf in range(2):
                        v = src[hf * 1024:(hf + 1) * 1024, c0:c0 + 512].rearrange("(c p) n -> p c n", p=128)
                        us.append((v, 8, 512, hf * 4096))
                    jobs.append((128, us, ("w", bid, 8192)))
                cols = [OFF_Z + i * 512 for i in range(4)] + [OFF_XBC + i * 512 for i in range(6)] + \
                       [OFF_Q + i * 512 for i in range(4)] + [OFF_K] + [OFF_QM + i * 512 for i in range(4)] + \
                       [OFF_G + i * 512 for i in range(12)]
                for i, c0 in enumerate(cols):
                    std(w_in[l], c0, base + BID["in"] + i)
                jobs.append((128, [(w_in[l][:, OFF_DT:OFF_DT + 32].rearrange("(c p) n -> p c n", p=128), 16, 32, 0)],
                             ("w", base + BID["dt"], 512)))
                for nm, wm in (("ossm", w_o_ssm), ("omem", w_o_mem), ("out", w_out)):
                    for i in range(4):
                        std(wm[l], i * 512, base + BID[nm] + i)
                for i in range(8):
                    us = []
                    for hf in range(2):
                        v = w_o_swa[l][hf * 1024:(hf + 1) * 1024, i * 256:(i + 1) * 256].rearrange("(c p) n -> p c n", p=64)
                        us.append((v, 16, 256, hf * 4096))
                    jobs.append((64, us, ("w", base + BID["oswa"] + i, 8192)))
            if do_peer:
                for (src, dst) in ((peer_u, usc), (peer_v, vsc)):
                    for blk in range(64 * n_layers):
                        r0 = blk * 256
                        jobs.append((128, [(src[r0:r0 + 256, :].rearrange("(a p) d -> p a d", p=128), 2, D, 0)],
                                     ("t", dst[r0:r0 + 256, :].rearrange("(a p) d -> p a d", p=128), 4096)))
            units = []
            for ji, (kp, us, dst) in enumerate(jobs):
                for ui, u in enumerate(us):
                    units.append((ji, kp, u, ui == len(us) - 1, dst))

            def issue_in(k):
                ji, kp, (src, a_, n_, off), last, dst = units[k]
                wt, wr = WB[k % 2]
                kb.dma("sp", wt[0:kp, 0:a_ * n_].rearrange("p (c n) -> p c n", n=n_), src, writes=[wr])
            if units:
                issue_in(0)
            for k in range(len(units)):
                if k + 1 < len(units):
                    issue_in(k + 1)
                ji, kp, (src, a_, n_, off), last, dst = units[k]
                wt, wr = WB[k % 2]
                stg, stgr = WB[2 + ji % 2]
                stb = stg[0:kp, :].bitcast(BF16)
                dstv = stb[:, off:off + a_ * n_]
                srcv = wt[0:kp, 0:a_ * n_]
                e_ = k % 3
                if e_ == 0:
                    V(lambda e: e.tensor_copy(dstv, srcv), r=[wr], w=[stgr])
                elif e_ == 1:
                    A(lambda e: e.copy(dstv, srcv), r=[wr], w=[stgr])
                else:
                    G(lambda e: e.tensor_copy(dstv, srcv), r=[wr], w=[stgr])
                if last:
                    if dst[0] == "w":
                        kb.dma("sp", wsc[dst[1]][0:kp, 0:dst[2]], stb[:, 0:dst[2]], reads=[stgr], writes=[Res("wsc_tmp")])
                    else:
                        kb.dma("sp", dst[1], stb[:, 0:dst[2]].rearrange("p (a d) -> p a d", d=D), reads=[stgr],
                               writes=[Res("tsc_tmp")])
            kb.dma_barrier("sp")

        INB = {"z": 0, "xbc": 4, "q": 10, "kv": 14, "qm": 15, "g": 19}

        def rms_rstd(ss_ap, rs_ap, n, T, ssr, rsr):
            A(lambda e: e.activation(rs_ap, ss_ap, AF.Sqrt, bias=epst[:T, :], scale=1.0 / n), r=[ssr, epstr], w=[rsr])
            V(lambda e: e.reciprocal(rs_ap, rs_ap), r=[rsr], w=[rsr])

        def norm_T(src, srcr, T, gcol_fn, dst, dstr, scale=None):
            ss, ssr = SM["ss"]
            rs, rsr = SM["rs"]
            V(lambda e: e.memset(ss[:T, 0:1], 0.0), w=[ssr])
            A(lambda e: e.activation(XN[:T, :], src[:T, :], AF.Square, accum_out=ss[:T, 0:1]), r=[srcr], w=[XNr, ssr])
            rms_rstd(ss[:T, 0:1], rs[:T, 0:1], D, T, ssr, rsr)
            A(lambda e: e.activation(XN[:T, :], src[:T, :], AF.Copy, scale=rs[:T, 0:1]), r=[srcr, rsr], w=[XNr])
            transp(XN, XNr, T, 16, gcol_fn, dst, dstr)

        tctr = [0]

        def transp(src, srcr, T, nch, gcol_fn, dst, dstr, width=128, src_off=0):
            for c in range(nch):
                b = tctr[0] % 2
                tctr[0] += 1
                pt, ptr = PS[b]
                P(lambda e: e.transpose(pt[:width, 0:T], src[:T, src_off + c * width: src_off + (c + 1) * width],
                                        ident[:T, :T]), r=[srcr, identr], w=[ptr])
                if gcol_fn is None:
                    if c % 2 == 0:
                        V(lambda e: e.tensor_copy(dst[:width, c, 0:T], pt[:width, 0:T]), r=[ptr], w=[dstr])
                    else:
                        A(lambda e: e.copy(dst[:width, c, 0:T], pt[:width, 0:T]), r=[ptr], w=[dstr])
                else:
                    V(lambda e: e.tensor_scalar(dst[:width, c, 0:T], pt[:width, 0:T], gcol_fn(c), None, ALU.mult),
                      r=[ptr, pcolsr], w=[dstr])

        def mem_precompute(l):
            g0, g0r = Gb[0]
            g1, g1r = Gb[1]
            g2, g2r = Gb[2]
            kT = g2[:, 0:2048].rearrange("p (c t) -> p c t", t=128)
            kb.dma("sp", g1[:, 0:D], gkm_bc[l], writes=[g1r])
            for mt in range(2):
                kb.dma("sp", X[:, :], memp[mt * 128:(mt + 1) * 128, :], writes=[Xr])
                norm_T(X, Xr, 128, lambda c: pc(l, 32 + c), HT, HTr)
                for blk in range(8):
                    pt, ptr = PS[2 + blk % 2]
                    proj(lambda c: HT[:, c, :], 16, 128, w_mem_kv[l][:, blk * 512:(blk + 1) * 512], 512, 128,
                         pt[:, :], ptr, HTr)
                    if blk < 4:
                        ss, ssr = SM["ss"]
                        rs, rsr = SM["rs"]
                        V(lambda e: e.memset(ss[:, 1:2], 0.0), w=[ssr])
                        A(lambda e: e.activation(g0[:, blk * 512:(blk + 1) * 512], pt[:, :], AF.Square,
                                                 accum_out=ss[:, 1:2]), r=[ptr], w=[g0r, ssr])
                        rms_rstd(ss[:, 1:2], rs[:, 1:2], 512, 128, ssr, rsr)
                        V(lambda e: e.scalar_tensor_tensor(g0[:, blk * 512:(blk + 1) * 512], pt[:, :], rs[:, 1:2],
                                                           g1[:, blk * 512:(blk + 1) * 512], ALU.mult, ALU.mult),
                          r=[ptr, rsr, g1r], w=[g0r])
                    else:
                        A(lambda e: e.copy(MG[:, (blk - 4) * 512:(blk - 3) * 512], pt[:, :]), r=[ptr], w=[MGr])
                kb.dma("sp", pmk[l][mt * 128:(mt + 1) * 128, :], g0[:, 0:D], reads=[g0r], writes=[ores_pmk])
                kb.dma("sp", pmv[l][mt * 128:(mt + 1) * 128, :], MG[:, :], reads=[MGr], writes=[pmv_r[l]])
                transp(g0, g0r, 128, 16, None, kT, g2r)
                kb.dma("sp", pmkT[l][:, :, mt * 128:(mt + 1) * 128], kT, reads=[g2r], writes=[pmkT_r[l]])

        def tile_layer(T, l, has_prev, memK_ap, memK_r, memV_ap, memV_r, conv_out_ap, conv_out_r, win_out=None, run_peer=True):
            Sl, Slr = S[l]
            CTl, CTlr = CT[l]
            KTPl, KTPlr = KTP[l]
            VPl, VPlr = VP[l]
            ss, ssr = SM["ss"]
            rs, rsr = SM["rs"]
            g0, g0r = Gb[0]
            g1, g1r = Gb[1]
            g2, g2r = Gb[2]
            g3, g3r = Gb[3]
            g4, g4r = Gb[4]
            g5, g5r = Gb[5]
            bg, bgr = SM["bg"]

            wb0 = l * NBLK
            norm_T(X, Xr, T, lambda c: pc(l, c), HTb, HTr)

            def gate_merge(br, blk, br_ps, br_psr, first):
                pg, pgr = PS[4 + blk % 2]
                col = OFF_G + br * 2048 + blk * 512
                kb.dma("act", bg[0:1, :], bgate[l][:, br * 2048 + blk * 512: br * 2048 + (blk + 1) * 512], writes=[bgr])
                projb(lambda c: HTb[:, c, :T], 16, 128, wb0 + BID["in"] + INB["g"] + br * 4 + blk, 512, T, pg[:T, :], pgr, HTr,
                      extra=(ones[0:1, 0:T], bg[0:1, 0:512], [onesr, bgr]))
                gs = g1[:T, 2600:3112]
                A(lambda e: e.activation(gs, pg[:T, :], AF.Sigmoid), r=[pgr], w=[g1r])
                mgs = MG[:T, blk * 512:(blk + 1) * 512]
                if first:
                    V(lambda e: e.tensor_tensor(mgs, gs, br_ps, ALU.mult), r=[g1r, br_psr], w=[MGr])
                else:
                    V(lambda e: e.tensor_tensor(gs, gs, br_ps, ALU.mult), r=[g1r, br_psr], w=[g1r])
                    V(lambda e: e.tensor_tensor(mgs, mgs, gs, ALU.add), r=[g1r, MGr], w=[MGr])

            def out_branch(br, lhs_fn, nk, kp, bid0, lhs_res, ncols):
                for blk in range(4):
                    pb, pbr = PS[6 + blk % 2]
                    projb(lhs_fn, nk, kp, bid0 + blk, 512, T, pb[:T, :], pbr, lhs_res)
                    gate_merge(br, blk, pb[:T, :], pbr, br == 0)

            xraw = g0
            for blk in range(6):
                pt, ptr = PS[2 + blk % 2]
                projb(lambda c: HTb[:, c, :T], 16, 128, wb0 + BID["in"] + INB["xbc"] + blk, 512, T,
                      pt[:T, :], ptr, HTr)
                if blk % 2 == 0:
                    A(lambda e: e.copy(xraw[:T, blk * 512:(blk + 1) * 512], pt[:T, :]), r=[ptr], w=[g0r])
                else:
                    V(lambda e: e.tensor_copy(xraw[:T, blk * 512:(blk + 1) * 512], pt[:T, :]), r=[ptr], w=[g0r])
            if conv_out_ap is not None:
                kb.dma("sp", conv_out_ap, xraw[T - 3:T, 0:3072], reads=[g0r], writes=[conv_out_r])
            xbcT = g1[:, 0:24 * 131].rearrange("p (c t) -> p c t", t=131)
            V(lambda e: e.tensor_copy(xbcT[:, :, 0:3], CTl[:, :, :]), r=[CTlr], w=[g1r])
            for c in range(24):
                b = tctr[0] % 2
                tctr[0] += 1
                pt, ptr = PS[b]
                P(lambda e: e.transpose(pt[:, 0:T], xraw[:T, c * 128:(c + 1) * 128], ident[:T, :T]),
                  r=[g0r, identr], w=[ptr])
                if c % 2 == 0:
                    V(lambda e: e.tensor_copy(xbcT[:, c, 3:3 + T], pt[:, 0:T]), r=[ptr], w=[g1r])
                else:
                    A(lambda e: e.copy(xbcT[:, c, 3:3 + T], pt[:, 0:T]), r=[ptr], w=[g1r])
            V(lambda e: e.tensor_copy(CTl[:, :, :], xbcT[:, :, T:T + 3]), r=[g1r], w=[CTlr])
            xact = g2[:, 0:24 * 128].rearrange("p (c t) -> p c t", t=128)
            for c in range(24):
                V(lambda e: e.tensor_scalar(xact[:, c, 0:T], xbcT[:, c, 0:T], pc(l, 68 + c), pc(l, 164 + c),
                                            ALU.mult, ALU.add), r=[g1r, pcolsr], w=[g2r])
                for k in range(1, 4):
                    V(lambda e: e.scalar_tensor_tensor(xact[:, c, 0:T], xbcT[:, c, k:k + T], pc(l, 68 + k * 24 + c),
                                                       xact[:, c, 0:T], ALU.mult, ALU.add),
                      r=[g1r, g2r, pcolsr], w=[g2r])
            A(lambda e: e.activation(xact[:, :, 0:T], xact[:, :, 0:T], AF.Silu), r=[g2r], w=[g2r])
            xs = g3
            xs3 = g3[:, 0:2048].rearrange("p (c t) -> p c t", t=128)
            transp_fm(xact, g2r, T, 0, 16, xs3, g3r)
            bt3 = g4[:, 0:512].rearrange("p (c t) -> p c t", t=128)
            transp_fm(xact, g2r, T, 16, 4, bt3, g4r)
            dt, dtr = SM["dt"]
            dA, dAr = SM["dA"]
            Aneg, Anegr = SM["A"]
            pt, ptr = PS[2]
            projb(lambda c: HTb[:, c, :T], 16, 128, wb0 + BID["dt"], 32, T, pt[:T, 0:32], ptr, HTr)
            V(lambda e: e.tensor_tensor(dt[:T, :], pt[:T, 0:32], pr(l, 0, 32)[:T, :], ALU.add), r=[ptr, prowr], w=[dtr])
            A(lambda e: e.activation(dt[:T, :], dt[:T, :], AF.Exp), r=[dtr], w=[dtr])
            A(lambda e: e.activation(dt[:T, :], dt[:T, :], AF.Ln, bias=onec[:T, :], scale=1.0), r=[dtr, onecr], w=[dtr])
            A(lambda e: e.activation(Aneg[:, :], pr(l, 32, 32), AF.Exp), r=[prowr], w=[Anegr])
            V(lambda e: e.scalar_tensor_tensor(dA[:T, :], dt[:T, :], -1.0, Aneg[:T, :], ALU.mult, ALU.mult),
              r=[dtr, Anegr], w=[dAr])
            cum, cumr = SM["cum"]
            cl, clr = SM["cl"]
            ecum, ecumr = SM["ecum"]
            wend, wendr = SM["wend"]
            pt, ptr = PS[3]
            P(lambda e: e.matmul(pt[:T, 0:32], triu[:T, :T], dA[:T, :], start=True, stop=True), r=[triur, dAr], w=[ptr])
            P(lambda e: e.matmul(pt[:, 32:64], ones[:T, :], dA[:T, :], start=True, stop=True), r=[onesr, dAr], w=[ptr])
            V(lambda e: e.tensor_copy(cum[:T, :], pt[:T, 0:32]), r=[ptr], w=[cumr])
            V(lambda e: e.tensor_copy(cl[:, :], pt[:, 32:64]), r=[ptr], w=[clr])
            A(lambda e: e.activation(ecum[:T, :], cum[:T, :], AF.Exp), r=[cumr], w=[ecumr])
            V(lambda e: e.tensor_tensor(wend[:T, :], cl[:T, :], cum[:T, :], ALU.subtract), r=[clr, cumr], w=[wendr])
            A(lambda e: e.activation(wend[:T, :], wend[:T, :], AF.Exp), r=[wendr], w=[wendr])
            V(lambda e: e.tensor_tensor(wend[:T, :], wend[:T, :], dt[:T, :], ALU.mult), r=[wendr, dtr], w=[wendr])
            cbm = g4[:, 512:1024].rearrange("p (c t) -> p c t", t=128)
            pt, ptr = PS[2]
            for gq in range(4):
                P(lambda e: e.matmul(pt[:T, gq * 128: gq * 128 + T], xact[:, 16 + gq, 0:T], xact[:, 20 + gq, 0:T],
                                     start=True, stop=True), r=[g2r], w=[ptr])
            for gq in range(4):
                V(lambda e: e.tensor_tensor(cbm[:T, gq, 0:T], pt[:T, gq * 128: gq * 128 + T], triu[:T, :T], ALU.mult),
                  r=[ptr, triur], w=[g4r])
            ysb = g5
            for hf in range(2):
                Rp = g0[:, 0:2048].rearrange("p (h t) -> p h t", t=128)
                for hh in range(16):
                    h = hf * 16 + hh
                    V(lambda e: e.tensor_scalar(Rp[:T, hh, 0:T], triu[:T, :T], dA[:T, h:h + 1], None, ALU.mult),
                      r=[triur, dAr], w=[g0r])
                MT = g1[:, 0:2048].rearrange("p (h t) -> p h t", t=128)
                for q4 in range(4):
                    pt, ptr = PS[4 + q4]
                    for hq in range(4):
                        hh = q4 * 4 + hq
                        P(lambda e: e.matmul(pt[:T, hq * 128: hq * 128 + T], ones[:T, :T], Rp[:T, hh, 0:T],
                                             start=True, stop=True), r=[onesr, g0r], w=[ptr])
                    for hq in range(4):
                        hh = q4 * 4 + hq
                        h = hf * 16 + hh
                        V(lambda e: e.tensor_scalar(MT[:T, hh, 0:T], pt[:T, hq * 128: hq * 128 + T], cum[:T, h:h + 1], 0.0,
                                                    ALU.subtract, ALU.min), r=[ptr, cumr], w=[g1r])
                A(lambda e: e.activation(MT[:T, :, 0:T], MT[:T, :, 0:T], AF.Exp), r=[g1r], w=[g1r])
                for hh in range(16):
                    h = hf * 16 + hh
                    V(lambda e: e.scalar_tensor_tensor(MT[:T, hh, 0:T], MT[:T, hh, 0:T], dt[:T, h:h + 1],
                                                       cbm[:T, h // 8, 0:T], ALU.mult, ALU.mult),
                      r=[g1r, dtr, g4r], w=[g1r])
                for hh in range(16):
                    h = hf * 16 + hh
                    pt, ptr = PS[hh // 8]
                    P(lambda e: e.matmul(pt[:T, (hh % 8) * 64:(hh % 8 + 1) * 64], MT[:T, hh, 0:T],
                                         xs[:T, h * 64:(h + 1) * 64], start=True, stop=True), r=[g1r, g3r], w=[ptr])
                for gq in range(2):
                    gg = hf * 2 + gq
                    pt, ptr = PS[2 + gq]
                    P(lambda e: e.matmul(pt[:T, :], xact[:, 20 + gg, 0:T], Sl[:, gg * 512:(gg + 1) * 512],
                                         start=True, stop=True), r=[g2r, Slr], w=[ptr])
                for gq in range(2):
                    gg = hf * 2 + gq
                    pi, pir = PS[gq]
                    pst, pstr = PS[2 + gq]
                    yv = ysb[:T, gg * 512:(gg + 1) * 512].rearrange("p (h d) -> p h d", d=64)
                    V(lambda e: e.tensor_tensor(yv, pst[:T, :].rearrange("p (h d) -> p h d", d=64),
                                                ecum[:T, gg * 8:(gg + 1) * 8].unsqueeze(2).to_broadcast([T, 8, 64]),
                                                ALU.mult), r=[pstr, ecumr], w=[g5r])
                    V(lambda e: e.tensor_tensor(ysb[:T, gg * 512:(gg + 1) * 512], ysb[:T, gg * 512:(gg + 1) * 512],
                                                pi[:T, :], ALU.add), r=[pir, g5r], w=[g5r])
            xw = g0
            V(lambda e: e.tensor_tensor(xw[:T, 0:2048].rearrange("p (h d) -> p h d", d=64),
                                        xs[:T, 0:2048].rearrange("p (h d) -> p h d", d=64),
                                        pr(l, 64, 32)[:T, :].unsqueeze(2).to_broadcast([T, 32, 64]), ALU.mult),
              r=[g3r, prowr], w=[g0r])
            V(lambda e: e.tensor_tensor(ysb[:T, 0:2048], ysb[:T, 0:2048], xw[:T, 0:2048], ALU.add), r=[g0r, g5r], w=[g5r])
            V(lambda e: e.tensor_tensor(xw[:T, 0:2048].rearrange("p (h d) -> p h d", d=64),
                                        xs[:T, 0:2048].rearrange("p (h d) -> p h d", d=64),
                                        wend[:T, :].unsqueeze(2).to_broadcast([T, 32, 64]), ALU.mult),
              r=[g3r, wendr], w=[g0r])
            A(lambda e: e.activation(cl[:, :], cl[:, :], AF.Exp), r=[clr], w=[clr])
            for gg in range(4):
                pt, ptr = PS[4 + gg]
                P(lambda e: e.matmul(pt[:, :], bt3[:T, gg, :], xw[:T, gg * 512:(gg + 1) * 512], start=True, stop=True),
                  r=[g4r, g0r], w=[ptr])
            V(lambda e: e.tensor_tensor(Sl[:, :].rearrange("p (h d) -> p h d", d=64),
                                        Sl[:, :].rearrange("p (h d) -> p h d", d=64),
                                        cl[:, :].unsqueeze(2).to_broadcast([128, 32, 64]), ALU.mult),
              r=[Slr, clr], w=[Slr])
            for gg in range(4):
                pt, ptr = PS[4 + gg]
                V(lambda e: e.tensor_tensor(Sl[:, gg * 512:(gg + 1) * 512], Sl[:, gg * 512:(gg + 1) * 512], pt[:, :],
                                            ALU.add), r=[Slr, ptr], w=[Slr])
            for blk in range(4):
                pt, ptr = PS[blk % 2]
                projb(lambda c: HTb[:, c, :T], 16, 128, wb0 + BID["in"] + INB["z"] + blk, 512, T,
                      pt[:T, :], ptr, HTr)
                zs = g1[:T, 0:512]
                A(lambda e: e.activation(zs, pt[:T, :], AF.Silu), r=[ptr], w=[g1r])
                V(lambda e: e.tensor_tensor(ysb[:T, blk * 512:(blk + 1) * 512], ysb[:T, blk * 512:(blk + 1) * 512], zs,
                                            ALU.mult), r=[g1r, g5r], w=[g5r])
                V(lambda e: e.memset(ss[:T, 4 + blk:5 + blk], 0.0), w=[ssr])
                A(lambda e: e.activation(zs, ysb[:T, blk * 512:(blk + 1) * 512], AF.Square,
                                         accum_out=ss[:T, 4 + blk:5 + blk]), r=[g5r], w=[g1r, ssr])
            rms_rstd(ss[:T, 4:8], rs[:T, 4:8], 512, T, ssr, rsr)
            V(lambda e: e.tensor_tensor(ysb[:T, 0:2048].rearrange("p (g d) -> p g d", d=512),
                                        ysb[:T, 0:2048].rearrange("p (g d) -> p g d", d=512),
                                        rs[:T, 4:8].unsqueeze(2).to_broadcast([T, 4, 512]), ALU.mult),
              r=[g5r, rsr], w=[g5r])
            yT = g2[:, 0:1024].bitcast(BF16).rearrange("p (c t) -> p c t", t=128)
            transp(ysb, g5r, T, 16, lambda c: pc(l, 48 + c), yT, g2r)
            out_branch(0, lambda c: yT[:, c, 0:T], 16, 128, wb0 + BID["ossm"], g2r, 512)

            qsb = g0
            for blk in range(4):
                pt, ptr = PS[2 + blk % 2]
                projb(lambda c: HTb[:, c, :T], 16, 128, wb0 + BID["in"] + INB["q"] + blk, 512, T,
                      pt[:T, :], ptr, HTr)
                A(lambda e: e.copy(qsb[:T, blk * 512:(blk + 1) * 512], pt[:T, :]), r=[ptr], w=[g0r])
            pt, ptr = PS[2]
            projb(lambda c: HTb[:, c, :T], 16, 128, wb0 + BID["in"] + INB["kv"], 512, T, pt[:T, :], ptr, HTr)
            kv = g1
            A(lambda e: e.copy(kv[:T, 0:512], pt[:T, :]), r=[ptr], w=[g1r])
            sq = g2
            V(lambda e: e.tensor_tensor(sq[:T, 0:2048], qsb[:T, 0:2048], qsb[:T, 0:2048], ALU.mult), r=[g0r], w=[g2r])
            V(lambda e: e.tensor_reduce(ss[:T, 8:40], sq[:T, 0:2048].rearrange("p (h d) -> p h d", d=64), AX.X, ALU.add),
              r=[g2r], w=[ssr])
            rms_rstd(ss[:T, 8:40], rs[:T, 8:40], 64, T, ssr, rsr)
            V(lambda e: e.tensor_tensor(qsb[:T, 0:2048].rearrange("p (h d) -> p h d", d=64),
                                        qsb[:T, 0:2048].rearrange("p (h d) -> p h d", d=64),
                                        rs[:T, 8:40].unsqueeze(2).to_broadcast([T, 32, 64]), ALU.mult),
              r=[g0r, rsr], w=[g0r])
            V(lambda e: e.tensor_tensor(sq[:T, 0:256], kv[:T, 0:256], kv[:T, 0:256], ALU.mult), r=[g1r], w=[g2r])
            V(lambda e: e.tensor_reduce(ss[:T, 0:4], sq[:T, 0:256].rearrange("p (h d) -> p h d", d=64), AX.X, ALU.add),
              r=[g2r], w=[ssr])
            rms_rstd(ss[:T, 0:4], rs[:T, 0:4], 64, T, ssr, rsr)
            ktok = kv[:T, 512:768]
            V(lambda e: e.tensor_tensor(ktok.rearrange("p (h d) -> p h d", d=64),
                                        kv[:T, 0:256].rearrange("p (h d) -> p h d", d=64),
                                        rs[:T, 0:4].unsqueeze(2).to_broadcast([T, 4, 64]), ALU.mult), r=[g1r, rsr], w=[g1r])
            V(lambda e: e.tensor_tensor(ktok, ktok, pr(l, 128, 256)[:T, :], ALU.mult), r=[g1r, prowr], w=[g1r])
            transp(kv, g1r, T, 4, None, KTC, KTCr, width=64, src_off=512)
            if win_out is not None:
                win_out(kv, g1r)
            esink, esinkr = SM["esink"]
            A(lambda e: e.activation(esink[:, :], pr(l, 96, 32), AF.Exp), r=[prowr], w=[esinkr])
            nT = 8 * T
            for gq in range(4):
                qT = g2[:64, 0:1024].rearrange("p (h t) -> p h t", t=128)
                transp(qsb, g0r, T, 8, lambda c: pc(l, 188)[:64, :], qT, g2r, width=64, src_off=gq * 512)
                Em = g3
                kb.dma("act", Em[:, 0:2048], emask_d[gq], writes=[g3r])
                Em4 = g3[:, 0:2048].rearrange("p (b h q) -> p b h q", b=2, h=8)
                PT = g4[:, 0:2048].rearrange("p (b h q) -> p b h q", b=2, h=8)
                blocks = ([0] if has_prev else []) + [1]
                for bi, kbk in enumerate(blocks):
                    nk = 128 if kbk == 0 else T
                    for hb in range(2):
                        pt, ptr = PS[2 + hb]
                        for hq in range(4):
                            hh = hb * 4 + hq
                            if kbk == 0:
                                P(lambda e: e.matmul(pt[:nk, hq * 128: hq * 128 + T], KTPl[:64, gq, 0:nk], qT[:64, hh, 0:T],
                                                     start=True, stop=True), r=[KTPlr, g2r], w=[ptr])
                            else:
                                P(lambda e: e.matmul(pt[:nk, hq * 128: hq * 128 + T], KTC[:64, gq, 0:nk], qT[:64, hh, 0:T],
                                                     start=True, stop=True), r=[KTCr, g2r], w=[ptr])
                        pv = pt[:nk, :].rearrange("p (h q) -> p h q", q=128)[:, :, 0:T]
                        A(lambda e: e.activation(PT[:nk, kbk, hb * 4:(hb + 1) * 4, 0:T], pv, AF.Exp, scale=0.125),
                          r=[ptr], w=[g4r])
                        V(lambda e: e.tensor_tensor(PT[:nk, kbk, hb * 4:(hb + 1) * 4, 0:T],
                                                    PT[:nk, kbk, hb * 4:(hb + 1) * 4, 0:T],
                                                    Em4[:nk, kbk, hb * 4:(hb + 1) * 4, 0:T], ALU.mult),
                          r=[g4r, g3r], w=[g4r])
                for hb in range(2):
                    po, por = PS[4 + hb]
                    pd, pdr = PS[6 + hb]
                    for hq in range(4):
                        hh = hb * 4 + hq
                        for bi, kbk in enumerate(blocks):
                            nk = 128 if kbk == 0 else T
                            if kbk == 0:
                                vsrc, vr = VPl[:nk, gq * 64:(gq + 1) * 64], VPlr
                            else:
                                vsrc, vr = kv[:nk, 256 + gq * 64: 256 + (gq + 1) * 64], g1r
                            P(lambda e: e.matmul(po[:64, hq * 128: hq * 128 + T], vsrc, PT[:nk, kbk, hh, 0:T],
                                                 start=(bi == 0), stop=(bi == len(blocks) - 1)), r=[vr, g4r], w=[por])
                            P(lambda e: e.matmul(pd[:64, hq * 128: hq * 128 + T], ones[:nk, 0:64], PT[:nk, kbk, hh, 0:T],
                                                 start=(bi == 0), stop=(bi == len(blocks) - 1)), r=[onesr, g4r], w=[pdr])
                    oT = g5[:64, 0:2048].bitcast(BF16).rearrange("p (h t) -> p h t", t=128)
                    oTr = g5r
                    hbase = gq * 8 + hb * 4
                    dn = g1[:64, 1024:1536].rearrange("p (h t) -> p h t", t=128)
                    V(lambda e: e.tensor_tensor(dn[:, :, 0:T], pd[:64, :].rearrange("p (h q) -> p h q", q=128)[:, :, 0:T],
                                                esink[:64, gq * 8 + hb * 4: gq * 8 + hb * 4 + 4].unsqueeze(2).to_broadcast([64, 4, T]),
                                                ALU.add), r=[pdr, esinkr], w=[g1r])
                    V(lambda e: e.reciprocal(dn[:, :, 0:T], dn[:, :, 0:T]), r=[g1r], w=[g1r])
                    V(lambda e: e.tensor_tensor(oT[:, hbase:hbase + 4, 0:T],
                                                po[:64, :].rearrange("p (h q) -> p h q", q=128)[:, :, 0:T],
                                                dn[:, :, 0:T], ALU.mult), r=[por, g1r], w=[oTr])
            if T == 128:
                V(lambda e: e.tensor_copy(KTPl[:, :, :], KTC[:, :, :]), r=[KTCr], w=[KTPlr])
                V(lambda e: e.tensor_copy(VPl[:, :], kv[:, 256:512]), r=[g1r], w=[VPlr])
            win_src = (kv, g1r)

            oTa = g5[:64, 0:2048].bitcast(BF16).rearrange("p (h t) -> p h t", t=128)
            for blk in range(4):
                pb, pbr = PS[6 + blk % 2]
                for sub in range(2):
                    projb(lambda c: oTa[:, c, 0:T], 32, 64, wb0 + BID["oswa"] + blk * 2 + sub, 256, T,
                          pb[:T, sub * 256:(sub + 1) * 256], pbr, g5r)
                gate_merge(1, blk, pb[:T, :], pbr, False)

            qm = g0
            for blk in range(4):
                pt, ptr = PS[2 + blk % 2]
                projb(lambda c: HTb[:, c, :T], 16, 128, wb0 + BID["in"] + INB["qm"] + blk, 512, T,
                      pt[:T, :], ptr, HTr)
                V(lambda e: e.memset(ss[:T, blk:blk + 1], 0.0), w=[ssr])
                A(lambda e: e.activation(qm[:T, blk * 512:(blk + 1) * 512], pt[:T, :], AF.Square,
                                         accum_out=ss[:T, blk:blk + 1]), r=[ptr], w=[g0r, ssr])
                rms_rstd(ss[:T, blk:blk + 1], rs[:T, blk:blk + 1], 512, T, ssr, rsr)
                A(lambda e: e.activation(qm[:T, blk * 512:(blk + 1) * 512], pt[:T, :], AF.Copy, scale=rs[:T, blk:blk + 1]),
                  r=[ptr, rsr], w=[g0r])
            qmT = g1[:, 0:2048].rearrange("p (c t) -> p c t", t=128)
            transp(qm, g0r, T, 16, lambda c: pc(l, 64 + c % 4), qmT, g1r)
            omT = g5[:, 0:1024].bitcast(BF16).rearrange("p (c t) -> p c t", t=128)
            mx, mxr = SM["mx"]
            smm, smr = SM["sm"]
            for hm in range(4):
                KTh = g2[:, 0:1024].rearrange("p (c m) -> p c m", m=256)
                Vh = g3[:, 0:1024].rearrange("p (b d) -> p b d", d=512)
                kb.dma("act", KTh, memK_ap[:, hm * 4:(hm + 1) * 4, :], reads=[memK_r], writes=[g2r])
                kb.dma("act", Vh, memV_ap[:, hm * 512:(hm + 1) * 512].rearrange("(b p) d -> p b d", p=128),
                       reads=[memV_r], writes=[g3r])
                pt, ptr = PS[2]
                for c in range(4):
                    P(lambda e: e.matmul(pt[:T, 0:256], qmT[:, hm * 4 + c, 0:T], KTh[:, c, :], start=(c == 0), stop=(c == 3)),
                      r=[g1r, g2r], w=[ptr])
                V(lambda e: e.tensor_reduce(mx[:T, 0:1], pt[:T, 0:256], AX.X, ALU.max), r=[ptr], w=[mxr])
                V(lambda e: e.tensor_scalar(mx[:T, 0:1], mx[:T, 0:1], -(512 ** -0.5), None, ALU.mult), r=[mxr], w=[mxr])
                Pm = g4[:, 0:256]
                V(lambda e: e.memset(smm[:T, 0:1], 0.0), w=[smr])
                A(lambda e: e.activation(Pm[:T, :], pt[:T, 0:256], AF.Exp, bias=mx[:T, 0:1], scale=512 ** -0.5,
                                         accum_out=smm[:T, 0:1]), r=[ptr, mxr], w=[g4r, smr])
                V(lambda e: e.reciprocal(smm[:T, 0:1], smm[:T, 0:1]), r=[smr], w=[smr])
                V(lambda e: e.tensor_scalar(Pm[:T, :], Pm[:T, :], smm[:T, 0:1], None, ALU.mult), r=[g4r, smr], w=[g4r])
                PmT = g4[:, 512:768].rearrange("p (c t) -> p c t", t=128)
                transp(g4, g4r, T, 2, None, PmT, g4r)
                pt2, pt2r = PS[3]
                for dc in range(4):
                    for mc in range(2):
                        P(lambda e: e.matmul(pt2[:, dc * 128: dc * 128 + T], Vh[:, mc, dc * 128:(dc + 1) * 128],
                                             PmT[:, mc, 0:T], start=(mc == 0), stop=(mc == 1)), r=[g3r, g4r], w=[pt2r])
                A(lambda e: e.copy(omT[:, hm * 4:(hm + 1) * 4, 0:T],
                                   pt2[:, :].rearrange("p (c t) -> p c t", t=128)[:, :, 0:T]), r=[pt2r], w=[g5r])
            out_branch(2, lambda c: omT[:, c, 0:T], 16, 128, wb0 + BID["omem"], g5r, 512)

            mT = g2[:, 0:1024].bitcast(BF16).rearrange("p (c t) -> p c t", t=128)
            transp(MG, MGr, T, 16, None, mT, g2r)
            for blk in range(4):
                pt, ptr = PS[2 + blk % 2]
                projb(lambda c: mT[:, c, 0:T], 16, 128, wb0 + BID["out"] + blk, 512, T, pt[:T, :], ptr, g2r)
                V(lambda e: e.tensor_tensor(X[:T, blk * 512:(blk + 1) * 512], X[:T, blk * 512:(blk + 1) * 512], pt[:T, :],
                                            ALU.add), r=[ptr, Xr], w=[Xr])
            if do_peer and run_peer:
                peer(T, l, X, Xr)
            return win_src

        def transp_fm(src3, srcr, T, c0, nch, dst3, dstr):
            for c in range(nch):
                b = tctr[0] % 2
                tctr[0] += 1
                pt, ptr = PS[b]
                P(lambda e: e.transpose(pt[:T, 0:128], src3[:, c0 + c, 0:T], ident[:, :]), r=[srcr, identr], w=[ptr])
                if c % 2 == 0:
                    V(lambda e: e.tensor_copy(dst3[:T, c, :], pt[:T, 0:128]), r=[ptr], w=[dstr])
                else:
                    A(lambda e: e.copy(dst3[:T, c, :], pt[:T, 0:128]), r=[ptr], w=[dstr])

        def peer(T, l, Xt, Xtr):
            ss, ssr = SM["ss"]
            rs, rsr = SM["rs"]
            g0, g0r = Gb[0]
            g1, g1r = Gb[1]
            g2, g2r = Gb[2]
            g3, g3r = Gb[3]
            g4, g4r = Gb[4]
            g5, g5r = Gb[5]
            for i_, nm_ in enumerate(["k1", "k2", "posf", "idxf", "gates", "acol", "wcol", "sc16"]):
                SM[nm_] = (g0[:, 2048 + i_ * 128: 2048 + (i_ + 1) * 128], g0r)
            norm_T(Xt, Xtr, T, lambda c: pc(l, 16 + c), HT, HTr)
            h2b = g0[:, 0:1024].bitcast(BF16)
            kb.dma("sp", g5[:, 0:D], gffn_bc[l], writes=[g5r])
            V(lambda e: e.tensor_tensor(h2b[:T, :], g5[:T, 0:D], XN[:T, :], ALU.mult), r=[g5r, XNr], w=[g0r])
            for blk in range(4):
                pt, ptr = PS[2 + blk % 2]
                proj(lambda c: HT[:, c, :T], 16, 128, w_peer_q[l][:, blk * 512:(blk + 1) * 512], 512, T, pt[:T, :], ptr, HTr)
                A(lambda e: e.copy(g1[:T, blk * 512:(blk + 1) * 512], pt[:T, :]), r=[ptr], w=[g1r])
            qT = g2[:, 0:2048].rearrange("p (c t) -> p c t", t=128)
            transp(g1, g1r, T, 16, None, qT, g2r)
            skT = g3[:, 0:2048].rearrange("p (c n) -> p c n", n=128)
            kb.dma("sp", skT, skT_d[l], writes=[g3r])
            sc = g4[:, 0:2048].rearrange("p (c n) -> p c n", n=128)
            for q4 in range(4):
                pt, ptr = PS[4 + q4]
                for j in range(4):
                    hc = q4 * 4 + j
                    P(lambda e: e.matmul(pt[:T, j * 128:(j + 1) * 128], qT[:, hc, 0:T], skT[:, hc, :], start=True, stop=True),
                      r=[g2r, g3r], w=[ptr])
                A(lambda e: e.copy(g4[:T, q4 * 512:(q4 + 1) * 512], pt[:T, :]), r=[ptr], w=[g4r])
            sc2 = g5[:, 0:2048].rearrange("p (c n) -> p c n", n=128)
            vtop, vtopr = SM["vtop"]
            itopf, itopfr = SM["itopf"]
            for hc in range(16):
                V(lambda e: e.max(vtop[:T, hc * 16: hc * 16 + 8], sc[:T, hc, :]), r=[g4r], w=[vtopr])
                V(lambda e: e.max_index(itop_u[:T, hc * 16: hc * 16 + 8], vtop[:T, hc * 16: hc * 16 + 8], sc[:T, hc, :]),
                  r=[g4r, vtopr], w=[itop_ur])
                V(lambda e: e.match_replace(sc2[:T, hc, :], vtop[:T, hc * 16: hc * 16 + 8], sc[:T, hc, :], -1e30),
                  r=[g4r, vtopr], w=[g5r])
                V(lambda e: e.max(vtop[:T, hc * 16 + 8: hc * 16 + 16], sc2[:T, hc, :]), r=[g5r], w=[vtopr])
                V(lambda e: e.max_index(itop_u[:T, hc * 16 + 8: hc * 16 + 16], vtop[:T, hc * 16 + 8: hc * 16 + 16],
                                        sc2[:T, hc, :]), r=[g5r, vtopr], w=[itop_ur])
            V(lambda e: e.tensor_copy(itopf[:T, :], itop_u[:T, :]), r=[itop_ur], w=[itopfr])
            cand = g1[:, 0:2048].rearrange("p (h a b) -> p h a b", h=8, a=16)
            cand2 = g3[:, 0:2048].rearrange("p (h n) -> p h n", n=256)
            v4 = vtop[:T, :].rearrange("p (h c k) -> p h c k", h=8, c=2)
            V(lambda e: e.tensor_tensor(cand[:T], v4[:, :, 0, :].unsqueeze(3).to_broadcast([T, 8, 16, 16]),
                                        v4[:, :, 1, :].unsqueeze(2).to_broadcast([T, 8, 16, 16]), ALU.add),
              r=[vtopr], w=[g1r])
            candf = g1[:, 0:2048].rearrange("p (h n) -> p h n", n=256)
            sc16, sc16r = SM["sc16"]
            for h in range(8):
                V(lambda e: e.max(sc16[:T, h * 16: h * 16 + 8], candf[:T, h, :]), r=[g1r], w=[sc16r])
                V(lambda e: e.max_index(pos_u[:T, h * 16: h * 16 + 8], sc16[:T, h * 16: h * 16 + 8], candf[:T, h, :]),
                  r=[g1r, sc16r], w=[pos_ur])
                V(lambda e: e.match_replace(cand2[:T, h, :], sc16[:T, h * 16: h * 16 + 8], candf[:T, h, :], -1e30),
                  r=[g1r, sc16r], w=[g3r])
                V(lambda e: e.max(sc16[:T, h * 16 + 8: h * 16 + 16], cand2[:T, h, :]), r=[g3r], w=[sc16r])
                V(lambda e: e.max_index(pos_u[:T, h * 16 + 8: h * 16 + 16], sc16[:T, h * 16 + 8: h * 16 + 16],
                                        cand2[:T, h, :]), r=[g3r, sc16r], w=[pos_ur])
            k1, k1r = SM["k1"]
            k2, k2r = SM["k2"]
            V(lambda e: e.tensor_single_scalar(idx_i[:T, :], pos_u[:T, :].bitcast(I32), 4, ALU.logical_shift_right), r=[pos_ur], w=[idx_ir])
            V(lambda e: e.tensor_copy(k1[:T, :], idx_i[:T, :]), r=[idx_ir], w=[k1r])
            V(lambda e: e.tensor_single_scalar(idx_i[:T, :], pos_u[:T, :].bitcast(I32), 15, ALU.bitwise_and), r=[pos_ur], w=[idx_ir])
            V(lambda e: e.tensor_copy(k2[:T, :], idx_i[:T, :]), r=[idx_ir], w=[k2r])
            oh = g4[:, 0:2048].rearrange("p (h k j) -> p h k j", h=8, k=16)
            i4 = itopf[:T, :].rearrange("p (h c k) -> p h c k", h=8, c=2)
            idxf, idxfr = SM["idxf"]
            posf, posfr = SM["posf"]
            for (kk, kkr, ci, dst) in ((k1, k1r, 0, idxf), (k2, k2r, 1, posf)):
                V(lambda e: e.tensor_tensor(oh[:T], kk[:T, :].rearrange("p (h k) -> p h k", h=8).unsqueeze(3).to_broadcast([T, 8, 16, 16]),
                                            iot16[:T, :].unsqueeze(1).unsqueeze(1).to_broadcast([T, 8, 16, 16]), ALU.is_equal),
                  r=[kkr, iot16r], w=[g4r])
                V(lambda e: e.tensor_tensor(oh[:T], oh[:T], i4[:, :, ci, :].unsqueeze(2).to_broadcast([T, 8, 16, 16]), ALU.mult),
                  r=[g4r, itopfr], w=[g4r])
                V(lambda e: e.tensor_reduce(dst[:T, :], g4[:T, 0:2048].rearrange("p (a j) -> p a j", j=16), AX.X, ALU.add),
                  r=[g4r], w=[idxfr if ci == 0 else posfr])
            V(lambda e: e.scalar_tensor_tensor(idxf[:T, :], idxf[:T, :], 128.0, posf[:T, :], ALU.mult, ALU.add),
              r=[idxfr, posfr], w=[idxfr])
            if l > 0:
                V(lambda e: e.tensor_scalar(idxf[:T, :], idxf[:T, :], float(l * 16384), None, ALU.add), r=[idxfr], w=[idxfr])
            V(lambda e: e.tensor_copy(idx_i[:T, :], idxf[:T, :]), r=[idxfr], w=[idx_ir])
            gates, gatesr = SM["gates"]
            t8, t8r = SM["t8"]
            s3 = sc16[:T, :].rearrange("p (h k) -> p h k", k=16)
            g3v = gates[:T, :].rearrange("p (h k) -> p h k", k=16)
            V(lambda e: e.tensor_tensor(g3v, s3, s3[:, :, 0:1].to_broadcast([T, 8, 16]), ALU.subtract), r=[sc16r], w=[gatesr])
            A(lambda e: e.activation(gates[:T, :], gates[:T, :], AF.Exp), r=[gatesr], w=[gatesr])
            V(lambda e: e.tensor_reduce(t8[:T, :], g3v, AX.X, ALU.add), r=[gatesr], w=[t8r])
            V(lambda e: e.reciprocal(t8[:T, :], t8[:T, :]), r=[t8r], w=[t8r])
            V(lambda e: e.tensor_tensor(g3v, g3v, t8[:T, :].unsqueeze(2).to_broadcast([T, 8, 16]), ALU.mult),
              r=[gatesr, t8r], w=[gatesr])
            acol, acolr = SM["acol"]
            wcol, wcolr = SM["wcol"]
            V(lambda e: e.memset(acol[:T, :], 0.0), w=[acolr])
            acres = [Res("ac%d" % i) for i in range(128)]
            for r_ in acres:
                r_.w = acolr.w
            gbufs = []
            for k_ in range(3):
                for i_ in range(1, 5):
                    r_ = Res("gs%d_%d" % (i_, k_))
                    r_.w = Gb[i_][1].w
                    r_.r = dict(Gb[i_][1].r)
                    gbufs.append((Gb[i_][0][:, k_ * 1024:(k_ + 1) * 1024], r_))
            NGB = len(gbufs)
            for s in range(128):
                gb, gbr = gbufs[s % NGB]
                gbv = gb[:, 0:1024].bitcast(BF16)
                kb.dma("pool", gbv[:T, :], usc, reads=[idx_ir], writes=[gbr],
                       indirect=bass.IndirectOffsetOnAxis(idx_i[:T, s:s + 1], 0))
                V(lambda e: e.scalar_tensor_tensor(gbv[:T, :], gbv[:T, :], 1.0, h2b[:T, :], ALU.mult, ALU.mult,
                                                   accum_out=acol[:T, s:s + 1]), r=[gbr, g0r], w=[gbr, acres[s]])
            A(lambda e: e.activation(wcol[:T, :], acol[:T, :], AF.Gelu), r=[acolr] + acres, w=[wcolr])
            V(lambda e: e.tensor_tensor(wcol[:T, :], wcol[:T, :], gates[:T, :], ALU.mult), r=[wcolr, gatesr], w=[wcolr])
            NDS = 8
            dres = [Res("diag%d" % i) for i in range(NDS)]
            for s in range(128):
                gb, gbr = gbufs[s % NGB]
                gbv = gb[:, 0:1024].bitcast(BF16)
                kb.dma("pool", gbv[:T, :], vsc, reads=[idx_ir], writes=[gbr],
                       indirect=bass.IndirectOffsetOnAxis(idx_i[:T, s:s + 1], 0))
                ds_ = s % NDS
                dv = g5[:, 0:1024].bitcast(BF16)[:T, ds_ * 128: ds_ * 128 + T]
                wl = [dres[ds_]] + ([g5r] if (s < NDS or s >= 128 - NDS) else [])
                V(lambda e: e.tensor_scalar(dv, ident[:T, :T], wcol[:T, s:s + 1], None, ALU.mult),
                  r=[identr, wcolr], w=wl)
                for q in range(4):
                    pq, pqr = PS[4 + q]
                    rl = [dres[ds_], gbr] + ([g5r] if s >= 128 - NDS else [])
                    P(lambda e: e.matmul(pq[:T, :], dv, gbv[:T, q * 512:(q + 1) * 512], start=(s == 0), stop=(s == 127)),
                      r=rl, w=[pqr])
            for q in range(4):
                pq, pqr = PS[4 + q]
                V(lambda e: e.tensor_tensor(Xt[:T, q * 512:(q + 1) * 512], Xt[:T, q * 512:(q + 1) * 512], pq[:T, :], ALU.add),
                  r=[pqr, Xtr], w=[Xtr])
            for j_, (gb_, r_) in enumerate(gbufs):
                gr_ = Gb[1 + j_ % 4][1]
                for src_, c_ in ([r_.w] if r_.w else []) + list(r_.r.items()):
                    if gr_.r.get(src_, 0) < c_:
                        gr_.r[src_] = c_

        def ssm_out(l, dst_ap, dst_r):
            Sl, Slr = S[l]
            g0, g0r = Gb[0]
            so = g0[:, 0:2048].rearrange("p (c n) -> p c n", n=128)
            transp(Sl, Slr, 128, 16, None, so, g0r)
            kb.dma("sp", dst_ap.rearrange("(c p) n -> p c n", p=128), so, reads=[g0r], writes=[dst_r])

        r_yp, r_ys = ores("yp"), ores("ys")
        r_pw, r_ps, r_pc = ores("pw"), ores("pssm"), ores("pconv")
        r_sw, r_ss, r_scv = ores("sw"), ores("sssm"), ores("sconv")

        precast_all()
        if NPT > 0:
            for l in range(n_layers):
                mem_precompute(l)
                V(lambda e: e.memset(S[l][0][:, :], 0.0), w=[S[l][1]])
                V(lambda e: e.memset(CT[l][0][:, :, :], 0.0), w=[CT[l][1]])
            for ti in range(NPT):
                kb.dma("sp", X[:, :], xp[ti * 128:(ti + 1) * 128, :], writes=[Xr])
                for l in range(n_layers):
                    last = ti == NPT - 1
                    def wo(kv, kvr, l=l):
                        kb.dma("sp", pwk[l], kv[:, 512:768], reads=[kvr], writes=[r_pw])
                        kb.dma("sp", pwv[l], kv[:, 256:512], reads=[kvr], writes=[r_pw])
                    tile_layer(128, l, ti > 0, pmkT[l], pmkT_r[l], pmv[l], pmv_r[l],
                               pconv[l] if last else None, r_pc, wo if last else None)
                    if last:
                        ssm_out(l, pssm[l], r_ps)
                kb.dma("sp", yp[ti * 128:(ti + 1) * 128, :], X[:, :], reads=[Xr], writes=[r_yp])
        cres = Res("cin")
        TS = 8 * NSQ
        XBt, XBr = None, None
        for l in range(n_layers if NSQ > 0 else 0):
            XBt, XBr = S[1] if l == 0 else S[0]
            for sq in range(NSQ):
                if l == 0:
                    kb.dma("sp", X[0:8, :], xsm[sq * 8:(sq + 1) * 8, :], writes=[Xr])
                else:
                    kb.dma("sp", X[0:8, :], XBt[sq * 8:(sq + 1) * 8, :], reads=[XBr], writes=[Xr])
                kb.dma("sp", S[l][0][:, :], sst[l, sq], writes=[S[l][1]])
                kb.dma("sp", CT[l][0][:, :, :], scv[l, sq], writes=[CT[l][1]])
                kb.dma("sp", KTP[l][0][:, :, :], cwkT[l, sq], writes=[KTP[l][1]])
                kb.dma("sp", VP[l][0][:, :], cwv[l, sq], writes=[VP[l][1]])

                def wo(kv, kvr, l=l, sq=sq):
                    kb.dma("sp", swk[l, sq, 0:120, :], cwk[l, sq, 8:128, :], writes=[r_sw])
                    kb.dma("sp", swv[l, sq, 0:120, :], cwv[l, sq, 8:128, :], writes=[r_sw])
                    kb.dma("sp", swk[l, sq, 120:128, :], kv[0:8, 512:768], reads=[kvr], writes=[r_sw])
                    kb.dma("sp", swv[l, sq, 120:128, :], kv[0:8, 256:512], reads=[kvr], writes=[r_sw])
                tile_layer(8, l, True, cmkT[l, sq], cres, cmv[l, sq], cres, sconv[l, sq], r_scv, wo, run_peer=False)
                kb.dma("sp", XBt[sq * 8:(sq + 1) * 8, :], X[0:8, :], reads=[Xr], writes=[XBr])
                ssm_out(l, sssm[l, sq], r_ss)
            if do_peer:
                peer(TS, l, XBt, XBr)
            if l == 0 and n_layers > 1:
                V(lambda e: e.tensor_copy(S[0][0][0:TS, :], S[1][0][0:TS, :]), r=[S[1][1]], w=[S[0][1]])
        if NSQ > 0:
            kb.dma("sp", ys[0:TS, :], XBt[0:TS, :], reads=[XBr], writes=[r_ys])
        kb.wait_all("sp", out_res)
        build.ninst = kb.ninst
    return nc


def _consts():
    ident = np.eye(128, dtype=np.float32)
    j = np.arange(128)
    triu = (j[:, None] <= j[None, :]).astype(np.float32)
    slopes = np.exp2(-8.0 * np.arange(1, 33, dtype=np.float32) / 32).astype(np.float32)
    k = np.arange(128)[:, None].astype(np.float32)
    q = np.arange(128)[None, :].astype(np.float32)
    em = np.zeros((4, 128, 2, 8, 128), np.float32)
    for h in range(32):
        d0 = q + 128 - k
        d1 = q - k
        em[h // 8, :, 0, h % 8, :] = np.where((d0 >= 0) & (d0 <= 128), np.exp(-slopes[h] * d0), 0.0)
        em[h // 8, :, 1, h % 8, :] = np.where((d1 >= 0) & (d1 <= 128), np.exp(-slopes[h] * d1), 0.0)
    return ident, triu, em.reshape(4, 128, 2048)


def _col(v, n):
    return np.ascontiguousarray(np.asarray(v, np.float32).reshape(n, 128).T)


def make_in_maps(inp, NPT, NSQ, cores):
    f = lambda a: np.ascontiguousarray(np.asarray(a, dtype=np.float32))
    ident, triu, em = _consts()
    pcols = np.zeros((128, 400), np.float32)
    prow = np.zeros((128, 768), np.float32)
    for l in range(2):
        o = l * 200
        pcols[:, o:o + 16] = _col(inp["g_mix"][l], 16)
        pcols[:, o + 16:o + 32] = _col(inp["g_ffn"][l], 16)
        pcols[:, o + 32:o + 48] = _col(inp["g_mem"][l], 16)
        pcols[:, o + 48:o + 64] = _col(inp["g_ssd_norm"][l], 16)
        pcols[:, o + 64:o + 68] = _col(inp["g_qm"][l], 4)
        for k in range(4):
            pcols[:, o + 68 + k * 24:o + 68 + (k + 1) * 24] = _col(inp["conv_w"][l][k], 24)
        pcols[:, o + 164:o + 188] = _col(inp["conv_b"][l], 24)
        pcols[:64, o + 188] = np.asarray(inp["g_q"][l], np.float32)
        r = l * 384
        prow[:, r:r + 32] = np.asarray(inp["dt_bias"][l], np.float32)[None, :]
        prow[:, r + 32:r + 64] = np.asarray(inp["a_log"][l], np.float32)[None, :]
        prow[:, r + 64:r + 96] = np.asarray(inp["d_skip"][l], np.float32)[None, :]
        prow[:, r + 96:r + 128] = np.asarray(inp["attn_sinks"][l], np.float32)[None, :]
        prow[:, r + 128:r + 384] = np.tile(np.asarray(inp["g_k"][l], np.float32), 4)[None, :]
    gkm_bc = f(np.broadcast_to(np.tile(np.asarray(inp["g_km"], np.float32), (1, 4))[:, None, :], (2, 128, D)))
    gffn_bc = f(np.broadcast_to(np.asarray(inp["g_ffn"], np.float32)[:, None, :], (2, 128, D)))
    skT = f(np.asarray(inp["peer_sub_keys"], np.float32).reshape(2, 16, 128, 128).transpose(0, 3, 1, 2))
    shared = dict(
        w_in=f(inp["w_in"]), w_mem_kv=f(inp["w_mem_kv"]), w_o_ssm=f(inp["w_o_ssm"]), w_o_swa=f(inp["w_o_swa"]),
        w_o_mem=f(inp["w_o_mem"]), w_out=f(inp["w_out"]), w_peer_q=f(inp["w_peer_q"]), peer_u=f(inp["peer_u"]).reshape(2 * 16384, D),
        peer_v=f(inp["peer_v"]).reshape(2 * 16384, D), skT=skT, pcols=pcols, prow=prow, gkm_bc=gkm_bc, gffn_bc=gffn_bc,
        bgate=f(np.asarray(inp["b_gate"], np.float32).reshape(2, 1, 6144)), ident=ident, triu=triu, emask=em)
    maps = []
    nq = max(NSQ, 1)
    for (pb, s0) in cores:
        m = dict(shared)
        m["xp"] = f(np.asarray(inp["x_prompt"])[pb, :max(NPT, 1) * 128])
        m["memp"] = f(np.asarray(inp["mem_prompt"])[pb])
        sl = slice(s0, s0 + nq)
        m["xsm"] = f(np.asarray(inp["x_sample"])[sl].reshape(nq * 8, D))
        ck = np.asarray(inp["cache_win_k"], np.float32)[:, sl]
        m["cwk"] = f(ck.reshape(2, nq, 128, 256))
        m["cwkT"] = f(ck.transpose(0, 1, 4, 3, 2))
        m["cwv"] = f(np.asarray(inp["cache_win_v"], np.float32)[:, sl].reshape(2, nq, 128, 256))
        ssm = np.asarray(inp["state_ssm"], np.float32)[:, sl]
        m["sst"] = f(ssm.reshape(2, nq, 2048, 128).transpose(0, 1, 3, 2))
        cv = np.asarray(inp["state_conv"], np.float32)[:, sl]
        m["scv"] = f(cv.reshape(2, nq, 3, 24, 128).transpose(0, 1, 4, 3, 2))
        mk = np.asarray(inp["cache_mem_k"], np.float32)[:, sl].reshape(2, nq, 256, 16, 128)
        m["cmkT"] = f(mk.transpose(0, 1, 4, 3, 2))
        m["cmv"] = f(np.asarray(inp["cache_mem_v"], np.float32)[:, sl].reshape(2, nq, 256, D))
        maps.append(m)
    return maps


_NC_CACHE = {}


def kernel(**inputs):
    NPT, NSQ = 16, 4
    key = (NPT, NSQ)
    if key not in _NC_CACHE:
        _NC_CACHE[key] = build(NPT, NSQ)
    nc = _NC_CACHE[key]
    cores = [(c % 4, 4 * c) for c in range(8)]
    maps = make_in_maps(inputs, NPT, NSQ, cores)
    res = run_bass_kernel_spmd(nc, maps, core_ids=list(range(8))).results
    y_prompt = np.stack([res[b]["yp"] for b in range(4)]).astype(np.float32)
    y_sample = np.concatenate([res[c]["ys"].reshape(4, 8, D) for c in range(8)]).astype(np.float32)

    def pstack(name, shape):
        return np.stack([np.stack([res[b][name][l].reshape(shape) for b in range(4)]) for l in range(2)]).astype(np.float32)

    def sstack(name, shape):
        return np.stack([np.concatenate([res[c][name][l].reshape((4,) + shape) for c in range(8)]) for l in range(2)]).astype(np.float32)

    return (y_prompt, y_sample,
            pstack("pwk", (128, 4, 64)), pstack("pwv", (128, 4, 64)),
            pstack("pssm", (32, 64, 128)), pstack("pconv", (3, 3072)),
            pstack("pmk", (256, 4, 512)), pstack("pmv", (256, 4, 512)),
            sstack("swk", (128, 4, 64)), sstack("swv", (128, 4, 64)),
            sstack("sssm", (32, 64, 128)), sstack("sconv", (3, 3072)))
```
